# Optimizing a Trainium2 kernel written in Bass

```python
import math
import jax, jax.numpy as jnp
from jax import lax
import numpy as np

D_MODEL = 1024
BATCH = 4
SEQ = 4096
DEPTH = 4

GRID_W = 64
CTX_LEN = 256
CHUNK = 64
RMS_EPS = 1e-6
MIX_W = D_MODEL
GLA_W = MIX_W // 4
GLA_HEAD_V = 64
GLA_HEADS = GLA_W // GLA_HEAD_V
GLA_HEAD_K = GLA_HEAD_V // 2
GLA_KDIM = GLA_HEADS * GLA_HEAD_K
GLA_RANK = 16
GLA_GATE_TAU = 16.0
SSD_W = MIX_W // 2
SSD_HEAD_DIM = 64
SSD_HEADS = SSD_W // SSD_HEAD_DIM
SSD_GROUPS = 2
SSD_STATE = 128
SSD_CONV_W = 5
SSD_CONV_CH = SSD_W + 2 * SSD_GROUPS * SSD_STATE
RET_W = MIX_W - GLA_W - SSD_W
RET_HEAD_DIM = 64
RET_HEADS = RET_W // RET_HEAD_DIM
ROPE_BASE = 10000.0
GLA_COLS = 2 * GLA_KDIM + 2 * GLA_W + 2 * GLA_RANK
SSD_COLS = SSD_W + SSD_CONV_CH + 2 * SSD_HEADS
RET_COLS = 4 * RET_W
IN_COLS = GLA_COLS + SSD_COLS + RET_COLS
FFN_HIDDEN = -(-8 * D_MODEL // (3 * 256)) * 256

kernel_name = "hybrid_gla_ssd_retention_dit_block"


def rms_norm(x, w, eps=RMS_EPS):
    xf = x.astype(jnp.float32)
    y = xf * lax.rsqrt(jnp.mean(xf * xf, axis=-1, keepdims=True) + eps)
    return (y * w.astype(jnp.float32)).astype(x.dtype)


def layer_norm(x, w, eps=RMS_EPS):
    xf = x.astype(jnp.float32)
    mu = jnp.mean(xf, axis=-1, keepdims=True)
    xc = xf - mu
    y = xc * lax.rsqrt(jnp.mean(xc * xc, axis=-1, keepdims=True) + eps)
    return (y * w.astype(jnp.float32)).astype(x.dtype)


def modulate(x, shift, scale):
    return x * (1 + scale) + shift


def depthwise_conv(u, w, b):
    pad = (w.shape[0] - 1) // 2
    y = lax.conv_general_dilated(u, w[:, None, :], window_strides=(1,), padding=[(pad, pad)],
                                 dimension_numbers=('NWC', 'WIO', 'NWC'),
                                 feature_group_count=u.shape[-1])
    return y + b


def chunked_scan(q, k, v, logg, s0):
    B, T, H, Dk = q.shape
    Dv = v.shape[-1]
    n = T // CHUNK
    scalar = logg.shape[-1] == 1
    mask = jnp.tril(jnp.ones((CHUNK, CHUNK), dtype=bool))[None, :, :, None, None]

    def split(a):
        return a.reshape(B, n, CHUNK, H, a.shape[-1]).swapaxes(0, 1)

    def step(S, inp):
        qc, kc, vc, gc = inp
        G = jnp.cumsum(gc, axis=1)
        diff = G[:, :, None] - G[:, None, :]
        dec = jnp.where(mask, jnp.exp(jnp.minimum(diff, 0.0)), 0.0)
        if scalar:
            scores = jnp.einsum('bihd,bjhd->bijh', qc, kc) * dec[..., 0]
        else:
            scores = jnp.einsum('bihd,bjhd,bijhd->bijh', qc, kc, dec)
        intra = jnp.einsum('bijh,bjhe->bihe', scores, vc)
        inter = jnp.einsum('bihd,bhde->bihe', qc * jnp.exp(G), S)
        G_last = G[:, -1:]
        S_new = jnp.exp(G_last[:, 0])[..., None] * S + jnp.einsum(
            'bjhd,bjhe->bhde', kc * jnp.exp(G_last - G), vc)
        return S_new, intra + inter

    S, o = lax.scan(step, s0, (split(q), split(k), split(v), split(logg)))
    o = o.swapaxes(0, 1).reshape(B, T, H, Dv)
    return o, S


def bidir_scan(ctx_in, lat_in):
    qc, kfc, kbc, vc, gfc, gbc = ctx_in
    ql, kfl, kbl, vl, gfl, gbl = lat_in
    B, _, H, Dk = qc.shape
    Dv = vc.shape[-1]
    s0 = jnp.zeros((B, H, Dk, Dv), qc.dtype)
    flip = lambda a: jnp.flip(a, axis=1)
    oc_f, S_f = chunked_scan(qc, kfc, vc, gfc, s0)
    ol_f, _ = chunked_scan(ql, kfl, vl, gfl, S_f)
    oc_b, S_b = chunked_scan(flip(qc), flip(kbc), flip(vc), flip(gbc), s0)
    ol_b, _ = chunked_scan(flip(ql), flip(kbl), flip(vl), flip(gbl), S_b)
    return oc_f + flip(oc_b), ol_f + flip(ol_b)


def gla_mixer(p_ctx, p_lat, gate_up, gate_b, norm_w):
    def prep(p):
        B, T, _ = p.shape
        q, k, v, r, lr = jnp.split(p, [GLA_KDIM, 2 * GLA_KDIM, 2 * GLA_KDIM + GLA_W,
                                       2 * GLA_KDIM + 2 * GLA_W], axis=-1)
        q = q.reshape(B, T, GLA_HEADS, GLA_HEAD_K) * GLA_HEAD_K ** -0.5
        k = k.reshape(B, T, GLA_HEADS, GLA_HEAD_K)
        v = v.reshape(B, T, GLA_HEADS, GLA_HEAD_V)
        z = jnp.einsum('btnr,nrk->btnk', lr.reshape(B, T, 2, GLA_RANK), gate_up) + gate_b
        logg = jax.nn.log_sigmoid(z) / GLA_GATE_TAU
        g_f = logg[:, :, 0].reshape(B, T, GLA_HEADS, GLA_HEAD_K)
        g_b = logg[:, :, 1].reshape(B, T, GLA_HEADS, GLA_HEAD_K)
        return (q, k, k, v, g_f, g_b), r

    in_c, r_c = prep(p_ctx)
    in_l, r_l = prep(p_lat)
    o_c, o_l = bidir_scan(in_c, in_l)

    def out(o, r):
        B, T = o.shape[:2]
        o = rms_norm(o, norm_w.reshape(GLA_HEADS, GLA_HEAD_V)).reshape(B, T, GLA_W)
        return o * jax.nn.silu(r)

    return out(o_c, r_c), out(o_l, r_l)


def ssd_mixer(p_ctx, p_lat, conv_w, conv_b, dt_bias, a_log, d_skip, norm_w):
    def prep(p):
        B, T, _ = p.shape
        z, xbc, dt = jnp.split(p, [SSD_W, SSD_W + SSD_CONV_CH], axis=-1)
        xbc = jax.nn.silu(depthwise_conv(xbc, conv_w, conv_b))
        xs, bm, cm = jnp.split(xbc, [SSD_W, SSD_W + SSD_GROUPS * SSD_STATE], axis=-1)
        rep = SSD_HEADS // SSD_GROUPS
        xs = xs.reshape(B, T, SSD_HEADS, SSD_HEAD_DIM)
        bm = jnp.repeat(bm.reshape(B, T, SSD_GROUPS, SSD_STATE), rep, axis=2)
        cm = jnp.repeat(cm.reshape(B, T, SSD_GROUPS, SSD_STATE), rep, axis=2)
        dt = jax.nn.softplus(dt.reshape(B, T, 2, SSD_HEADS) + dt_bias)
        logg = dt * (-jnp.exp(a_log))
        k_f = bm * dt[:, :, 0, :, None]
        k_b = bm * dt[:, :, 1, :, None]
        return (cm, k_f, k_b, xs, logg[:, :, 0, :, None], logg[:, :, 1, :, None]), z, xs

    in_c, z_c, x_c = prep(p_ctx)
    in_l, z_l, x_l = prep(p_lat)
    y_c, y_l = bidir_scan(in_c, in_l)

    def out(y, z, xs):
        B, T = y.shape[:2]
        y = (y + d_skip[:, None] * xs).reshape(B, T, SSD_W)
        return rms_norm(y * jax.nn.silu(z), norm_w)

    return out(y_c, z_c, x_c), out(y_l, z_l, x_l)


def apply_rope(t, cos, sin):
    half = t.shape[-1] // 2
    t1, t2 = t[..., :half], t[..., half:]
    return jnp.concatenate([t1 * cos - t2 * sin, t2 * cos + t1 * sin], axis=-1)


def retention_mixer(p_ctx, p_lat, cos, sin, norm_w):
    log_gamma = jnp.log1p(-jnp.exp2(-5.0 - jnp.arange(RET_HEADS, dtype=jnp.float32)))
    log_gamma = log_gamma.astype(p_lat.dtype)

    def prep(p, rotate):
        B, T, _ = p.shape
        q, k, v, g = jnp.split(p, 4, axis=-1)
        q = q.reshape(B, T, RET_HEADS, RET_HEAD_DIM) * RET_HEAD_DIM ** -0.5
        k = k.reshape(B, T, RET_HEADS, RET_HEAD_DIM)
        v = v.reshape(B, T, RET_HEADS, RET_HEAD_DIM)
        if rotate:
            q = apply_rope(q, cos, sin)
            k = apply_rope(k, cos, sin)
        lg = jnp.broadcast_to(log_gamma[:, None], (B, T, RET_HEADS, 1))
        return (q, k, k, v, lg, lg), g

    in_c, g_c = prep(p_ctx, False)
    in_l, g_l = prep(p_lat, True)
    o_c, o_l = bidir_scan(in_c, in_l)

    def out(o, g):
        B, T = o.shape[:2]
        o = layer_norm(o, norm_w.reshape(RET_HEADS, RET_HEAD_DIM)).reshape(B, T, RET_W)
        return o * jax.nn.silu(g)

    return out(o_c, g_c), out(o_l, g_l)


def swiglu(h, w13, w2):
    gate, up = jnp.split(h @ w13, 2, axis=-1)
    return (jax.nn.silu(gate) * up) @ w2


def setup_inputs(seed: int = 0) -> dict:
    key = jax.random.key(seed)
    ks = jax.random.split(key, 24)
    f32 = jnp.float32
    nrm = lambda k, shape, s: jax.random.normal(k, shape, f32) * s
    gain = lambda k, shape: 1.0 + 0.05 * jax.random.normal(k, shape, f32)
    D = D_MODEL
    dt = jnp.exp(jax.random.uniform(ks[17], (DEPTH, 2, SSD_HEADS), f32)
                 * (math.log(0.1) - math.log(0.001)) + math.log(0.001))
    return {
        "x": nrm(ks[0], (BATCH, SEQ, D), 1.0),
        "c": nrm(ks[1], (BATCH, D), 1.0),
        "ctx": nrm(ks[2], (BATCH, CTX_LEN, D), 1.0),
        "c_ctx": nrm(ks[3], (D,), 1.0),
        "ada_w": nrm(ks[4], (DEPTH, D, 6 * D), D ** -0.5),
        "ada_b": nrm(ks[5], (DEPTH, 6 * D), 0.02),
        "norm_mix_pre": gain(ks[6], (DEPTH, D)),
        "norm_mix_post": gain(ks[7], (DEPTH, D)),
        "norm_ffn_pre": gain(ks[8], (DEPTH, D)),
        "norm_ffn_post": gain(ks[9], (DEPTH, D)),
        "w_in": nrm(ks[10], (DEPTH, D, IN_COLS), D ** -0.5),
        "w_out": nrm(ks[11], (DEPTH, MIX_W, D), MIX_W ** -0.5),
        "gla_gate_up": nrm(ks[12], (DEPTH, 2, GLA_RANK, GLA_KDIM), GLA_RANK ** -0.5),
        "gla_gate_b": nrm(ks[13], (DEPTH, 2, GLA_KDIM), 0.1),
        "gla_norm": gain(ks[14], (DEPTH, GLA_W)),
        "ssd_conv_w": nrm(ks[15], (DEPTH, SSD_CONV_W, SSD_CONV_CH), SSD_CONV_W ** -0.5),
        "ssd_conv_b": nrm(ks[16], (DEPTH, SSD_CONV_CH), 0.02),
        "ssd_dt_bias": dt + jnp.log(-jnp.expm1(-dt)),
        "ssd_a_log": jnp.log(jax.random.uniform(ks[18], (DEPTH, 2, SSD_HEADS), f32, 1.0, 16.0)),
        "ssd_d": gain(ks[19], (DEPTH, SSD_HEADS)),
        "ssd_norm": gain(ks[20], (DEPTH, SSD_W)),
        "ret_norm": gain(ks[21], (DEPTH, RET_W)),
        "ffn_w13": nrm(ks[22], (DEPTH, D, 2 * FFN_HIDDEN), D ** -0.5),
        "ffn_w2": nrm(ks[23], (DEPTH, FFN_HIDDEN, D), FFN_HIDDEN ** -0.5),
    }


def reference(x, c, ctx, c_ctx, ada_w, ada_b, norm_mix_pre, norm_mix_post, norm_ffn_pre,
              norm_ffn_post, w_in, w_out, gla_gate_up, gla_gate_b, gla_norm, ssd_conv_w,
              ssd_conv_b, ssd_dt_bias, ssd_a_log, ssd_d, ssd_norm, ret_norm, ffn_w13, ffn_w2):
    T = x.shape[1]
    rows = T // GRID_W
    row = jnp.repeat(jnp.arange(rows), GRID_W).astype(jnp.float32)
    col = jnp.tile(jnp.arange(GRID_W), rows).astype(jnp.float32)
    n_freq = RET_HEAD_DIM // 4
    inv_freq = ROPE_BASE ** (-jnp.arange(n_freq, dtype=jnp.float32) / n_freq)
    ang = jnp.concatenate([row[:, None] * inv_freq, col[:, None] * inv_freq], axis=-1)
    cos = jnp.cos(ang).astype(x.dtype)[None, :, None, :]
    sin = jnp.sin(ang).astype(x.dtype)[None, :, None, :]

    lat, cx = x, ctx
    s1, s2 = GLA_COLS, GLA_COLS + SSD_COLS
    for l in range(DEPTH):
        last = l == DEPTH - 1
        mod_l = (jax.nn.silu(c) @ ada_w[l] + ada_b[l])[:, None, :]
        mod_c = (jax.nn.silu(c_ctx) @ ada_w[l] + ada_b[l])[None, None, :]
        sh1, sc1, gt1, sh2, sc2, gt2 = jnp.split(mod_l, 6, axis=-1)
        csh1, csc1, cgt1, csh2, csc2, cgt2 = jnp.split(mod_c, 6, axis=-1)

        p_l = modulate(rms_norm(lat, norm_mix_pre[l]), sh1, sc1) @ w_in[l]
        p_c = modulate(rms_norm(cx, norm_mix_pre[l]), csh1, csc1) @ w_in[l]
        gla_c, gla_l = gla_mixer(p_c[..., :s1], p_l[..., :s1],
                                 gla_gate_up[l], gla_gate_b[l], gla_norm[l])
        ssd_c, ssd_l = ssd_mixer(p_c[..., s1:s2], p_l[..., s1:s2], ssd_conv_w[l], ssd_conv_b[l],
                                 ssd_dt_bias[l], ssd_a_log[l], ssd_d[l], ssd_norm[l])
        ret_c, ret_l = retention_mixer(p_c[..., s2:], p_l[..., s2:], cos, sin, ret_norm[l])
        mixed_l = jnp.concatenate([gla_l, ssd_l, ret_l], axis=-1) @ w_out[l]
        lat = lat + gt1 * rms_norm(mixed_l, norm_mix_post[l])

        h_l = modulate(rms_norm(lat, norm_ffn_pre[l]), sh2, sc2)
        lat = lat + gt2 * rms_norm(swiglu(h_l, ffn_w13[l], ffn_w2[l]), norm_ffn_post[l])

        if not last:
            mixed_c = jnp.concatenate([gla_c, ssd_c, ret_c], axis=-1) @ w_out[l]
            cx = cx + cgt1 * rms_norm(mixed_c, norm_mix_post[l])
            h_c = modulate(rms_norm(cx, norm_ffn_pre[l]), csh2, csc2)
            cx = cx + cgt2 * rms_norm(swiglu(h_c, ffn_w13[l], ffn_w2[l]), norm_ffn_post[l])
    return lat
```

```python
import concourse.bass as bass
import concourse.mybir as mybir

F32 = mybir.dt.float32
BF16 = mybir.dt.bfloat16
AF = mybir.ActivationFunctionType
ALU = mybir.AluOpType
AX = mybir.AxisListType

ENGINES = ("sp", "act", "pool", "dve", "pe")
NDMASEM = 12


import types


def _freeze(fn):
    if fn.__closure__ is None:
        return fn
    cells = []
    for c in fn.__closure__:
        try:
            cells.append(types.CellType(c.cell_contents))
        except ValueError:
            cells.append(c)
    return types.FunctionType(fn.__code__, fn.__globals__, fn.__name__, fn.__defaults__, tuple(cells))


class Prog:
    def __init__(self, nc):
        self.nc = nc
        self.ops = []
        self.last_w = {}
        self.readers = {}

    def op(self, eng, fn, reads=(), writes=(), dma=False):
        i = len(self.ops)
        deps = set()
        for r in reads:
            if r in self.last_w:
                deps.add(self.last_w[r])
            if isinstance(r, str) and r.startswith("ps"):
                for q in self.readers.get(r, ()):
                    if self.ops[q]["eng"] != eng:
                        deps.add(q)
        for w in writes:
            if w in self.last_w:
                deps.add(self.last_w[w])
            for q in self.readers.get(w, ()):
                deps.add(q)
        for r in reads:
            self.readers.setdefault(r, []).append(i)
        for w in writes:
            self.last_w[w] = i
            self.readers[w] = []
        deps.discard(i)
        self.ops.append(dict(eng=eng, fn=_freeze(fn), deps=deps, dma=dma, idx=i))
        return i

    def emit(self, final_wait_ops=()):
        nc = self.nc
        ops = self.ops
        needed = set()
        for o in ops:
            for d in o["deps"]:
                do = ops[d]
                if do["eng"] == "pe" and o["eng"] == "pe" and not do["dma"]:
                    continue
                needed.add(d)
        for d in final_wait_ops:
            needed.add(d)
        cnt = {e: 0 for e in ENGINES}
        dcnt = {e: 0 for e in ENGINES}
        for o in ops:
            e = o["eng"]
            if o["dma"]:
                n = dcnt[e]
                dcnt[e] += 1
                o["dma_n"] = n
            elif o["idx"] in needed:
                cnt[e] += 1
                o["ticket"] = cnt[e]
        self.cnt = cnt
        import contextlib
        with contextlib.ExitStack() as es:
            sems = {e: es.enter_context(nc.semaphore("s_" + e)) for e in ENGINES}
            dsems = {e: [es.enter_context(nc.semaphore("d_%s_%d" % (e, k))) for k in range(NDMASEM)]
                     for e in ENGINES if dcnt[e] > 0}
            block = es.enter_context(nc.Block())
            per_eng = {e: [o for o in ops if o["eng"] == e] for e in ENGINES}

            def run(e, engobj, extra_final=False):
                waited = {}
                for o in per_eng[e]:
                    waits = []
                    for d in sorted(o["deps"]):
                        do = ops[d]
                        if do["dma"]:
                            n = do["dma_n"]
                            waits.append((dsems[do["eng"]][n % NDMASEM], 16 * (n // NDMASEM + 1), ("d", do["eng"], n % NDMASEM)))
                        else:
                            if do["eng"] == "pe" and e == "pe":
                                continue
                            waits.append((sems[do["eng"]], do["ticket"], ("c", do["eng"])))
                    if o["dma"]:
                        n = o["dma_n"]
                        if n >= NDMASEM:
                            waits.append((dsems[e][n % NDMASEM], 16 * (n // NDMASEM), ("d", e, n % NDMASEM)))
                    for sem, val, key in waits:
                        if waited.get(key, 0) >= val:
                            continue
                        waited[key] = val
                        engobj.wait_ge(sem, val)
                    ins = o["fn"](engobj)
                    if o["dma"]:
                        ins.then_inc(dsems[e][o["dma_n"] % NDMASEM], 16)
                    elif "ticket" in o:
                        ins.then_inc(sems[e], 1)
                if extra_final:
                    for qe in ENGINES:
                        for k in range(NDMASEM):
                            c = len(range(k, dcnt[qe], NDMASEM))
                            if c > 0:
                                engobj.wait_ge(dsems[qe][k], 16 * c)
                    for d in final_wait_ops:
                        do = ops[d]
                        if do["dma"]:
                            n = do["dma_n"]
                            engobj.wait_ge(dsems[do["eng"]][n % NDMASEM], 16 * (n // NDMASEM + 1))
                        else:
                            engobj.wait_ge(sems[do["eng"]], do["ticket"])

            @block.sync
            def _(eng):
                run("sp", eng, extra_final=True)

            @block.scalar
            def _(eng):
                run("act", eng)

            @block.gpsimd
            def _(eng):
                run("pool", eng)

            @block.vector
            def _(eng):
                run("dve", eng)

            @block.tensor
            def _(eng):
                run("pe", eng)

import contextlib
import numpy as np
from concourse.bass_utils import run_bass_kernel_spmd

D = 1024
SEQ = 4096
CTX = 256
NTOK = SEQ + CTX
NT = NTOK // 128
NCT = CTX // 128
DEPTH = 4
FH = 2816
NJ = FH // 128
FM_QK, FM_LR, FM_X, TM0 = 0, 256, 320, 1344
TM_A, TM_B, TM_C, TM_D, TM_E = TM0, TM0 + 512, TM0 + 1024, TM0 + 1536, TM0 + 2048
NCOLS = TM0 + 2304


def _colmap():
    m = -np.ones(NCOLS, dtype=np.int64)
    m[0:256] = np.arange(0, 256)
    m[FM_LR:FM_LR + 16] = np.arange(768, 784)
    m[FM_LR + 32:FM_LR + 48] = np.arange(784, 800)
    m[FM_X:FM_X + 1024] = np.arange(1312, 2336)
    m[TM_A:TM_A + 128] = np.arange(128, 256)
    m[TM_A + 128:TM_A + 384] = np.arange(256, 512)
    m[TM_A + 384:TM_A + 400] = np.arange(2336, 2352)
    m[TM_B:TM_B + 256] = np.arange(512, 768)
    m[TM_B + 256:TM_B + 512] = np.arange(3120, 3376)
    m[TM_C:TM_C + 512] = np.arange(800, 1312)
    m[TM_D:TM_D + 256] = np.arange(2352, 2608)
    m[TM_D + 256:TM_D + 512] = np.arange(2608, 2864)
    m[TM_E:TM_E + 256] = np.arange(2864, 3120)
    return m


class Cols:
    def __init__(self):
        self.off = {}
        self.n = 0

    def add(self, name, w):
        self.off[name] = (self.n, self.n + w)
        self.n += w

    def __getitem__(self, name):
        return self.off[name]


CF = Cols()
for _n, _w in [("eps", 1), ("lnqs", 1), ("one", 1), ("ident", 128), ("maskf", 128), ("maskb", 128),
               ("slf", 128), ("slb", 128), ("Rf", 129), ("Rb", 129), ("Lf", 128), ("Lb", 128), ("ones", 128),
               ("retM", 512), ("retEQf", 4), ("retEQb", 4), ("retWf", 4), ("retWb", 4), ("retdec", 4),
               ("bmg", 128), ("bmr", 128)]:
    CF.add(_n, _w)

PLC = Cols()
for _n, _w in [("adab", 48), ("npre", 8), ("npost", 8), ("nfpre", 8), ("nfpost", 8), ("GU", 256),
               ("glan", 256), ("ssdn", 512), ("retn", 256), ("ssdd", 512), ("convw", 40), ("convb", 8),
               ("dtb", 16), ("alog", 16)]:
    PLC.add(_n, _w)


def _host_consts():
    c = np.zeros((128, CF.n), np.float32)
    def put(name, arr):
        a, b = CF[name]
        c[:, a:b] = np.asarray(arr, np.float32).reshape(128, b - a)
    j = np.arange(128)[:, None]
    i = np.arange(128)[None, :]
    maskf = (j <= i).astype(np.float32)
    maskb = (j >= i).astype(np.float32)
    put("eps", np.full((128, 1), 1e-6))
    put("lnqs", np.full((128, 1), np.log(32.0 ** -0.5)))
    put("one", np.ones((128, 1)))
    put("ident", np.eye(128))
    put("maskf", maskf)
    put("maskb", maskb)
    put("slf", 1 - maskf)
    put("slb", 1 - maskb)
    put("Rf", np.concatenate([maskf, np.ones((128, 1))], 1) * (-1 / 16))
    put("Rb", np.concatenate([maskb, np.ones((128, 1))], 1) * (-1 / 16))
    put("Lf", (1 - maskf) * (-1 / 16))
    put("Lb", (1 - maskb) * (-1 / 16))
    put("ones", np.ones((128, 128)))
    lg = np.log1p(-np.exp2(-5.0 - np.arange(4, dtype=np.float32))).astype(np.float32).astype(np.float64)
    M = np.zeros((128, 4, 128))
    for h in range(4):
        M[:, (h % 2) * 2 + h // 2, :] = np.exp(lg[h] * np.abs(i - j)) * np.where(i == j, 2.0, 1.0)
    put("retM", M)
    tt = np.arange(128)[:, None].astype(np.float64)
    put("retEQf", np.exp(lg[None, :] * (tt + 1)))
    put("retEQb", np.exp(lg[None, :] * (128 - tt)))
    put("retWf", np.exp(lg[None, :] * (127 - tt)))
    put("retWb", np.exp(lg[None, :] * tt))
    put("retdec", np.tile(np.exp(lg * 128)[None, :], (128, 1)))
    p = np.arange(128)[:, None]
    cc = np.arange(128)[None, :]
    put("bmg", ((p // 32) == (cc // 64)).astype(np.float32))
    put("bmr", ((p // 64) == (cc // 64)).astype(np.float32))
    return c


def _host_rope():
    rows = SEQ // 64
    row = np.repeat(np.arange(rows), 64).astype(np.float32)
    col = np.tile(np.arange(64), rows).astype(np.float32)
    inv = (np.float32(10000.0) ** (-np.arange(16, dtype=np.float32) / np.float32(16))).astype(np.float32)
    ang = np.concatenate([row[:, None] * inv, col[:, None] * inv], -1).astype(np.float32)
    cos, sin = np.cos(ang).astype(np.float32), np.sin(ang).astype(np.float32)
    t = np.zeros((SEQ, 256), np.float32)
    t[:, 0:64] = np.concatenate([cos, cos], 1) * 0.125
    t[:, 64:128] = np.concatenate([-sin, sin], 1) * 0.125
    t[:, 128:192] = np.concatenate([cos, cos], 1)
    t[:, 192:256] = np.concatenate([-sin, sin], 1)
    return t


def _fm(v, n):
    return np.asarray(v, np.float32).reshape(n, 128).T


def _host_pl(inp, l):
    a = np.zeros((128, PLC.n), np.float32)
    def put(name, arr):
        s, e = PLC[name]
        a[:, s:e] = np.asarray(arr, np.float32).reshape(128, e - s)
    put("adab", _fm(inp["ada_b"][l], 48))
    put("npre", _fm(inp["norm_mix_pre"][l], 8))
    put("npost", _fm(inp["norm_mix_post"][l], 8))
    put("nfpre", _fm(inp["norm_ffn_pre"][l], 8))
    put("nfpost", _fm(inp["norm_ffn_post"][l], 8))
    gu = np.zeros((128, 256), np.float32)
    gu[0:16, 0:128] = inp["gla_gate_up"][l][0]
    gu[16, 0:128] = inp["gla_gate_b"][l][0]
    gu[32:48, 128:256] = inp["gla_gate_up"][l][1]
    gu[48, 128:256] = inp["gla_gate_b"][l][1]
    put("GU", gu)
    rep = lambda v: np.tile(np.asarray(v, np.float32)[None, :], (128, 1))
    put("glan", rep(inp["gla_norm"][l]))
    put("ssdn", rep(inp["ssd_norm"][l]))
    put("retn", rep(inp["ret_norm"][l]))
    put("ssdd", rep(np.repeat(inp["ssd_d"][l], 64)))
    cw = inp["ssd_conv_w"][l]
    put("convw", cw.reshape(5, 8, 128).transpose(2, 1, 0).reshape(128, 40))
    put("convb", _fm(inp["ssd_conv_b"][l], 8))
    put("dtb", rep(inp["ssd_dt_bias"][l].reshape(16)))
    put("alog", rep(inp["ssd_a_log"][l].reshape(16)))
    return a


STOP = None


class _Stop(Exception):
    pass


def _chk(stage):
    if STOP is not None and STOP == stage:
        raise _Stop()


def build_nc(depth=DEPTH, debug_layers=None):
    nc = bass.Bass("TRN2", target_bir_lowering=False)
    es = contextlib.ExitStack()
    with es:
        def din(name, shape, dt=F32):
            return nc.dram_tensor(name, shape, dt, kind="ExternalInput").ap()
        def dscr(name, shape, dt=F32):
            return nc.dram_tensor(name, shape, dt, kind="Internal").ap()
        xin = din("xin", [NTOK, D])
        cvec = din("cvec", [128, 16])
        constf_d = din("constf", [128, CF.n])
        rope_d = din("rope", [SEQ, 256])
        pl_d = din("pl", [depth, 128, PLC.n])
        cbrow_d = din("cbrow", [depth, 1, 768])
        w_in_d = din("w_in", [depth, D, NCOLS])
        w_out_d = din("w_out", [depth, D, D])
        w13_d = din("w13", [depth, D, 2 * FH])
        w2_d = din("w2", [depth, FH, D])
        ada_d = din("ada_w", [depth, D, 6 * D])
        out_d = nc.dram_tensor("out", [SEQ, D], F32, kind="ExternalOutput").ap()
        res_d = dscr("res", [NTOK, D])
        wb_in = dscr("wb_in", [depth, D, NCOLS], BF16)
        wb_out = dscr("wb_out", [depth, D, D], BF16)
        wb_13 = dscr("wb_13", [depth, D, 2 * FH], BF16)
        wb_2 = dscr("wb_2", [depth, FH, D], BF16)
        wb_ada = dscr("wb_ada", [depth, D, 6 * D], BF16)
        sA_d = dscr("sA", [NT, 128, 1026])
        sB_d = dscr("sB", [NT, 128, 1536], BF16)
        sS_d = dscr("sS", [NT, 128, 1040])
        sT_d = dscr("sT", [NT, 128, 768], BF16)
        gscr_d = dscr("gscr", [4, 8, 256])

        def sb(name, shape, dt=F32):
            return es.enter_context(nc.sbuf_tensor("sb_" + name, shape, dt))
        P = Prog(nc)
        constf = sb("constf", [128, CF.n])
        def C(name, rows=slice(0, 128)):
            a, b = CF[name]
            return constf[rows, a:b]
        identb = sb("identb", [128, 128], BF16)
        onesb = sb("onesb", [128, 128], BF16)
        pl = sb("pl", [128, PLC.n])
        def PL(name, rows=slice(0, 128)):
            a, b = PLC[name]
            return pl[rows, a:b]
        GUb = sb("GUb", [64, 256], BF16)
        cbrowb = sb("cbrowb", [64, 768], BF16)
        Arep = sb("Arep", [128, 16])
        cdiag = sb("cdiag", [128, 8 * 5 * 128], BF16)
        cdv = cdiag[:, :].rearrange("p (a k c) -> p a k c", a=8, k=5)
        rope_t = sb("rope_t", [128, 256])
        csil = sb("csil", [128, 16], BF16)
        cve = sb("cve", [128, 16])
        modT = sb("modT", [128, 96])
        modv = modT[:, :].rearrange("p (c s) -> p c s", s=2)
        AB = sb("AB", [128, 2 * 6 * 8])
        ABv = AB[:, :].rearrange("p (s v k) -> p s v k", s=2, v=6)
        Grep_ = sb("Grep_", [128, 2 * 2 * 1024])
        Gv = Grep_[:, :].rearrange("p (s v c) -> p s v c", s=2, v=2)
        dg = sb("dg", [128, 128])
        wbig = sb("wbig", [128, 30720], BF16)
        w_in_s = wbig[:, 0:8 * NCOLS].rearrange("p (k c) -> p k c", k=8)
        w_out_s = wbig[:, 0:8192].rearrange("p (k c) -> p k c", k=8)
        w2_s = wbig[:, 8192:8192 + NJ * 1024].rearrange("p (j c) -> p j c", j=NJ)
        arF = sb("arF", [128, 2560])
        arH = sb("arH", [128, 10336], BF16)
        adaw = sb("adaw", [128, 8 * 512], BF16)
        adawv = adaw[:, :].rearrange("p (k c) -> p k c", k=8)
        wgu = [arH[:, 4864 + i * 2048:4864 + (i + 1) * 2048] for i in range(2)]
        xt = sb("xt", [128, D])
        junk = sb("junk", [128, D])
        xh = sb("xh", [128, D], BF16)
        hT = sb("hT", [128, D], BF16)
        hTv = hT[:, :].rearrange("p (k t) -> p k t", k=8)
        st = sb("st", [128, 32])
        lrT = sb("lrT", [64, 128], BF16)
        xraw = [arH[:, 7168 + i * 1056:7168 + (i + 1) * 1056] for i in range(3)]
        xrv = [x_[:, :].rearrange("p (a t) -> p a t", a=8) for x_ in xraw]
        dtraw = [sb("dtraw%d" % i, [128, 16]) for i in range(2)]
        vg = sb("vg", [128, 256], BF16)
        ez = sb("ez", [128, 256])
        spt = sb("spt", [128, 256])
        EG = sb("EG", [64, 2 * 2 * 128])
        EGv = EG[:, :].rearrange("q (d p i) -> q d p i", d=2, p=2)
        EGn = sb("EGn", [64, 512])
        EGnv = EGn[:, :].rearrange("q (d p i) -> q d p i", d=2, p=2)
        gdec = sb("gdec", [64, 4])
        ED = sb("ED", [128, 256])
        qtg = sb("qtg", [64, 512], BF16)
        qtgv = qtg[:, :].rearrange("q (d p i) -> q d p i", d=2, p=2)
        ktg = sb("ktg", [64, 1024], BF16)
        ktgv = ktg[:, :].rearrange("q (d a p i) -> q d a p i", d=2, a=2, p=2)
        kTz = sb("kTz", [128, 512], BF16)
        kTzv = kTz[:, :].rearrange("q (a p i) -> q a p i", a=2, p=2)
        ksg = sb("ksg", [128, 256], BF16)
        Pg = arH[:, 6144:7168]
        Pgv = Pg[:, :].rearrange("p (d a b i) -> p d a b i", d=2, a=2, b=2)
        S_g = sb("S_g", [64, 256]); Sb_g = sb("Sb_g", [64, 256], BF16)
        S_s = sb("S_s", [128, 512]); Sb_s = sb("Sb_s", [128, 512], BF16)
        S_r = sb("S_r", [128, 256]); Sb_r = sb("Sb_r", [128, 256], BF16)
        tmpS = sb("tmpS", [128, 512])
        stA = sb("stA", [128, 1026]); stB = sb("stB", [128, 1536], BF16)
        stS = sb("stS", [128, 1040]); stT = sb("stT", [128, 768], BF16)
        ropetmp = sb("ropetmp", [128, 512])
        qr = sb("qr", [128, 256], BF16); kr = sb("kr", [128, 256], BF16); vr = sb("vr", [128, 256], BF16)
        qkT = sb("qkT", [128, 512], BF16)
        qkTv = qkT[:, :].rearrange("p (a t) -> p a t", a=4)
        Pr = sb("Pr", [128, 512], BF16)
        Prv = Pr[:, :].rearrange("p (a b i) -> p a b i", a=2, b=2)
        vtl = sb("vtl", [128, 512], BF16)
        xs = sb("xs", [128, 512], BF16); Btok = sb("Btok", [128, 256], BF16)
        BCT = sb("BCT", [128, 512], BF16)
        BCTv = BCT[:, :].rearrange("p (a t) -> p a t", a=4)
        dte = sb("dte", [128, 16]); dtv = sb("dtv", [128, 16]); lgt = sb("lgt", [128, 16])
        negG = sb("negG", [128, 16]); wexp = sb("wexp", [128, 16]); decrep = sb("decrep", [128, 16]); EGi = sb("EGi", [128, 16])
        gts = sb("gts", [8, 256])
        Grp = arF[:, 0:2048]
        Grpv = Grp[:, :].rearrange("p (h d i) -> p h d i", h=8, d=2)
        Lm = arH[:, 0:2048]
        Lmv = Lm[:, :].rearrange("p (d h i) -> p d h i", d=2, h=8)
        CBm = arF[:, 2048:2560]
        CBmv = CBm[:, :].rearrange("p (d g i) -> p d g i", d=2, g=2)
        Ps = arH[:, 2048:4096]
        Psv = Ps[:, :].rearrange("p (d h i) -> p d h i", d=2, h=8)
        vts = arH[:, 4096:5120]
        xss = arH[:, 5120:6144]
        Oall = arF[:, 0:1024]
        mixed = arH[:, 0:1024]
        mixT = arH[:, 1024:2048]
        mixTv = mixT[:, :].rearrange("p (k t) -> p k t", k=8)
        fsq = arF[:, 1024:1536]
        sil = arF[:, 1536:2048]
        actT = arH[:, 2048:2048 + NJ * 128]
        actTv = actT[:, :].rearrange("p (j t) -> p j t", j=NJ)
        sgt = arF[:, 2048:2176]
        psf = [es.enter_context(nc.psum_tensor("psf%d" % i, [128, 512], F32)) for i in range(6)]
        psb = [es.enter_context(nc.psum_tensor("psb%d" % i, [128, 1024], BF16)) for i in range(2)]
        free_f = list(range(6)); free_b = list(range(2))
        def PSF():
            i = free_f.pop(0)
            return psf[i], "psf%d" % i
        def PSB():
            i = free_b.pop(0)
            return psb[i], "psb%d" % i
        def REL(*keys):
            for key in keys:
                (free_f if key.startswith("psf") else free_b).append(int(key[3:]))

        op = P.op
        def bc(ap, shape):
            return ap.to_broadcast(shape)

        ARKEYS = ["Grp", "CBm", "Lm", "Ps", "vts", "xss", "Pg", "xraw0", "xraw1", "xraw2", "Og", "Os", "Or", "fsq", "fsqb", "fsq2",
                  "sil", "sgt", "mixed", "mixT", "actT", "wgu0", "wgu1", "wgu0u", "wgu1u"]
        bard = sb("bard", [128, 1])
        def barrier():
            op("pool", lambda e: e.memset(bard[:, :], 0.0), reads=[], writes=ARKEYS + ["bard"])
        op("sp", lambda e: e.dma_start(out=constf[:, :], in_=constf_d[:, :]), writes=["constf"], dma=True)
        op("sp", lambda e: e.dma_start(out=cve[:, :], in_=cvec[:, :]), writes=["cve"], dma=True)
        op("dve", lambda e: e.tensor_copy(out=identb[:, :], in_=C("ident")), reads=["constf"], writes=["identb"])
        op("dve", lambda e: e.tensor_copy(out=onesb[:, :], in_=C("ones")), reads=["constf"], writes=["onesb"])
        op("act", lambda e: e.activation(out=csil[:, :], in_=cve[:, :], func=AF.Silu), reads=["cve"], writes=["csil"])
        for nm_, tn_ in (("stA", stA), ("stB", stB), ("stS", stS), ("stT", stT)):
            op("pool", lambda e, tn_=tn_: e.memset(tn_[:, :], 0.0), writes=[nm_])
        for t in range(NT):
            op("sp", lambda e, t=t: e.dma_start(out=res_d[t * 128:(t + 1) * 128, :], in_=xin[t * 128:(t + 1) * 128, :]),
               writes=[("res", t)], dma=True)
        def cast(dst, src, rows, key, l, piece=128):
            for r0 in range(0, rows, piece):
                op("pool", lambda e, r0=r0: e.dma_start(out=dst[l, r0:r0 + piece, :], in_=src[l, r0:r0 + piece, :]),
                   writes=[(key, l, r0 // piece)], dma=True)
        for l in range(depth):
            cast(wb_ada, ada_d, D, "wada", l)
            cast(wb_in, w_in_d, D, "win", l)
            cast(wb_out, w_out_d, D, "wout", l)
            cast(wb_13, w13_d, D, "w13", l)
            cast(wb_2, w2_d, FH, "w2", l)
        W8 = lambda key, l: [(key, l, i) for i in range(8)]
        W22 = lambda key, l: [(key, l, i) for i in range(22)]

        def rstd_from(ss_ap, out_ap, n, reads, writes):
            rows = slice(0, 128)
            op("act", lambda e: e.activation(out=out_ap, in_=ss_ap, func=AF.Ln, scale=1.0 / n, bias=C("eps")),
               reads=list(reads) + ["constf"], writes=writes)
            op("act", lambda e: e.activation(out=out_ap, in_=out_ap, func=AF.Exp, scale=-0.5),
               reads=writes, writes=writes)

        try:
          _chk("prologue")
          for l in range(depth):
              last = (l == depth - 1)
              op("sp", lambda e, l=l: e.dma_start(out=pl[:, :], in_=pl_d[l, :, :]), writes=["pl"], dma=True)
              pm, pmk = PSF()
              for piece in range(12):
                  op("sp", lambda e, l=l, piece=piece: e.dma_start(
                      out=adawv, in_=wb_ada[l, :, piece * 512:(piece + 1) * 512].rearrange("(k p) c -> p k c", p=128)),
                     reads=W8("wada", l), writes=["adaw"], dma=True)
                  for cc in range(4):
                      ch = piece * 4 + cc
                      for k in range(8):
                          op("pe", lambda e, ch=ch, cc=cc, k=k: e.matmul(pm[:, ch * 2:ch * 2 + 2], lhsT=adawv[:, k, cc * 128:(cc + 1) * 128],
                                                                           rhs=csil[:, 2 * k:2 * k + 2],
                                                                           start=(k == 0), stop=(k == 7)),
                             reads=["adaw", "csil"], writes=[pmk])
              op("dve", lambda e: e.tensor_tensor(out=modv, in0=pm[:, 0:96].rearrange("p (c s) -> p c s", s=2),
                                                  in1=PL("adab").unsqueeze(2).to_broadcast([128, 48, 2]), op=ALU.add),
                 reads=[pmk, "pl"], writes=["modT"])
              REL(pmk)
              for s in range(2):
                  def mv(c0, s=s):
                      return modv[:, c0:c0 + 8, s]
                  op("dve", lambda e, s=s, mv=mv: e.scalar_tensor_tensor(out=ABv[:, s, 0, :], in0=mv(8), scalar=1.0, in1=PL("npre"), op0=ALU.add, op1=ALU.mult),
                     reads=["modT", "pl"], writes=["AB"])
                  op("dve", lambda e, s=s, mv=mv: e.tensor_copy(out=ABv[:, s, 1, :], in_=mv(0)), reads=["modT"], writes=["AB"])
                  op("dve", lambda e, s=s, mv=mv: e.tensor_tensor(out=ABv[:, s, 2, :], in0=mv(16), in1=PL("npost"), op=ALU.mult), reads=["modT", "pl"], writes=["AB"])
                  op("dve", lambda e, s=s, mv=mv: e.scalar_tensor_tensor(out=ABv[:, s, 3, :], in0=mv(32), scalar=1.0, in1=PL("nfpre"), op0=ALU.add, op1=ALU.mult),
                     reads=["modT", "pl"], writes=["AB"])
                  op("dve", lambda e, s=s, mv=mv: e.tensor_copy(out=ABv[:, s, 4, :], in_=mv(24)), reads=["modT"], writes=["AB"])
                  op("dve", lambda e, s=s, mv=mv: e.tensor_tensor(out=ABv[:, s, 5, :], in0=mv(40), in1=PL("nfpost"), op=ALU.mult), reads=["modT", "pl"], writes=["AB"])
                  for vi, vsrc in enumerate((2, 5)):
                      for half in range(2):
                          pg_, pgk = PSF()
                          for c4 in range(4):
                              k = half * 4 + c4
                              op("dve", lambda e, s=s, vsrc=vsrc, k=k: e.tensor_scalar(out=dg[:, :], in0=C("ident"), scalar1=ABv[:, s, vsrc, k:k + 1], scalar2=None, op0=ALU.mult),
                                 reads=["AB", "constf"], writes=["dg"])
                              op("pe", lambda e, pg_=pg_, c4=c4: e.matmul(pg_[:, c4 * 128:(c4 + 1) * 128], lhsT=C("ones"), rhs=dg[:, :], start=True, stop=True),
                                 reads=["dg", "constf"], writes=[pgk])
                          op("act", lambda e, s=s, vi=vi, half=half, pg_=pg_: e.activation(out=Gv[:, s, vi, half * 512:(half + 1) * 512], in_=pg_[:, :], func=AF.Copy),
                             reads=[pgk], writes=["Grep"])
                          REL(pgk)
              _chk("mod")
              op("dve", lambda e: e.tensor_copy(out=GUb[:, :], in_=PL("GU", slice(0, 64))), reads=["pl"], writes=["GUb"])
              op("pool", lambda e: e.memset(cbrowb[:, :], 0.0), writes=["cbrowb"])
              op("pool", lambda e, l=l: e.dma_start(out=cbrowb[0:1, :], in_=cbrow_d[l, :, :]), writes=["cbrowb"], dma=True)
              op("act", lambda e: e.activation(out=Arep[:, :], in_=PL("alog"), func=AF.Exp), reads=["pl"], writes=["Arep"])
              op("dve", lambda e: e.tensor_scalar(out=Arep[:, :], in0=Arep[:, :], scalar1=-1.0, scalar2=None, op0=ALU.mult), reads=["Arep"], writes=["Arep"])
              cwv = PL("convw").rearrange("p (a k) -> p a k", a=8)
              for ct in range(8):
                  for k in range(5):
                      op("dve", lambda e, ct=ct, k=k: e.tensor_scalar(out=cdv[:, ct, k, :], in0=C("ident"), scalar1=cwv[:, ct, k:k + 1], scalar2=None, op0=ALU.mult),
                         reads=["pl", "constf"], writes=["cdiag"])
              op("sp", lambda e, l=l: e.dma_start(out=w_in_s, in_=wb_in[l, :, :].rearrange("(k p) c -> p k c", p=128)),
                 reads=W8("win", l), writes=["wbig"], dma=True)
              op("pool", lambda e: e.memset(lrT[:, :], 1.0), writes=["lrT"])
              op("pool", lambda e: e.memset(ktg[:, :], 0.0), writes=["ktg"])
              op("pool", lambda e: e.memset(kTz[:, :], 0.0), writes=["kTz"])
              for nm, tns in (("S_g", S_g), ("S_s", S_s), ("S_r", S_r), ("Sb_g", Sb_g), ("Sb_s", Sb_s), ("Sb_r", Sb_r)):
                  op("pool", lambda e, tns=tns: e.memset(tns[:, :], 0.0), writes=[nm])

              _chk("derived")
              barrier()
              def ssd_tile(t):
                  seg = 0 if t < NCT else 1
                  u = xrv[t % 3]
                  uk = "xraw%d" % (t % 3)
                  px, pxk = PSF(); pB, pBk = PSF(); pBC, pBCk = PSF()
                  for ct in range(6):
                      o_ = px[:, ct * 128:(ct + 1) * 128] if ct < 4 else pB[:, (ct - 4) * 128:(ct - 3) * 128]
                      ok_ = pxk if ct < 4 else pBk
                      for k in range(5):
                          op("pe", lambda e, o_=o_, ct=ct, k=k: e.matmul(o_, lhsT=u[:, ct, k:k + 128], rhs=cdv[:, ct, k, :], start=(k == 0), stop=False),
                             reads=[uk, "cdiag"], writes=[ok_])
                      op("pe", lambda e, o_=o_, ct=ct: e.matmul(o_, lhsT=onesb[0:64, :], rhs=cbrowb[0:64, ct * 128:(ct + 1) * 128], start=False, stop=True, tile_position=(0, 0)),
                         reads=["onesb", "cbrowb"], writes=[ok_])
                  for idx, ct in enumerate((4, 5, 6, 7)):
                      for k in range(5):
                          op("pe", lambda e, idx=idx, ct=ct, k=k: e.matmul(pBC[:, idx * 128:(idx + 1) * 128], lhsT=cdv[:, ct, k, :], rhs=u[:, ct, k:k + 128], start=(k == 0), stop=(k == 4)),
                             reads=[uk, "cdiag"], writes=[pBCk])
                  op("act", lambda e: e.activation(out=xs[:, :], in_=px[:, :], func=AF.Silu), reads=[pxk], writes=["xs"])
                  op("act", lambda e: e.activation(out=Btok[:, :], in_=pB[:, 0:256], func=AF.Silu), reads=[pBk], writes=["Btok"])
                  for idx, ct in enumerate((4, 5, 6, 7)):
                      a0 = PLC["convb"][0]
                      op("act", lambda e, idx=idx, ct=ct, a0=a0: e.activation(out=BCTv[:, idx, :], in_=pBC[:, idx * 128:(idx + 1) * 128], func=AF.Silu, bias=pl[:, a0 + ct:a0 + ct + 1]),
                         reads=[pBCk, "pl"], writes=["BCT"])
                  REL(pxk, pBk, pBCk)
                  op("pool", lambda e: e.tensor_copy(out=stT[:, 0:512], in_=xs[:, :]), reads=["xs"], writes=["stT"])
                  op("pool", lambda e: e.tensor_copy(out=stT[:, 512:768], in_=BCT[:, 256:512]), reads=["BCT"], writes=["stT"])
                  _chk("S%da" % t)
                  dr = dtraw[t % 2]; drk = "dtraw%d" % (t % 2)
                  op("act", lambda e: e.activation(out=dte[:, :], in_=dr[:, :], func=AF.Exp), reads=[drk], writes=["dte"])
                  op("act", lambda e: e.activation(out=dtv[:, :], in_=dte[:, :], func=AF.Ln, bias=C("one")), reads=["dte", "constf"], writes=["dtv"])
                  op("dve", lambda e: e.tensor_tensor(out=lgt[:, :], in0=dtv[:, :], in1=Arep[:, :], op=ALU.mult), reads=["dtv", "Arep"], writes=["lgt"])
                  pg2, pg2k = PSF()
                  for d in range(2):
                      U = C("maskf") if d == 0 else C("maskb")
                      SLm = C("slf") if d == 0 else C("slb")
                      op("pe", lambda e, d=d, U=U: e.matmul(pg2[:, d * 8:(d + 1) * 8], lhsT=U, rhs=lgt[:, d * 8:(d + 1) * 8], start=True, stop=True), reads=["lgt", "constf"], writes=[pg2k])
                      op("pe", lambda e, d=d, SLm=SLm: e.matmul(pg2[:, 16 + d * 8:16 + (d + 1) * 8], lhsT=SLm, rhs=lgt[:, d * 8:(d + 1) * 8], start=True, stop=True), reads=["lgt", "constf"], writes=[pg2k])
                      op("pe", lambda e, d=d, U=U: e.matmul(pg2[0:8, 64 + d * 128:64 + (d + 1) * 128], lhsT=lgt[:, d * 8:(d + 1) * 8], rhs=U, start=True, stop=True), reads=["lgt", "constf"], writes=[pg2k])
                  op("pe", lambda e: e.matmul(pg2[:, 32:48], lhsT=C("ones"), rhs=lgt[:, :], start=True, stop=True), reads=["lgt", "constf"], writes=[pg2k])
                  op("dve", lambda e: e.tensor_scalar(out=negG[:, :], in0=pg2[:, 0:16], scalar1=-1.0, scalar2=None, op0=ALU.mult), reads=[pg2k], writes=["negG"])
                  op("act", lambda e: e.activation(out=EGi[:, :], in_=pg2[:, 0:16], func=AF.Exp), reads=[pg2k], writes=["EGi"])
                  op("act", lambda e: e.activation(out=wexp[:, :], in_=pg2[:, 16:32], func=AF.Exp), reads=[pg2k], writes=["wexp"])
                  op("act", lambda e: e.activation(out=decrep[:, :], in_=pg2[:, 32:48], func=AF.Exp), reads=[pg2k], writes=["decrep"])
                  op("dve", lambda e: e.tensor_copy(out=gts[:, :], in_=pg2[0:8, 64:320]), reads=[pg2k], writes=["gts"])
                  REL(pg2k)
                  sl = t % 4
                  op("sp", lambda e, sl=sl: e.dma_start(out=gscr_d[sl, :, :], in_=gts[:, :]), reads=["gts"], writes=[("gscr", sl)], dma=True)
                  op("sp", lambda e, sl=sl: e.dma_start(out=Grp[:, :], in_=gscr_d[sl:sl + 1, :, :].rearrange("o h c -> o (h c)").to_broadcast([128, 2048])),
                     reads=[("gscr", sl)], writes=["Grp"], dma=True)
                  for d in range(2):
                      for h in range(8):
                          op("dve", lambda e, d=d, h=h: e.tensor_scalar(out=Grpv[:, h, d, :], in0=Grpv[:, h, d, :], scalar1=negG[:, d * 8 + h:d * 8 + h + 1], scalar2=0.0, op0=ALU.add, op1=ALU.min),
                             reads=["Grp", "negG"], writes=["Grp"])
                          op("act", lambda e, d=d, h=h: e.activation(out=Lmv[:, d, h, :], in_=Grpv[:, h, d, :], func=AF.Exp),
                             reads=["Grp"], writes=["Lm"])
                  _chk("S%db" % t)
                  pcb, pcbk = PSF()
                  for g in range(2):
                      op("pe", lambda e, g=g: e.matmul(pcb[:, g * 128:(g + 1) * 128], lhsT=BCTv[:, g, :], rhs=BCTv[:, 2 + g, :], start=True, stop=True), reads=["BCT"], writes=[pcbk])
                  for d in range(2):
                      mk = C("maskf") if d == 0 else C("maskb")
                      op("dve", lambda e, d=d, mk=mk: e.tensor_tensor(out=CBmv[:, d, :, :], in0=pcb[:, 0:256].rearrange("p (g i) -> p g i", g=2),
                                                                      in1=mk.unsqueeze(1).to_broadcast([128, 2, 128]), op=ALU.mult), reads=[pcbk, "constf"], writes=["CBm"])
                  REL(pcbk)
                  for d in range(2):
                      for g in range(2):
                          op("dve", lambda e, d=d, g=g: e.scalar_tensor_tensor(out=Psv[:, d, 4 * g:4 * g + 4, :], in0=Lmv[:, d, 4 * g:4 * g + 4, :], scalar=1.0,
                                                                               in1=CBmv[:, d, g, :].unsqueeze(1).to_broadcast([128, 4, 128]), op0=ALU.min, op1=ALU.mult),
                             reads=["Lm", "CBm"], writes=["Ps"])
                      op("dve", lambda e, d=d: e.tensor_tensor(out=vts[:, d * 512:(d + 1) * 512].rearrange("p (h c) -> p h c", h=8), in0=xs[:, :].rearrange("p (h c) -> p h c", h=8),
                                                               in1=dtv[:, d * 8:(d + 1) * 8].unsqueeze(2).to_broadcast([128, 8, 64]), op=ALU.mult), reads=["xs", "dtv"], writes=["vts"])
                      op("dve", lambda e, d=d: e.tensor_tensor(out=xss[:, d * 512:(d + 1) * 512].rearrange("p (h c) -> p h c", h=8), in0=vts[:, d * 512:(d + 1) * 512].rearrange("p (h c) -> p h c", h=8),
                                                               in1=wexp[:, d * 8:(d + 1) * 8].unsqueeze(2).to_broadcast([128, 8, 64]), op=ALU.mult), reads=["vts", "wexp"], writes=["xss"])
                  _chk("S%dc" % t)
                  po, pok = PSF(); pi, pik = PSF()
                  for h in range(8):
                      for d in range(2):
                          op("pe", lambda e, h=h, d=d: e.matmul(po[:, h * 64:(h + 1) * 64], lhsT=Psv[:, d, h, :], rhs=vts[:, d * 512 + h * 64:d * 512 + (h + 1) * 64], start=(d == 0), stop=(d == 1)),
                             reads=["Ps", "vts"], writes=[pok])
                  for g in range(2):
                      op("pe", lambda e, g=g: e.matmul(pi[:, g * 256:(g + 1) * 256], lhsT=BCTv[:, 2 + g, :], rhs=Sb_s[:, g * 256:(g + 1) * 256], start=True, stop=True), reads=["BCT", "Sb_s"], writes=[pik])
                  op("dve", lambda e: e.tensor_tensor(out=tmpS[:, :].rearrange("p (h c) -> p h c", h=8), in0=pi[:, :].rearrange("p (h c) -> p h c", h=8),
                                                      in1=EGi[:, 0:8].unsqueeze(2).to_broadcast([128, 8, 64]), op=ALU.mult), reads=[pik, "EGi"], writes=["tmpS"])
                  op("dve", lambda e: e.tensor_tensor(out=stS[:, 0:512], in0=tmpS[:, :], in1=po[:, :], op=ALU.add), reads=["tmpS", pok], writes=["stS"])
                  REL(pok, pik)
                  op("pool", lambda e: e.tensor_copy(out=stS[:, 512:520], in_=EGi[:, 8:16]), reads=["EGi"], writes=["stS"])
                  op("pool", lambda e: e.tensor_copy(out=stS[:, 520:528], in_=decrep[:, 8:16]), reads=["decrep"], writes=["stS"])
                  pds = []
                  for d in range(2):
                      pd_, pdk = PSF()
                      pds.append((pd_, pdk))
                      for g in range(2):
                          op("pe", lambda e, d=d, g=g, pd_=pd_: e.matmul(pd_[:, g * 256:(g + 1) * 256], lhsT=Btok[:, g * 128:(g + 1) * 128], rhs=xss[:, d * 512 + g * 256:d * 512 + (g + 1) * 256], start=True, stop=True),
                             reads=["Btok", "xss"], writes=[pdk])
                  op("act", lambda e: e.activation(out=stS[:, 528:1040], in_=pds[1][0][:, :], func=AF.Copy), reads=[pds[1][1]], writes=["stS"])
                  op("dve", lambda e: e.tensor_tensor(out=tmpS[:, :].rearrange("p (h c) -> p h c", h=8), in0=S_s[:, :].rearrange("p (h c) -> p h c", h=8),
                                                      in1=decrep[:, 0:8].unsqueeze(2).to_broadcast([128, 8, 64]), op=ALU.mult), reads=["S_s", "decrep"], writes=["tmpS"])
                  op("dve", lambda e: e.tensor_tensor(out=S_s[:, :], in0=tmpS[:, :], in1=pds[0][0][:, :], op=ALU.add), reads=["tmpS", pds[0][1]], writes=["S_s"])
                  REL(pds[0][1], pds[1][1])
                  op("act", lambda e: e.activation(out=Sb_s[:, :], in_=S_s[:, :], func=AF.Copy), reads=["S_s"], writes=["Sb_s"])
                  op("sp", lambda e, t=t: e.dma_start(out=sS_d[t, :, :], in_=stS[:, :]), reads=["stS"], writes=[("sS", t)], dma=True)
                  op("sp", lambda e, t=t: e.dma_start(out=sT_d[t, :, :], in_=stT[:, :]), reads=["stT"], writes=[("sT", t)], dma=True)

              for t in range(NT):
                  seg = 0 if t < NCT else 1
                  op("sp", lambda e, t=t: e.dma_start(out=xt[:, :], in_=res_d[t * 128:(t + 1) * 128, :]), reads=[("res", t)], writes=["xt"], dma=True)
                  if seg == 1:
                      op("sp", lambda e, t=t: e.dma_start(out=rope_t[:, :], in_=rope_d[(t - NCT) * 128:(t - NCT + 1) * 128, :]), writes=["rope_t"], dma=True)
                  op("pool", lambda e: e.memset(st[:, 0:1], 0.0), writes=["st0"])
                  op("act", lambda e: e.activation(out=junk[:, :], in_=xt[:, :], func=AF.Square, accum_out=st[:, 0:1]), reads=["xt", "st0"], writes=["junk", "st0"])
                  rstd_from(st[:, 0:1], st[:, 1:2], D, ["st0"], ["st1"])
                  op("dve", lambda e: e.tensor_scalar(out=xh[:, :], in0=xt[:, :], scalar1=st[:, 1:2], scalar2=None, op0=ALU.mult), reads=["xt", "st1"], writes=["xh"])
                  pT, pTk = PSB()
                  for k in range(8):
                      op("pe", lambda e, k=k: e.transpose(pT[:, k * 128:(k + 1) * 128], xh[:, k * 128:(k + 1) * 128], identb[:, :]), reads=["xh", "identb"], writes=[pTk])
                  for k in range(8):
                      op("act", lambda e, k=k, seg=seg: e.activation(out=hTv[:, k, :], in_=pT[:, k * 128:(k + 1) * 128], func=AF.Identity,
                                                                    scale=ABv[:, seg, 0, k:k + 1], bias=ABv[:, seg, 1, k:k + 1]), reads=[pTk, "AB"], writes=["hT"])
                  REL(pTk)
                  _chk("A%da" % t)
                  pA, pAk = PSF()
                  for g in range(4):
                      for k in range(8):
                          op("pe", lambda e, g=g, k=k: e.matmul(pA[0:64, g * 128:(g + 1) * 128], lhsT=w_in_s[:, k, FM_QK + g * 64:FM_QK + (g + 1) * 64], rhs=hTv[:, k, :], start=(k == 0), stop=(k == 7)),
                             reads=["wbig", "hT"], writes=[pAk])
                  pL, pLk = PSF()
                  for k in range(8):
                      op("pe", lambda e, k=k: e.matmul(pL[0:64, 0:128], lhsT=w_in_s[:, k, FM_LR:FM_LR + 64], rhs=hTv[:, k, :], start=(k == 0), stop=(k == 7)), reads=["wbig", "hT"], writes=[pLk])
                  op("act", lambda e: e.activation(out=lrT[0:16, :], in_=pL[0:16, 0:128], func=AF.Copy), reads=[pLk], writes=["lrT"])
                  op("act", lambda e: e.activation(out=lrT[32:48, :], in_=pL[32:48, 0:128], func=AF.Copy), reads=[pLk], writes=["lrT"])
                  REL(pLk)
                  cur = xrv[t % 3]; curk = "xraw%d" % (t % 3)
                  prv = xrv[(t - 1) % 3]; prvk = "xraw%d" % ((t - 1) % 3)
                  first_in_seg = (t == 0 or t == NCT)
                  last_in_seg = (t == NCT - 1 or t == NT - 1)
                  pXs = []
                  for hx in range(2):
                      pX, pXk = PSF()
                      pXs.append((pX, pXk))
                      for c4 in range(4):
                          ct = hx * 4 + c4
                          for k in range(8):
                              op("pe", lambda e, pX=pX, c4=c4, ct=ct, k=k: e.matmul(pX[:, c4 * 128:(c4 + 1) * 128], lhsT=w_in_s[:, k, FM_X + ct * 128:FM_X + (ct + 1) * 128], rhs=hTv[:, k, :], start=(k == 0), stop=(k == 7)),
                                 reads=["wbig", "hT"], writes=[pXk])
                  for hx in range(2):
                      pX, pXk = pXs[hx]
                      pv = pX[:, :].rearrange("p (a t) -> p a t", a=4)
                      op("act", lambda e, hx=hx, pv=pv: e.activation(out=cur[:, hx * 4:(hx + 1) * 4, 2:130], in_=pv, func=AF.Copy), reads=[pXk], writes=[curk])
                      if first_in_seg:
                          op("pool", lambda e, hx=hx: e.memset(cur[:, hx * 4:(hx + 1) * 4, 0:2], 0.0), writes=[curk])
                      else:
                          op("dve", lambda e, hx=hx, pv=pv: e.tensor_copy(out=prv[:, hx * 4:(hx + 1) * 4, 130:132], in_=pv[:, :, 0:2]), reads=[pXk], writes=[prvk])
                          op("pool", lambda e, hx=hx: e.tensor_copy(out=cur[:, hx * 4:(hx + 1) * 4, 0:2], in_=prv[:, hx * 4:(hx + 1) * 4, 128:130]), reads=[prvk], writes=[curk])
                      if last_in_seg:
                          op("pool", lambda e, hx=hx: e.memset(cur[:, hx * 4:(hx + 1) * 4, 130:132], 0.0), writes=[curk])
                      REL(pXk)
                  banks = {}
                  for nm, c0, w in (("A", TM_A, 512), ("B", TM_B, 512), ("C", TM_C, 512), ("D", TM_D, 512), ("E", TM_E, 256)):
                      pb_, pbk = PSF()
                      banks[nm] = (pb_, pbk)
                      for k in range(8):
                          op("pe", lambda e, pb_=pb_, c0=c0, w=w, k=k: e.matmul(pb_[:, 0:w], lhsT=hTv[:, k, :], rhs=w_in_s[:, k, c0:c0 + w], start=(k == 0), stop=(k == 7)),
                             reads=["wbig", "hT"], writes=[pbk])
                  bA, bAk = banks["A"]; bB, bBk = banks["B"]; bC, bCk = banks["C"]; bD, bDk = banks["D"]; bE, bEk = banks["E"]
                  op("act", lambda e: e.activation(out=vg[:, :], in_=bA[:, 128:384], func=AF.Copy), reads=[bAk], writes=["vg"])
                  op("dve", lambda e, t=t: e.tensor_tensor(out=dtraw[t % 2][:, :], in0=bA[:, 384:400], in1=PL("dtb"), op=ALU.add), reads=[bAk, "pl"], writes=["dtraw%d" % (t % 2)])
                  op("act", lambda e: e.activation(out=stB[:, 0:512], in_=bB[:, :], func=AF.Copy), reads=[bBk], writes=["stB"])
                  op("act", lambda e: e.activation(out=stB[:, 512:1024], in_=bC[:, :], func=AF.Copy), reads=[bCk], writes=["stB"])
                  op("act", lambda e: e.activation(out=vr[:, :], in_=bE[:, 0:256], func=AF.Copy), reads=[bEk], writes=["vr"])
                  REL(bBk, bCk, bEk)
                  if seg == 1:
                      for which, src0, dst, tb0 in ((0, 0, qr, 0), (1, 256, kr, 128)):
                          sv = bD[:, src0:src0 + 256].rearrange("p (h s c) -> p h s c", h=4, s=2)
                          tv = ropetmp[:, 0:256].rearrange("p (h s c) -> p h s c", h=4, s=2)
                          tv2 = ropetmp[:, 256:512].rearrange("p (h s c) -> p h s c", h=4, s=2)
                          cosv = rope_t[:, tb0:tb0 + 64].rearrange("p (s c) -> p s c", s=2)
                          sinv = rope_t[:, tb0 + 64:tb0 + 128].rearrange("p (s c) -> p s c", s=2)
                          op("dve", lambda e, sv=sv, tv=tv, cosv=cosv: e.tensor_tensor(out=tv, in0=sv, in1=cosv.unsqueeze(1).to_broadcast([128, 4, 2, 32]), op=ALU.mult), reads=[bDk, "rope_t"], writes=["ropetmp"])
                          op("dve", lambda e, sv=sv, tv2=tv2, sinv=sinv: e.tensor_tensor(out=tv2[:, :, 0, :], in0=sv[:, :, 1, :], in1=sinv[:, 0, :].unsqueeze(1).to_broadcast([128, 4, 32]), op=ALU.mult), reads=[bDk, "rope_t"], writes=["ropetmp2"])
                          op("dve", lambda e, sv=sv, tv2=tv2, sinv=sinv: e.tensor_tensor(out=tv2[:, :, 1, :], in0=sv[:, :, 0, :], in1=sinv[:, 1, :].unsqueeze(1).to_broadcast([128, 4, 32]), op=ALU.mult), reads=[bDk, "rope_t"], writes=["ropetmp2"])
                          op("dve", lambda e, dst=dst: e.tensor_tensor(out=dst[:, :], in0=ropetmp[:, 0:256], in1=ropetmp[:, 256:512], op=ALU.add), reads=["ropetmp", "ropetmp2"], writes=["qr" if which == 0 else "kr"])
                          _chk("R%dw%d" % (t, which))
                  else:
                      op("act", lambda e: e.activation(out=qr[:, :], in_=bD[:, 0:256], func=AF.Copy, scale=0.125), reads=[bDk], writes=["qr"])
                      op("act", lambda e: e.activation(out=kr[:, :], in_=bD[:, 256:512], func=AF.Copy), reads=[bDk], writes=["kr"])
                  REL(bDk)
                  _chk("A%db" % t)
                  pz, pzk = PSF()
                  op("pe", lambda e: e.matmul(pz[:, 0:256], lhsT=lrT[0:64, :], rhs=GUb[0:64, :], start=True, stop=True, tile_position=(0, 0)), reads=["lrT", "GUb"], writes=[pzk])
                  _chk("A%db0" % t)
                  op("act", lambda e: e.activation(out=ez[:, :], in_=pz[:, 0:256], func=AF.Exp, scale=-1.0), reads=[pzk], writes=["ez"])
                  REL(pzk)
                  _chk("A%db0e" % t)
                  op("act", lambda e: e.activation(out=spt[:, :], in_=ez[:, :], func=AF.Ln, bias=C("one")), reads=["ez", "constf"], writes=["spt"])
                  _chk("A%db1" % t)
                  pGs = []
                  for d in range(2):
                      pG, pGk = PSF()
                      pGs.append((pG, pGk))
                      R = C("Rf") if d == 0 else C("Rb")
                      for p_ in range(2):
                          op("pe", lambda e, pG=pG, d=d, p_=p_, R=R: e.matmul(pG[0:64, p_ * 129:(p_ + 1) * 129], lhsT=spt[:, d * 128 + p_ * 64:d * 128 + (p_ + 1) * 64], rhs=R, start=True, stop=True),
                             reads=["spt", "constf"], writes=[pGk])
                  pD, pDk = PSF()
                  for d in range(2):
                      Lc = C("Lf") if d == 0 else C("Lb")
                      op("pe", lambda e, d=d, Lc=Lc: e.matmul(pD[:, d * 128:(d + 1) * 128], lhsT=Lc, rhs=spt[:, d * 128:(d + 1) * 128], start=True, stop=True), reads=["spt", "constf"], writes=[pDk])
                  for d in range(2):
                      pG, pGk = pGs[d]
                      gv = pG[0:64, 0:258].rearrange("q (p i) -> q p i", p=2)
                      op("act", lambda e, d=d, gv=gv: e.activation(out=EGv[:, d, :, :], in_=gv[:, :, 0:128], func=AF.Exp, bias=C("lnqs", slice(0, 64))), reads=[pGk, "constf"], writes=["EG"])
                      op("act", lambda e, d=d, gv=gv: e.activation(out=EGnv[:, d, :, :], in_=gv[:, :, 0:128], func=AF.Exp, scale=-1.0), reads=[pGk], writes=["EGn"])
                      op("act", lambda e, d=d, gv=gv: e.activation(out=gdec[:, d * 2:(d + 1) * 2], in_=gv[:, :, 128], func=AF.Exp), reads=[pGk], writes=["gdec"])
                  op("act", lambda e: e.activation(out=ED[:, :], in_=pD[:, 0:256], func=AF.Exp), reads=[pDk], writes=["ED"])
                  REL(pGs[0][1], pGs[1][1], pDk)
                  _chk("A%db2" % t)
                  qv_ = pA[0:64, 0:256].rearrange("q (p i) -> q p i", p=2)
                  kv_ = pA[0:64, 256:512].rearrange("q (p i) -> q p i", p=2)
                  for d in range(2):
                      op("dve", lambda e, d=d: e.tensor_tensor(out=qtgv[:, d, :, :], in0=qv_, in1=EGv[:, d, :, :], op=ALU.mult), reads=[pAk, "EG"], writes=["qtg"])
                      for hh in range(2):
                          op("dve", lambda e, d=d, hh=hh: e.tensor_tensor(out=ktgv[32 * hh:32 * hh + 32, d, hh, :, :], in0=kv_[32 * hh:32 * hh + 32, :, :], in1=EGnv[32 * hh:32 * hh + 32, d, :, :], op=ALU.mult), reads=[pAk, "EGn"], writes=["ktg"])
                      op("dve", lambda e, d=d: e.tensor_tensor(out=ksg[:, d * 128:(d + 1) * 128], in0=bA[:, 0:128], in1=ED[:, d * 128:(d + 1) * 128], op=ALU.mult), reads=[bAk, "ED"], writes=["ksg"])
                  REL(pAk, bAk)
                  _chk("A%db2d" % t)
                  for d in range(2):
                      mk = C("maskf") if d == 0 else C("maskb")
                      for hh in range(2):
                          pS, pSk = PSF()
                          for p_ in range(2):
                              op("pe", lambda e, pS=pS, d=d, p_=p_, hh=hh: e.matmul(pS[:, p_ * 128:(p_ + 1) * 128], lhsT=ktgv[:, d, hh, p_, :], rhs=qtgv[:, d, p_, :], start=True, stop=True, tile_position=(0, 0)),
                                 reads=["ktg", "qtg"], writes=[pSk])
                          op("dve", lambda e, pS=pS, d=d, hh=hh, mk=mk: e.tensor_tensor(out=Pgv[:, d, hh, :, :], in0=pS[:, 0:256].rearrange("p (b i) -> p b i", b=2),
                                                                                in1=mk.unsqueeze(1).to_broadcast([128, 2, 128]), op=ALU.mult), reads=[pSk, "constf"], writes=["Pg"])
                          REL(pSk)
                  _chk("A%db3" % t)
                  pO, pOk = PSF()
                  for p_ in range(2):
                      op("pe", lambda e, p_=p_: e.matmul(pO[:, p_ * 128:(p_ + 1) * 128], lhsT=qtgv[:, 0, p_, :], rhs=Sb_g[:, p_ * 128:(p_ + 1) * 128], start=True, stop=False, skip_group_check=True, tile_position=(0, 0)),
                         reads=["qtg", "Sb_g"], writes=[pOk])
                      for h in (2 * p_, 2 * p_ + 1):
                          for d in range(2):
                              op("pe", lambda e, h=h, d=d: e.matmul(pO[:, h * 64:(h + 1) * 64], lhsT=Pgv[:, d, h % 2, h // 2, :], rhs=vg[:, h * 64:(h + 1) * 64], start=False, stop=(d == 1 and h == 2 * p_ + 1), skip_group_check=True),
                                 reads=["Pg", "vg"], writes=[pOk])
                  op("act", lambda e: e.activation(out=stA[:, 0:256], in_=pO[:, 0:256], func=AF.Copy), reads=[pOk], writes=["stA"])
                  REL(pOk)
                  _chk("A%db4" % t)
                  pDS, pDSk = PSF()
                  for d in range(2):
                      for p_ in range(2):
                          op("pe", lambda e, d=d, p_=p_: e.matmul(pDS[0:64, d * 256 + p_ * 128:d * 256 + (p_ + 1) * 128], lhsT=ksg[:, d * 128 + p_ * 64:d * 128 + (p_ + 1) * 64], rhs=vg[:, p_ * 128:(p_ + 1) * 128], start=True, stop=True),
                             reads=["ksg", "vg"], writes=[pDSk])
                  for p_ in range(2):
                      op("dve", lambda e, p_=p_: e.scalar_tensor_tensor(out=S_g[:, p_ * 128:(p_ + 1) * 128], in0=S_g[:, p_ * 128:(p_ + 1) * 128], scalar=gdec[:, p_:p_ + 1],
                                                                        in1=pDS[0:64, p_ * 128:(p_ + 1) * 128], op0=ALU.mult, op1=ALU.add), reads=["S_g", "gdec", pDSk], writes=["S_g"])
                  op("dve", lambda e: e.tensor_tensor(out=Sb_g[:, :].rearrange("q (p c) -> q p c", p=2), in0=S_g[:, :].rearrange("q (p c) -> q p c", p=2),
                                                      in1=C("bmg", slice(0, 64)).unsqueeze(1).to_broadcast([64, 2, 128]), op=ALU.mult), reads=["S_g", "constf"], writes=["Sb_g"])
                  op("act", lambda e: e.activation(out=stA[0:64, 768:1024], in_=pDS[0:64, 256:512], func=AF.Copy), reads=[pDSk], writes=["stA"])
                  REL(pDSk)
                  op("pool", lambda e: e.tensor_copy(out=stA[0:64, 1024:1026], in_=gdec[:, 2:4]), reads=["gdec"], writes=["stA"])
                  op("pool", lambda e: e.tensor_copy(out=stB[0:64, 1280:1536], in_=qtg[:, 256:512]), reads=["qtg"], writes=["stB"])
                  _chk("A%dc" % t)
                  pT2, pT2k = PSB()
                  for a_ in range(4):
                      srct = qr if a_ < 2 else kr
                      op("pe", lambda e, a_=a_, srct=srct: e.transpose(pT2[:, a_ * 128:(a_ + 1) * 128], srct[:, (a_ % 2) * 128:(a_ % 2 + 1) * 128], identb[:, :]), reads=["qr", "kr", "identb"], writes=[pT2k])
                  op("act", lambda e: e.activation(out=qkT[:, 0:256], in_=pT2[:, 0:256], func=AF.Copy), reads=[pT2k], writes=["qkT"])
                  for hh in range(2):
                      op("act", lambda e, hh=hh: e.activation(out=kTzv[64 * hh:64 * hh + 64, hh, :, :], in_=pT2[64 * hh:64 * hh + 64, 256:512].rearrange("q (p i) -> q p i", p=2), func=AF.Copy), reads=[pT2k], writes=["kTz"])
                  REL(pT2k)
                  op("pool", lambda e: e.tensor_copy(out=stB[:, 1024:1280], in_=qkT[:, 0:256]), reads=["qkT"], writes=["stB"])
                  for hh in range(2):
                      pSr, pSrk = PSF()
                      for p_ in range(2):
                          op("pe", lambda e, pSr=pSr, p_=p_, hh=hh: e.matmul(pSr[:, p_ * 128:(p_ + 1) * 128], lhsT=kTzv[:, hh, p_, :], rhs=qkTv[:, p_, :], start=True, stop=True), reads=["qkT", "kTz"], writes=[pSrk])
                      a0 = CF["retM"][0]
                      op("dve", lambda e, pSr=pSr, hh=hh, a0=a0: e.tensor_tensor(out=Pr[:, hh * 256:(hh + 1) * 256], in0=pSr[:, 0:256], in1=constf[:, a0 + hh * 256:a0 + (hh + 1) * 256], op=ALU.mult), reads=[pSrk, "constf"], writes=["Pr"])
                      REL(pSrk)
                  pOr, pOrk = PSF()
                  for h in range(4):
                      op("pe", lambda e, h=h: e.matmul(pOr[:, h * 64:(h + 1) * 64], lhsT=Prv[:, h % 2, h // 2, :], rhs=vr[:, h * 64:(h + 1) * 64], start=True, stop=True), reads=["Pr", "vr"], writes=[pOrk])
                  for p_ in range(2):
                      op("pe", lambda e, p_=p_: e.matmul(pOr[:, 256 + p_ * 128:256 + (p_ + 1) * 128], lhsT=qkTv[:, p_, :], rhs=Sb_r[:, p_ * 128:(p_ + 1) * 128], start=True, stop=True), reads=["qkT", "Sb_r"], writes=[pOrk])
                  op("dve", lambda e: e.tensor_tensor(out=tmpS[:, 0:256].rearrange("p (h c) -> p h c", h=4), in0=pOr[:, 256:512].rearrange("p (h c) -> p h c", h=4),
                                                      in1=C("retEQf").unsqueeze(2).to_broadcast([128, 4, 64]), op=ALU.mult), reads=[pOrk, "constf"], writes=["tmpS"])
                  op("dve", lambda e: e.tensor_tensor(out=stA[:, 256:512], in0=tmpS[:, 0:256], in1=pOr[:, 0:256], op=ALU.add), reads=["tmpS", pOrk], writes=["stA"])
                  REL(pOrk)
                  for d in range(2):
                      Wc = C("retWf") if d == 0 else C("retWb")
                      op("dve", lambda e, d=d, Wc=Wc: e.tensor_tensor(out=vtl[:, d * 256:(d + 1) * 256].rearrange("p (h c) -> p h c", h=4), in0=vr[:, :].rearrange("p (h c) -> p h c", h=4),
                                                                      in1=Wc.unsqueeze(2).to_broadcast([128, 4, 64]), op=ALU.mult), reads=["vr", "constf"], writes=["vtl"])
                  pDr, pDrk = PSF()
                  for d in range(2):
                      for p_ in range(2):
                          op("pe", lambda e, d=d, p_=p_: e.matmul(pDr[:, d * 256 + p_ * 128:d * 256 + (p_ + 1) * 128], lhsT=kr[:, p_ * 128:(p_ + 1) * 128], rhs=vtl[:, d * 256 + p_ * 128:d * 256 + (p_ + 1) * 128], start=True, stop=True),
                             reads=["kr", "vtl"], writes=[pDrk])
                  op("dve", lambda e: e.tensor_tensor(out=tmpS[:, 256:512].rearrange("p (h c) -> p h c", h=4), in0=S_r[:, :].rearrange("p (h c) -> p h c", h=4),
                                                      in1=C("retdec").unsqueeze(2).to_broadcast([128, 4, 64]), op=ALU.mult), reads=["S_r", "constf"], writes=["tmpS"])
                  op("dve", lambda e: e.tensor_tensor(out=S_r[:, :], in0=tmpS[:, 256:512], in1=pDr[:, 0:256], op=ALU.add), reads=["tmpS", pDrk], writes=["S_r"])
                  op("dve", lambda e: e.tensor_tensor(out=Sb_r[:, :].rearrange("p (a c) -> p a c", a=2), in0=S_r[:, :].rearrange("p (a c) -> p a c", a=2),
                                                      in1=C("bmr").unsqueeze(1).to_broadcast([128, 2, 128]), op=ALU.mult), reads=["S_r", "constf"], writes=["Sb_r"])
                  op("act", lambda e: e.activation(out=stA[:, 512:768], in_=pDr[:, 256:512], func=AF.Copy), reads=[pDrk], writes=["stA"])
                  REL(pDrk)
                  op("sp", lambda e, t=t: e.dma_start(out=sA_d[t, :, :], in_=stA[:, :]), reads=["stA"], writes=[("sA", t)], dma=True)
                  op("sp", lambda e, t=t: e.dma_start(out=sB_d[t, :, :], in_=stB[:, :]), reads=["stB"], writes=[("sB", t)], dma=True)
                  _chk("A%dd" % t)
                  if not first_in_seg:
                      ssd_tile(t - 1)
                  if last_in_seg:
                      ssd_tile(t)
                  _chk("A%d" % t)

              _chk("A")
              barrier()
              op("sp", lambda e, l=l: e.dma_start(out=w_out_s, in_=wb_out[l, :, :].rearrange("(k p) c -> p k c", p=128)), reads=W8("wout", l), writes=["wbig"], dma=True)
              op("sp", lambda e, l=l: e.dma_start(out=w2_s, in_=wb_2[l, :, :].rearrange("(j p) c -> p j c", p=128)), reads=W22("w2", l), writes=["wbig"], dma=True)
              for nm, tns in (("S_g", S_g), ("S_s", S_s), ("S_r", S_r), ("Sb_g", Sb_g), ("Sb_s", Sb_s), ("Sb_r", Sb_r)):
                  op("pool", lambda e, tns=tns: e.memset(tns[:, :], 0.0), writes=[nm])
              order = list(range(NCT - 1, -1, -1)) + list(range(NT - 1, NCT - 1, -1))
              for t in order:
                  seg = 0 if t < NCT else 1
                  op("sp", lambda e, t=t: e.dma_start(out=stA[:, :], in_=sA_d[t, :, :]), reads=[("sA", t)], writes=["stA"], dma=True)
                  op("sp", lambda e, t=t: e.dma_start(out=stB[:, :], in_=sB_d[t, :, :]), reads=[("sB", t)], writes=["stB"], dma=True)
                  op("sp", lambda e, t=t: e.dma_start(out=stS[:, :], in_=sS_d[t, :, :]), reads=[("sS", t)], writes=["stS"], dma=True)
                  op("sp", lambda e, t=t: e.dma_start(out=stT[:, :], in_=sT_d[t, :, :]), reads=[("sT", t)], writes=["stT"], dma=True)
                  pI, pIk = PSF(); pIs, pIsk = PSF()
                  for p_ in range(2):
                      op("pe", lambda e, p_=p_: e.matmul(pI[:, p_ * 128:(p_ + 1) * 128], lhsT=stB[0:64, 1280 + p_ * 128:1280 + (p_ + 1) * 128], rhs=Sb_g[:, p_ * 128:(p_ + 1) * 128], start=True, stop=True, tile_position=(0, 0)), reads=["stB", "Sb_g"], writes=[pIk])
                      op("pe", lambda e, p_=p_: e.matmul(pI[:, 256 + p_ * 128:256 + (p_ + 1) * 128], lhsT=stB[:, 1024 + p_ * 128:1024 + (p_ + 1) * 128], rhs=Sb_r[:, p_ * 128:(p_ + 1) * 128], start=True, stop=True), reads=["stB", "Sb_r"], writes=[pIk])
                  for g in range(2):
                      op("pe", lambda e, g=g: e.matmul(pIs[:, g * 256:(g + 1) * 256], lhsT=stT[:, 512 + g * 128:512 + (g + 1) * 128], rhs=Sb_s[:, g * 256:(g + 1) * 256], start=True, stop=True), reads=["stT", "Sb_s"], writes=[pIsk])
                  op("dve", lambda e: e.tensor_tensor(out=Oall[:, 0:256], in0=stA[:, 0:256], in1=pI[:, 0:256], op=ALU.add), reads=["stA", pIk], writes=["Og"])
                  op("dve", lambda e: e.tensor_tensor(out=tmpS[:, :].rearrange("p (h c) -> p h c", h=8), in0=pIs[:, :].rearrange("p (h c) -> p h c", h=8),
                                                      in1=stS[:, 512:520].unsqueeze(2).to_broadcast([128, 8, 64]), op=ALU.mult), reads=[pIsk, "stS"], writes=["tmpS"])
                  op("dve", lambda e: e.tensor_tensor(out=Oall[:, 256:768], in0=tmpS[:, :], in1=stS[:, 0:512], op=ALU.add), reads=["tmpS", "stS"], writes=["Os"])
                  op("dve", lambda e: e.tensor_tensor(out=fsq[:, 0:256].rearrange("p (h c) -> p h c", h=4), in0=pI[:, 256:512].rearrange("p (h c) -> p h c", h=4),
                                                      in1=C("retEQb").unsqueeze(2).to_broadcast([128, 4, 64]), op=ALU.mult), reads=[pIk, "constf"], writes=["fsq"])
                  op("dve", lambda e: e.tensor_tensor(out=Oall[:, 768:1024], in0=fsq[:, 0:256], in1=stA[:, 256:512], op=ALU.add), reads=["fsq", "stA"], writes=["Or"])
                  REL(pIk, pIsk)
                  for p_ in range(2):
                      op("dve", lambda e, p_=p_: e.scalar_tensor_tensor(out=S_g[:, p_ * 128:(p_ + 1) * 128], in0=S_g[:, p_ * 128:(p_ + 1) * 128], scalar=stA[0:64, 1024 + p_:1025 + p_],
                                                                        in1=stA[0:64, 768 + p_ * 128:768 + (p_ + 1) * 128], op0=ALU.mult, op1=ALU.add), reads=["S_g", "stA", pIk], writes=["S_g"])
                  op("dve", lambda e: e.tensor_tensor(out=Sb_g[:, :].rearrange("q (p c) -> q p c", p=2), in0=S_g[:, :].rearrange("q (p c) -> q p c", p=2),
                                                      in1=C("bmg", slice(0, 64)).unsqueeze(1).to_broadcast([64, 2, 128]), op=ALU.mult), reads=["S_g", "constf"], writes=["Sb_g"])
                  op("dve", lambda e: e.tensor_tensor(out=tmpS[:, :].rearrange("p (h c) -> p h c", h=8), in0=S_s[:, :].rearrange("p (h c) -> p h c", h=8),
                                                      in1=stS[:, 520:528].unsqueeze(2).to_broadcast([128, 8, 64]), op=ALU.mult), reads=["S_s", "stS", "Os", pIsk], writes=["tmpS"])
                  op("dve", lambda e: e.tensor_tensor(out=S_s[:, :], in0=tmpS[:, :], in1=stS[:, 528:1040], op=ALU.add), reads=["tmpS", "stS"], writes=["S_s"])
                  op("act", lambda e: e.activation(out=Sb_s[:, :], in_=S_s[:, :], func=AF.Copy), reads=["S_s"], writes=["Sb_s"])
                  op("dve", lambda e: e.tensor_tensor(out=fsq[:, 256:512].rearrange("p (h c) -> p h c", h=4), in0=S_r[:, :].rearrange("p (h c) -> p h c", h=4),
                                                      in1=C("retdec").unsqueeze(2).to_broadcast([128, 4, 64]), op=ALU.mult), reads=["S_r", "constf", pIk], writes=["fsq2"])
                  op("dve", lambda e: e.tensor_tensor(out=S_r[:, :], in0=fsq[:, 256:512], in1=stA[:, 512:768], op=ALU.add), reads=["fsq2", "stA"], writes=["S_r"])
                  op("dve", lambda e: e.tensor_tensor(out=Sb_r[:, :].rearrange("p (a c) -> p a c", a=2), in0=S_r[:, :].rearrange("p (a c) -> p a c", a=2),
                                                      in1=C("bmr").unsqueeze(1).to_broadcast([128, 2, 128]), op=ALU.mult), reads=["S_r", "constf"], writes=["Sb_r"])
                  _chk("Bs%d" % t)
                  if last and seg == 0:
                      continue
                  op("dve", lambda e: e.tensor_tensor(out=fsq[:, 0:256], in0=Oall[:, 0:256], in1=Oall[:, 0:256], op=ALU.mult), reads=["Og", "Or"], writes=["fsq"])
                  op("dve", lambda e: e.tensor_reduce(out=st[:, 4:8], in_=fsq[:, 0:256].rearrange("p (h c) -> p h c", h=4), axis=AX.X, op=ALU.add), reads=["fsq"], writes=["st4"])
                  rstd_from(st[:, 4:8], st[:, 8:12], 64, ["st4"], ["st8"])
                  op("dve", lambda e: e.tensor_tensor(out=fsq[:, 0:256].rearrange("p (h c) -> p h c", h=4), in0=Oall[:, 0:256].rearrange("p (h c) -> p h c", h=4),
                                                      in1=st[:, 8:12].unsqueeze(2).to_broadcast([128, 4, 64]), op=ALU.mult), reads=["Og", "st8"], writes=["fsq"])
                  op("dve", lambda e: e.tensor_tensor(out=fsq[:, 0:256], in0=fsq[:, 0:256], in1=PL("glan"), op=ALU.mult), reads=["fsq", "pl"], writes=["fsq"])
                  op("act", lambda e: e.activation(out=sil[:, 0:256], in_=stB[:, 0:256], func=AF.Silu), reads=["stB"], writes=["sil"])
                  op("dve", lambda e: e.tensor_tensor(out=mixed[:, 0:256], in0=fsq[:, 0:256], in1=sil[:, 0:256], op=ALU.mult), reads=["fsq", "sil"], writes=["mixed"])
                  op("dve", lambda e: e.tensor_tensor(out=fsq[:, :], in0=stT[:, 0:512], in1=PL("ssdd"), op=ALU.mult), reads=["stT", "pl", "mixed"], writes=["fsq"])
                  op("dve", lambda e: e.tensor_tensor(out=fsq[:, :], in0=fsq[:, :], in1=Oall[:, 256:768], op=ALU.add), reads=["fsq", "Os"], writes=["fsq"])
                  op("act", lambda e: e.activation(out=sil[:, :], in_=stB[:, 512:1024], func=AF.Silu), reads=["stB", "mixed"], writes=["sil"])
                  op("dve", lambda e: e.tensor_tensor(out=fsq[:, :], in0=fsq[:, :], in1=sil[:, :], op=ALU.mult), reads=["fsq", "sil"], writes=["fsq"])
                  op("pool", lambda e: e.memset(st[:, 12:13], 0.0), writes=["st12"])
                  op("act", lambda e: e.activation(out=sil[:, :], in_=fsq[:, :], func=AF.Square, accum_out=st[:, 12:13]), reads=["fsq", "st12"], writes=["sil", "st12"])
                  rstd_from(st[:, 12:13], st[:, 13:14], 512, ["st12"], ["st13"])
                  op("dve", lambda e: e.scalar_tensor_tensor(out=mixed[:, 256:768], in0=fsq[:, :], scalar=st[:, 13:14], in1=PL("ssdn"), op0=ALU.mult, op1=ALU.mult), reads=["fsq", "st13", "pl"], writes=["mixed"])
                  op("dve", lambda e: e.tensor_reduce(out=st[:, 16:20], in_=Oall[:, 768:1024].rearrange("p (h c) -> p h c", h=4), axis=AX.X, op=ALU.add), reads=["Or"], writes=["st16"])
                  op("dve", lambda e: e.tensor_scalar(out=st[:, 16:20], in0=st[:, 16:20], scalar1=1.0 / 64, scalar2=None, op0=ALU.mult), reads=["st16"], writes=["st16"])
                  op("dve", lambda e: e.tensor_tensor(out=fsq[:, 0:256].rearrange("p (h c) -> p h c", h=4), in0=Oall[:, 768:1024].rearrange("p (h c) -> p h c", h=4),
                                                      in1=st[:, 16:20].unsqueeze(2).to_broadcast([128, 4, 64]), op=ALU.subtract), reads=["Or", "st16", "mixed"], writes=["fsq"])
                  op("dve", lambda e: e.tensor_tensor(out=fsq[:, 256:512], in0=fsq[:, 0:256], in1=fsq[:, 0:256], op=ALU.mult), reads=["fsq"], writes=["fsqb"])
                  op("dve", lambda e: e.tensor_reduce(out=st[:, 20:24], in_=fsq[:, 256:512].rearrange("p (h c) -> p h c", h=4), axis=AX.X, op=ALU.add), reads=["fsqb"], writes=["st20"])
                  rstd_from(st[:, 20:24], st[:, 24:28], 64, ["st20"], ["st24"])
                  op("dve", lambda e: e.tensor_tensor(out=fsq[:, 0:256].rearrange("p (h c) -> p h c", h=4), in0=fsq[:, 0:256].rearrange("p (h c) -> p h c", h=4),
                                                      in1=st[:, 24:28].unsqueeze(2).to_broadcast([128, 4, 64]), op=ALU.mult), reads=["fsq", "st24", "fsqb"], writes=["fsq"])
                  op("dve", lambda e: e.tensor_tensor(out=fsq[:, 0:256], in0=fsq[:, 0:256], in1=PL("retn"), op=ALU.mult), reads=["fsq", "pl"], writes=["fsq"])
                  op("act", lambda e: e.activation(out=sil[:, 0:256], in_=stB[:, 256:512], func=AF.Silu), reads=["stB", "mixed"], writes=["sil"])
                  op("dve", lambda e: e.tensor_tensor(out=mixed[:, 768:1024], in0=fsq[:, 0:256], in1=sil[:, 0:256], op=ALU.mult), reads=["fsq", "sil"], writes=["mixed"])
                  pT, pTk = PSB()
                  for k in range(8):
                      op("pe", lambda e, k=k: e.transpose(pT[:, k * 128:(k + 1) * 128], mixed[:, k * 128:(k + 1) * 128], identb[:, :]), reads=["mixed", "identb"], writes=[pTk])
                  op("act", lambda e: e.activation(out=mixT[:, :], in_=pT[:, :], func=AF.Copy), reads=[pTk], writes=["mixT"])
                  REL(pTk)
                  op("sp", lambda e, t=t: e.dma_start(out=xt[:, :], in_=res_d[t * 128:(t + 1) * 128, :]), reads=[("res", t)], writes=["xt"], dma=True)

                  def resid_update(wsel, nK, lhs_of, vi, tag):
                      pys = []
                      op("pool", lambda e: e.memset(st[:, 28:30], 0.0), writes=["st28"])
                      for half in range(2):
                          py, pyk = PSF()
                          pys.append((py, pyk))
                          for k in range(nK):
                              op("pe", lambda e, py=py, k=k, half=half: e.matmul(py[:, :], lhsT=lhs_of(k), rhs=wsel[:, k, half * 512:(half + 1) * 512], start=(k == 0), stop=(k == nK - 1)),
                                 reads=["wbig", tag], writes=[pyk])
                          op("act", lambda e, py=py, half=half: e.activation(out=junk[:, half * 512:(half + 1) * 512], in_=py[:, :], func=AF.Square, accum_out=st[:, 28 + half:29 + half]),
                             reads=[pyk, "st28"], writes=["junk", "st28"])
                      op("dve", lambda e: e.tensor_tensor(out=st[:, 30:31], in0=st[:, 28:29], in1=st[:, 29:30], op=ALU.add), reads=["st28"], writes=["st30"])
                      rstd_from(st[:, 30:31], st[:, 31:32], D, ["st30"], ["st31"])
                      for half in range(2):
                          py, pyk = pys[half]
                          op("dve", lambda e, py=py, half=half: e.scalar_tensor_tensor(out=junk[:, half * 512:(half + 1) * 512], in0=py[:, :], scalar=st[:, 31:32],
                                                                                       in1=Gv[:, seg, vi, half * 512:(half + 1) * 512], op0=ALU.mult, op1=ALU.mult), reads=[pyk, "st31", "Grep", "junk"], writes=["junk"])
                          REL(pyk)
                      op("dve", lambda e: e.tensor_tensor(out=xt[:, :], in0=xt[:, :], in1=junk[:, :], op=ALU.add), reads=["xt", "junk"], writes=["xt"])

                  resid_update(w_out_s, 8, lambda k: mixTv[:, k, :], 0, "mixT")
                  op("pool", lambda e: e.memset(st[:, 0:1], 0.0), writes=["st0"])
                  op("act", lambda e: e.activation(out=junk[:, :], in_=xt[:, :], func=AF.Square, accum_out=st[:, 0:1]), reads=["xt", "st0"], writes=["junk", "st0"])
                  rstd_from(st[:, 0:1], st[:, 1:2], D, ["st0"], ["st1"])
                  op("dve", lambda e: e.tensor_scalar(out=xh[:, :], in0=xt[:, :], scalar1=st[:, 1:2], scalar2=None, op0=ALU.mult), reads=["xt", "st1"], writes=["xh"])
                  pT, pTk = PSB()
                  for k in range(8):
                      op("pe", lambda e, k=k: e.transpose(pT[:, k * 128:(k + 1) * 128], xh[:, k * 128:(k + 1) * 128], identb[:, :]), reads=["xh", "identb"], writes=[pTk])
                  for k in range(8):
                      op("act", lambda e, k=k, seg=seg: e.activation(out=hTv[:, k, :], in_=pT[:, k * 128:(k + 1) * 128], func=AF.Identity,
                                                                    scale=ABv[:, seg, 3, k:k + 1], bias=ABv[:, seg, 4, k:k + 1]), reads=[pTk, "AB"], writes=["hT"])
                  REL(pTk)
                  for j in range(NJ):
                      wg_ = wgu[j % 2]; wgk = "wgu%d" % (j % 2)
                      wgv = wg_[:, :].rearrange("p (k c) -> p k c", k=8)
                      op("sp", lambda e, j=j, wgv=wgv, l=l: e.dma_start(out=wgv[:, :, 0:128], in_=wb_13[l, :, j * 128:(j + 1) * 128].rearrange("(k p) c -> p k c", p=128)), reads=W8("w13", l), writes=[wgk], dma=True)
                      op("sp", lambda e, j=j, wgv=wgv, l=l: e.dma_start(out=wgv[:, :, 128:256], in_=wb_13[l, :, FH + j * 128:FH + (j + 1) * 128].rearrange("(k p) c -> p k c", p=128)), reads=W8("w13", l), writes=[wgk + "u"], dma=True)
                      pgu, pguk = PSF()
                      for k in range(8):
                          op("pe", lambda e, pgu=pgu, wgv=wgv, k=k: e.matmul(pgu[:, 0:128], lhsT=wgv[:, k, 0:128], rhs=hTv[:, k, :], start=(k == 0), stop=(k == 7)), reads=[wgk, "hT"], writes=[pguk])
                      for k in range(8):
                          op("pe", lambda e, pgu=pgu, wgv=wgv, k=k: e.matmul(pgu[:, 128:256], lhsT=wgv[:, k, 128:256], rhs=hTv[:, k, :], start=(k == 0), stop=(k == 7)), reads=[wgk + "u", "hT"], writes=[pguk])
                      op("act", lambda e, pgu=pgu: e.activation(out=sgt[:, :], in_=pgu[:, 0:128], func=AF.Silu), reads=[pguk], writes=["sgt"])
                      op("dve", lambda e, pgu=pgu, j=j: e.tensor_tensor(out=actTv[:, j, :], in0=sgt[:, :], in1=pgu[:, 128:256], op=ALU.mult), reads=["sgt", pguk], writes=["actT"])
                      REL(pguk)
                  resid_update(w2_s, NJ, lambda k: actTv[:, k, :], 1, "actT")
                  op("sp", lambda e, t=t: e.dma_start(out=res_d[t * 128:(t + 1) * 128, :], in_=xt[:, :]), reads=["xt"], writes=[("res", t)], dma=True)
                  if last and seg == 1:
                      fo = op("sp", lambda e, t=t: e.dma_start(out=out_d[(t - NCT) * 128:(t - NCT + 1) * 128, :], in_=xt[:, :]), reads=["xt"], writes=[("out", t)], dma=True)
                      finals.append(fo)
                  _chk("B%d" % t)
        except _Stop:
            fo = op("sp", lambda e: e.dma_start(out=out_d[0:128, :], in_=xt[:, :]), reads=["xt"], writes=[("out", -1)], dma=True)
            finals.append(fo)
        lastop = {}
        for o_ in P.ops:
            lastop[(o_["eng"], o_["dma"])] = o_["idx"]
        for v_ in lastop.values():
            if v_ not in finals:
                finals.append(v_)
        P.emit(final_wait_ops=finals)
        global LASTP
        LASTP = P
    return nc


finals = []


def kernel(**inp):
    global finals
    finals = []
    inp = {k: np.asarray(v) for k, v in inp.items()}
    depth = DEPTH
    cm = _colmap()
    w_in = inp["w_in"]
    w_in_r = np.where(cm[None, None, :] >= 0, w_in[:, :, np.maximum(cm, 0)], np.float32(0)).astype(np.float32)
    constf = _host_consts()
    rope = _host_rope()
    pl = np.stack([_host_pl(inp, l) for l in range(depth)], 0)
    nc = build_nc(depth)
    in_maps = []
    for b in range(4):
        xin = np.concatenate([inp["ctx"][b], inp["x"][b]], 0).astype(np.float32)
        cv = np.zeros((128, 16), np.float32)
        cv[:, 0::2] = _fm(inp["c_ctx"], 8)
        cv[:, 1::2] = _fm(inp["c"][b], 8)
        in_maps.append({"xin": xin, "cvec": cv, "constf": constf, "rope": rope, "pl": pl, "cbrow": np.ascontiguousarray(inp["ssd_conv_b"][:depth, None, 0:768]).astype(np.float32), "w_in": w_in_r[:depth],
                        "w_out": inp["w_out"][:depth], "w13": inp["ffn_w13"][:depth], "w2": inp["ffn_w2"][:depth], "ada_w": inp["ada_w"][:depth]})
    res = run_bass_kernel_spmd(nc, in_maps, core_ids=[0, 1, 2, 3])
    return np.stack([np.asarray(r["out"], np.float32) for r in res.results], 0)
```

```python
import concourse.bass as bass
import concourse.mybir as mybir

F32 = mybir.dt.float32
BF16 = mybir.dt.bfloat16
AF = mybir.ActivationFunctionType
ALU = mybir.AluOpType
AX = mybir.AxisListType

ENGINES = ("sp", "act", "pool", "dve", "pe")
NDMASEM = 12


import types


def _freeze(fn):
    if fn.__closure__ is None:
        return fn
    cells = []
    for c in fn.__closure__:
        try:
            cells.append(types.CellType(c.cell_contents))
        except ValueError:
            cells.append(c)
    return types.FunctionType(fn.__code__, fn.__globals__, fn.__name__, fn.__defaults__, tuple(cells))


class Prog:
    def __init__(self, nc):
        self.nc = nc
        self.ops = []
        self.last_w = {}
        self.readers = {}

    def op(self, eng, fn, reads=(), writes=(), dma=False):
        i = len(self.ops)
        deps = set()
        raw = set()
        for r in reads:
            if r in self.last_w:
                deps.add(self.last_w[r])
                raw.add(self.last_w[r])
            if isinstance(r, str) and r.startswith("ps"):
                for q in self.readers.get(r, ()):
                    if self.ops[q]["eng"] != eng:
                        deps.add(q)
        for w in writes:
            if w in self.last_w:
                deps.add(self.last_w[w])
            for q in self.readers.get(w, ()):
                deps.add(q)
        for r in reads:
            self.readers.setdefault(r, []).append(i)
        for w in writes:
            self.last_w[w] = i
            self.readers[w] = []
        deps.discard(i)
        self.ops.append(dict(eng=eng, fn=_freeze(fn), deps=deps, dma=dma, idx=i))
        return i

    def emit(self, final_wait_ops=()):
        nc = self.nc
        ops = self.ops
        needed = set()
        for o in ops:
            for d in o["deps"]:
                do = ops[d]
                if do["eng"] == "pe" and o["eng"] == "pe" and not do["dma"]:
                    continue
                needed.add(d)
        for d in final_wait_ops:
            needed.add(d)
        cnt = {e: 0 for e in ENGINES}
        dcnt = {e: 0 for e in ENGINES}
        for o in ops:
            e = o["eng"]
            if o["dma"]:
                n = dcnt[e]
                dcnt[e] += 1
                o["dma_n"] = n
            elif o["idx"] in needed:
                cnt[e] += 1
                o["ticket"] = cnt[e]
        self.cnt = cnt
        import contextlib
        with contextlib.ExitStack() as es:
            sems = {e: es.enter_context(nc.semaphore("s_" + e)) for e in ENGINES}
            dsems = {e: [es.enter_context(nc.semaphore("d_%s_%d" % (e, k))) for k in range(NDMASEM)]
                     for e in ENGINES if dcnt[e] > 0}
            block = es.enter_context(nc.Block())
            per_eng = {e: [o for o in ops if o["eng"] == e] for e in ENGINES}

            def run(e, engobj, extra_final=False):
                waited = {}
                for o in per_eng[e]:
                    waits = []
                    for d in sorted(o["deps"]):
                        do = ops[d]
                        if do["dma"]:
                            n = do["dma_n"]
                            waits.append((dsems[do["eng"]][n % NDMASEM], 16 * (n // NDMASEM + 1), ("d", do["eng"], n % NDMASEM)))
                        else:
                            if do["eng"] == "pe" and e == "pe":
                                continue
                            waits.append((sems[do["eng"]], do["ticket"], ("c", do["eng"])))
                    if o["dma"]:
                        n = o["dma_n"]
                        if n >= NDMASEM:
                            waits.append((dsems[e][n % NDMASEM], 16 * (n // NDMASEM), ("d", e, n % NDMASEM)))
                    for sem, val, key in waits:
                        if waited.get(key, 0) >= val:
                            continue
                        waited[key] = val
                        engobj.wait_ge(sem, val)
                    ins = o["fn"](engobj)
                    if o["dma"]:
                        ins.then_inc(dsems[e][o["dma_n"] % NDMASEM], 16)
                    elif "ticket" in o:
                        ins.then_inc(sems[e], 1)
                if extra_final:
                    for qe in ENGINES:
                        for k in range(NDMASEM):
                            c = len(range(k, dcnt[qe], NDMASEM))
                            if c > 0:
                                engobj.wait_ge(dsems[qe][k], 16 * c)
                    for d in final_wait_ops:
                        do = ops[d]
                        if do["dma"]:
                            n = do["dma_n"]
                            engobj.wait_ge(dsems[do["eng"]][n % NDMASEM], 16 * (n // NDMASEM + 1))
                        else:
                            engobj.wait_ge(sems[do["eng"]], do["ticket"])

            @block.sync
            def _(eng):
                run("sp", eng, extra_final=True)

            @block.scalar
            def _(eng):
                run("act", eng)

            @block.gpsimd
            def _(eng):
                run("pool", eng)

            @block.vector
            def _(eng):
                run("dve", eng)

            @block.tensor
            def _(eng):
                run("pe", eng)

import contextlib
import numpy as np
from concourse.bass_utils import run_bass_kernel_spmd

D = 1024
SEQ = 4096
CTX = 256
NTOK = SEQ + CTX
NT = NTOK // 128
NCT = CTX // 128
DEPTH = 4
FH = 2816
NJ = FH // 128
FM_QK, FM_LR, FM_X, TM0 = 0, 256, 320, 1344
TM_A, TM_B, TM_C, TM_D, TM_E = TM0, TM0 + 512, TM0 + 1024, TM0 + 1536, TM0 + 2048
NCOLS = TM0 + 2304


def _colmap():
    m = -np.ones(NCOLS, dtype=np.int64)
    m[0:256] = np.arange(0, 256)
    m[FM_LR:FM_LR + 16] = np.arange(768, 784)
    m[FM_LR + 32:FM_LR + 48] = np.arange(784, 800)
    m[FM_X:FM_X + 1024] = np.arange(1312, 2336)
    m[TM_A:TM_A + 128] = np.arange(128, 256)
    m[TM_A + 128:TM_A + 384] = np.arange(256, 512)
    m[TM_A + 384:TM_A + 400] = np.arange(2336, 2352)
    m[TM_B:TM_B + 256] = np.arange(512, 768)
    m[TM_B + 256:TM_B + 512] = np.arange(3120, 3376)
    m[TM_C:TM_C + 512] = np.arange(800, 1312)
    m[TM_D:TM_D + 256] = np.arange(2352, 2608)
    m[TM_D + 256:TM_D + 512] = np.arange(2608, 2864)
    m[TM_E:TM_E + 256] = np.arange(2864, 3120)
    return m


class Cols:
    def __init__(self):
        self.off = {}
        self.n = 0

    def add(self, name, w):
        self.off[name] = (self.n, self.n + w)
        self.n += w

    def __getitem__(self, name):
        return self.off[name]


CF = Cols()
for _n, _w in [("eps", 1), ("lnqs", 1), ("one", 1), ("ident", 128), ("maskf", 128), ("maskb", 128),
               ("slf", 128), ("slb", 128), ("Rf", 129), ("Rb", 129), ("Lf", 128), ("Lb", 128), ("ones", 128),
               ("retM", 512), ("retEQf", 4), ("retEQb", 4), ("retWf", 4), ("retWb", 4), ("retdec", 4),
               ("bmg", 128), ("bmr", 128)]:
    CF.add(_n, _w)

PLC = Cols()
for _n, _w in [("adab", 48), ("npre", 8), ("npost", 8), ("nfpre", 8), ("nfpost", 8), ("GU", 256),
               ("glan", 256), ("ssdn", 512), ("retn", 256), ("ssdd", 512), ("convw", 40), ("convb", 8),
               ("dtb", 16), ("alog", 16)]:
    PLC.add(_n, _w)


def _host_consts():
    c = np.zeros((128, CF.n), np.float32)
    def put(name, arr):
        a, b = CF[name]
        c[:, a:b] = np.asarray(arr, np.float32).reshape(128, b - a)
    j = np.arange(128)[:, None]
    i = np.arange(128)[None, :]
    maskf = (j <= i).astype(np.float32)
    maskb = (j >= i).astype(np.float32)
    put("eps", np.full((128, 1), 1e-6))
    put("lnqs", np.full((128, 1), np.log(32.0 ** -0.5)))
    put("one", np.ones((128, 1)))
    put("ident", np.eye(128))
    put("maskf", maskf)
    put("maskb", maskb)
    put("slf", 1 - maskf)
    put("slb", 1 - maskb)
    put("Rf", np.concatenate([maskf, np.ones((128, 1))], 1) * (-1 / 16))
    put("Rb", np.concatenate([maskb, np.ones((128, 1))], 1) * (-1 / 16))
    put("Lf", (1 - maskf) * (-1 / 16))
    put("Lb", (1 - maskb) * (-1 / 16))
    put("ones", np.ones((128, 128)))
    lg = np.log1p(-np.exp2(-5.0 - np.arange(4, dtype=np.float32))).astype(np.float32).astype(np.float64)
    M = np.zeros((128, 4, 128))
    for h in range(4):
        M[:, (h % 2) * 2 + h // 2, :] = np.exp(lg[h] * np.abs(i - j)) * np.where(i == j, 2.0, 1.0)
    put("retM", M)
    tt = np.arange(128)[:, None].astype(np.float64)
    put("retEQf", np.exp(lg[None, :] * (tt + 1)))
    put("retEQb", np.exp(lg[None, :] * (128 - tt)))
    put("retWf", np.exp(lg[None, :] * (127 - tt)))
    put("retWb", np.exp(lg[None, :] * tt))
    put("retdec", np.tile(np.exp(lg * 128)[None, :], (128, 1)))
    p = np.arange(128)[:, None]
    cc = np.arange(128)[None, :]
    put("bmg", ((p // 32) == (cc // 64)).astype(np.float32))
    put("bmr", ((p // 64) == (cc // 64)).astype(np.float32))
    return c


def _host_rope():
    rows = SEQ // 64
    row = np.repeat(np.arange(rows), 64).astype(np.float32)
    col = np.tile(np.arange(64), rows).astype(np.float32)
    inv = (np.float32(10000.0) ** (-np.arange(16, dtype=np.float32) / np.float32(16))).astype(np.float32)
    ang = np.concatenate([row[:, None] * inv, col[:, None] * inv], -1).astype(np.float32)
    cos, sin = np.cos(ang).astype(np.float32), np.sin(ang).astype(np.float32)
    t = np.zeros((SEQ, 256), np.float32)
    t[:, 0:64] = np.concatenate([cos, cos], 1) * 0.125
    t[:, 64:128] = np.concatenate([-sin, sin], 1) * 0.125
    t[:, 128:192] = np.concatenate([cos, cos], 1)
    t[:, 192:256] = np.concatenate([-sin, sin], 1)
    return t


def _fm(v, n):
    return np.asarray(v, np.float32).reshape(n, 128).T


def _host_pl(inp, l):
    a = np.zeros((128, PLC.n), np.float32)
    def put(name, arr):
        s, e = PLC[name]
        a[:, s:e] = np.asarray(arr, np.float32).reshape(128, e - s)
    put("adab", _fm(inp["ada_b"][l], 48))
    put("npre", _fm(inp["norm_mix_pre"][l], 8))
    put("npost", _fm(inp["norm_mix_post"][l], 8))
    put("nfpre", _fm(inp["norm_ffn_pre"][l], 8))
    put("nfpost", _fm(inp["norm_ffn_post"][l], 8))
    gu = np.zeros((128, 256), np.float32)
    gu[0:16, 0:128] = inp["gla_gate_up"][l][0]
    gu[16, 0:128] = inp["gla_gate_b"][l][0]
    gu[32:48, 128:256] = inp["gla_gate_up"][l][1]
    gu[48, 128:256] = inp["gla_gate_b"][l][1]
    put("GU", gu)
    rep = lambda v: np.tile(np.asarray(v, np.float32)[None, :], (128, 1))
    put("glan", rep(inp["gla_norm"][l]))
    put("ssdn", rep(inp["ssd_norm"][l]))
    put("retn", rep(inp["ret_norm"][l]))
    put("ssdd", rep(np.repeat(inp["ssd_d"][l], 64)))
    cw = inp["ssd_conv_w"][l]
    put("convw", cw.reshape(5, 8, 128).transpose(2, 1, 0).reshape(128, 40))
    put("convb", _fm(inp["ssd_conv_b"][l], 8))
    put("dtb", rep(inp["ssd_dt_bias"][l].reshape(16)))
    put("alog", rep(inp["ssd_a_log"][l].reshape(16)))
    return a


STOP = None


class _Stop(Exception):
    pass


def _chk(stage):
    if STOP is not None and STOP == stage:
        raise _Stop()


def build_nc(depth=DEPTH, debug_layers=None):
    nc = bass.Bass("TRN2", target_bir_lowering=False)
    es = contextlib.ExitStack()
    with es:
        def din(name, shape, dt=F32):
            return nc.dram_tensor(name, shape, dt, kind="ExternalInput").ap()
        def dscr(name, shape, dt=F32):
            return nc.dram_tensor(name, shape, dt, kind="Internal").ap()
        xin = din("xin", [NTOK, D])
        cvec = din("cvec", [128, 16])
        constf_d = din("constf", [128, CF.n])
        rope_d = din("rope", [SEQ, 256])
        pl_d = din("pl", [depth, 128, PLC.n])
        cbrow_d = din("cbrow", [depth, 1, 768])
        w_in_d = din("w_in", [depth, D, NCOLS])
        w_out_d = din("w_out", [depth, D, D])
        w13_d = din("w13", [depth, D, 2 * FH])
        w2_d = din("w2", [depth, FH, D])
        ada_d = din("ada_w", [depth, D, 6 * D])
        out_d = nc.dram_tensor("out", [SEQ, D], F32, kind="ExternalOutput").ap()
        res_d = dscr("res", [NTOK, D])
        wb_in = dscr("wb_in", [depth, D, NCOLS], BF16)
        wb_out = dscr("wb_out", [depth, D, D], BF16)
        wb_13 = dscr("wb_13", [depth, NJ, 128, 8 * 256], BF16)
        wb_2 = dscr("wb_2", [depth, FH, D], BF16)
        wb_ada = dscr("wb_ada", [depth, D, 6 * D], BF16)
        sA_d = dscr("sA", [NT, 128, 1026])
        sB_d = dscr("sB", [NT, 128, 1536], BF16)
        sS_d = dscr("sS", [NT, 128, 1040])
        sT_d = dscr("sT", [NT, 128, 768], BF16)
        gscr_d = dscr("gscr", [4, 8, 256])

        def sb(name, shape, dt=F32):
            return es.enter_context(nc.sbuf_tensor("sb_" + name, shape, dt))
        P = Prog(nc)
        constf = sb("constf", [128, CF.n])
        def C(name, rows=slice(0, 128)):
            a, b = CF[name]
            return constf[rows, a:b]
        identb = sb("identb", [128, 128], BF16)
        onesb = sb("onesb", [128, 128], BF16)
        pl = sb("pl", [128, PLC.n])
        def PL(name, rows=slice(0, 128)):
            a, b = PLC[name]
            return pl[rows, a:b]
        GUb = sb("GUb", [64, 256], BF16)
        cbrowb = sb("cbrowb", [64, 768], BF16)
        Arep = sb("Arep", [128, 16])
        cdiag = sb("cdiag", [128, 8 * 5 * 128], BF16)
        cdv = cdiag[:, :].rearrange("p (a k c) -> p a k c", a=8, k=5)
        rope_t = sb("rope_t", [128, 256])
        csil = sb("csil", [128, 16], BF16)
        cve = sb("cve", [128, 16])
        modT = sb("modT", [128, 96])
        modv = modT[:, :].rearrange("p (c s) -> p c s", s=2)
        AB = sb("AB", [128, 2 * 6 * 8])
        ABv = AB[:, :].rearrange("p (s v k) -> p s v k", s=2, v=6)
        Grep_ = sb("Grep_", [128, 2 * 2 * 1024])
        Gv = Grep_[:, :].rearrange("p (s v c) -> p s v c", s=2, v=2)
        dg = sb("dg", [128, 128])
        wbig = sb("wbig", [128, 30720], BF16)
        w_in_s = wbig[:, 0:8 * NCOLS].rearrange("p (k c) -> p k c", k=8)
        w_out_s = wbig[:, 0:8192].rearrange("p (k c) -> p k c", k=8)
        w2_s = wbig[:, 8192:8192 + NJ * 1024].rearrange("p (j c) -> p j c", j=NJ)
        arF = sb("arF", [128, 2560])
        arH = sb("arH", [128, 10336], BF16)
        adaw = sb("adaw", [128, 8 * 512], BF16)
        adawv = adaw[:, :].rearrange("p (k c) -> p k c", k=8)
        wgu = [arH[:, 4864 + i * 2048:4864 + (i + 1) * 2048] for i in range(2)] + [adaw[:, i * 2048:(i + 1) * 2048] for i in range(2)]
        xt = sb("xt", [128, D])
        junk = sb("junk", [128, D])
        xh = sb("xh", [128, D], BF16)
        hT = sb("hT", [128, 2 * D], BF16)
        hTv = hT[:, 0:D].rearrange("p (k t) -> p k t", k=8)
        hT2v = hT[:, :].rearrange("p (k t) -> p k t", k=8)
        st = sb("st", [128, 32])
        lrT = sb("lrT", [64, 128], BF16)
        xraw = [arH[:, 7168 + i * 1056:7168 + (i + 1) * 1056] for i in range(3)]
        xrv = [x_[:, :].rearrange("p (a t) -> p a t", a=8) for x_ in xraw]
        dtraw = [sb("dtraw%d" % i, [128, 16]) for i in range(2)]
        vg = sb("vg", [128, 256], BF16)
        ez = sb("ez", [128, 256])
        spt = sb("spt", [128, 256])
        EG = sb("EG", [64, 2 * 2 * 128])
        EGv = EG[:, :].rearrange("q (d p i) -> q d p i", d=2, p=2)
        EGn = sb("EGn", [64, 512])
        EGnv = EGn[:, :].rearrange("q (d p i) -> q d p i", d=2, p=2)
        gdec = sb("gdec", [64, 4])
        ED = sb("ED", [128, 256])
        qtg = sb("qtg", [64, 512], BF16)
        qtgv = qtg[:, :].rearrange("q (d p i) -> q d p i", d=2, p=2)
        ktg = sb("ktg", [64, 1024], BF16)
        ktgv = ktg[:, :].rearrange("q (d a p i) -> q d a p i", d=2, a=2, p=2)
        kTz = sb("kTz", [128, 512], BF16)
        kTzv = kTz[:, :].rearrange("q (a p i) -> q a p i", a=2, p=2)
        ksg = sb("ksg", [128, 256], BF16)
        Pg = arH[:, 6144:7168]
        Pgv = Pg[:, :].rearrange("p (d a b i) -> p d a b i", d=2, a=2, b=2)
        S_g = sb("S_g", [64, 256]); Sb_g = sb("Sb_g", [64, 256], BF16)
        S_s = sb("S_s", [128, 512]); Sb_s = sb("Sb_s", [128, 512], BF16)
        S_r = sb("S_r", [128, 256]); Sb_r = sb("Sb_r", [128, 256], BF16)
        tmpS = sb("tmpS", [128, 512])
        stA = sb("stA", [128, 1026]); stB = sb("stB", [128, 1536], BF16)
        stS = sb("stS", [128, 1040]); stT = sb("stT", [128, 768], BF16)
        ropetmp = sb("ropetmp", [128, 512])
        qr = sb("qr", [128, 256], BF16); kr = sb("kr", [128, 256], BF16); vr = sb("vr", [128, 256], BF16)
        qkT = sb("qkT", [128, 512], BF16)
        qkTv = qkT[:, :].rearrange("p (a t) -> p a t", a=4)
        Pr = sb("Pr", [128, 512], BF16)
        Prv = Pr[:, :].rearrange("p (a b i) -> p a b i", a=2, b=2)
        vtl = sb("vtl", [128, 512], BF16)
        xs = sb("xs", [128, 512], BF16); Btok = sb("Btok", [128, 256], BF16)
        BCT = sb("BCT", [128, 512], BF16)
        BCTv = BCT[:, :].rearrange("p (a t) -> p a t", a=4)
        dte = sb("dte", [128, 16]); dtv = sb("dtv", [128, 16]); lgt = sb("lgt", [128, 16])
        negG = sb("negG", [128, 16]); wexp = sb("wexp", [128, 16]); decrep = sb("decrep", [128, 16]); EGi = sb("EGi", [128, 16])
        gts = sb("gts", [8, 256])
        Grp = arF[:, 0:2048]
        Grpv = Grp[:, :].rearrange("p (h d i) -> p h d i", h=8, d=2)
        Lm = arH[:, 0:2048]
        Lmv = Lm[:, :].rearrange("p (d h i) -> p d h i", d=2, h=8)
        CBm = arF[:, 2048:2560]
        CBmv = CBm[:, :].rearrange("p (d g i) -> p d g i", d=2, g=2)
        Ps = arH[:, 2048:4096]
        Psv = Ps[:, :].rearrange("p (d h i) -> p d h i", d=2, h=8)
        vts = arH[:, 4096:5120]
        xss = arH[:, 5120:6144]
        Oall = arF[:, 0:1024]
        mixed = arH[:, 0:1024]
        mixT = arH[:, 1024:2048]
        mixTv = mixT[:, :].rearrange("p (k t) -> p k t", k=8)
        fsq = arF[:, 1024:1536]
        sil = arF[:, 1536:2048]
        actT = arH[:, 2048:2048 + NJ * 128]
        actTv = actT[:, :].rearrange("p (j t) -> p j t", j=NJ)
        sgt = arF[:, 2048:2304]
        actTb = cdiag[:, 0:NJ * 128]
        actTbv = actTb.rearrange("p (j t) -> p j t", j=NJ)
        psf = [es.enter_context(nc.psum_tensor("psf%d" % i, [128, 512], F32)) for i in range(6)]
        psb = [es.enter_context(nc.psum_tensor("psb%d" % i, [128, 1024], BF16)) for i in range(2)]
        free_f = list(range(6)); free_b = list(range(2))
        def PSF():
            i = free_f.pop(0)
            return psf[i], "psf%d" % i
        def PSB():
            i = free_b.pop(0)
            return psb[i], "psb%d" % i
        def REL(*keys):
            for key in keys:
                (free_f if key.startswith("psf") else free_b).append(int(key[3:]))

        op = P.op
        def bc(ap, shape):
            return ap.to_broadcast(shape)

        ARKEYS = ["Grp", "CBm", "Lm", "Ps", "vts", "xss", "Pg", "xraw0", "xraw1", "xraw2", "Og", "Os", "Or", "fsq", "fsqb", "fsq2",
                  "sil", "sgt", "mixed", "mixT", "actT", "wgu0", "wgu1", "wgu0u", "wgu1u", "wgu2", "wgu3", "wgu2u", "wgu3u", "adaw", "cdiag", "actTb"]
        bard = sb("bard", [128, 1])
        def barrier():
            op("pool", lambda e: e.memset(bard[:, :], 0.0), reads=[], writes=ARKEYS + ["bard"])
        op("sp", lambda e: e.dma_start(out=constf[:, :], in_=constf_d[:, :]), writes=["constf"], dma=True)
        op("sp", lambda e: e.dma_start(out=cve[:, :], in_=cvec[:, :]), writes=["cve"], dma=True)
        op("dve", lambda e: e.tensor_copy(out=identb[:, :], in_=C("ident")), reads=["constf"], writes=["identb"])
        op("dve", lambda e: e.tensor_copy(out=onesb[:, :], in_=C("ones")), reads=["constf"], writes=["onesb"])
        op("act", lambda e: e.activation(out=csil[:, :], in_=cve[:, :], func=AF.Silu), reads=["cve"], writes=["csil"])
        for nm_, tn_ in (("stA", stA), ("stB", stB), ("stS", stS), ("stT", stT)):
            op("pool", lambda e, tn_=tn_: e.memset(tn_[:, :], 0.0), writes=[nm_])
        for t in range(NT):
            op("sp", lambda e, t=t: e.dma_start(out=res_d[t * 128:(t + 1) * 128, :], in_=xin[t * 128:(t + 1) * 128, :]),
               writes=[("res", t)], dma=True)
        def cast(dst, src, rows, key, l, piece=128):
            for r0 in range(0, rows, piece):
                op("pool", lambda e, r0=r0: e.dma_start(out=dst[l, r0:r0 + piece, :], in_=src[l, r0:r0 + piece, :]),
                   writes=[(key, l, r0 // piece)], dma=True)
        for l in range(depth):
            cast(wb_ada, ada_d, D, "wada", l)
            cast(wb_in, w_in_d, D, "win", l)
            cast(wb_out, w_out_d, D, "wout", l)
            for k_ in range(8):
                for part in range(2):
                    op("pool", lambda e, l=l, k_=k_, part=part: e.dma_start(
                        out=wb_13[l, :, :, k_ * 256 + part * 128:k_ * 256 + (part + 1) * 128],
                        in_=w13_d[l, k_ * 128:(k_ + 1) * 128, part * FH:(part + 1) * FH].rearrange("p (j c) -> j p c", c=128)),
                       writes=[("w13", l, k_)], dma=True)
            cast(wb_2, w2_d, FH, "w2", l)
        W8 = lambda key, l: [(key, l, i) for i in range(8)]
        W22 = lambda key, l: [(key, l, i) for i in range(22)]

        def rstd_from(ss_ap, out_ap, n, reads, writes):
            rows = slice(0, 128)
            op("act", lambda e: e.activation(out=out_ap, in_=ss_ap, func=AF.Ln, scale=1.0 / n, bias=C("eps")),
               reads=list(reads) + ["constf"], writes=writes)
            op("act", lambda e: e.activation(out=out_ap, in_=out_ap, func=AF.Exp, scale=-0.5),
               reads=writes, writes=writes)

        try:
          _chk("prologue")
          for l in range(depth):
              last = (l == depth - 1)
              barrier()
              op("sp", lambda e, l=l: e.dma_start(out=pl[:, :], in_=pl_d[l, :, :]), writes=["pl"], dma=True)
              pm, pmk = PSF()
              for piece in range(12):
                  op("sp", lambda e, l=l, piece=piece: e.dma_start(
                      out=adawv, in_=wb_ada[l, :, piece * 512:(piece + 1) * 512].rearrange("(k p) c -> p k c", p=128)),
                     reads=W8("wada", l), writes=["adaw"], dma=True)
                  for cc in range(4):
                      ch = piece * 4 + cc
                      for k in range(8):
                          op("pe", lambda e, ch=ch, cc=cc, k=k: e.matmul(pm[:, ch * 2:ch * 2 + 2], lhsT=adawv[:, k, cc * 128:(cc + 1) * 128],
                                                                           rhs=csil[:, 2 * k:2 * k + 2],
                                                                           start=(k == 0), stop=(k == 7)),
                             reads=["adaw", "csil"], writes=[pmk])
              op("dve", lambda e: e.tensor_tensor(out=modv, in0=pm[:, 0:96].rearrange("p (c s) -> p c s", s=2),
                                                  in1=PL("adab").unsqueeze(2).to_broadcast([128, 48, 2]), op=ALU.add),
                 reads=[pmk, "pl"], writes=["modT"])
              REL(pmk)
              for s in range(2):
                  def mv(c0, s=s):
                      return modv[:, c0:c0 + 8, s]
                  op("dve", lambda e, s=s, mv=mv: e.scalar_tensor_tensor(out=ABv[:, s, 0, :], in0=mv(8), scalar=1.0, in1=PL("npre"), op0=ALU.add, op1=ALU.mult),
                     reads=["modT", "pl"], writes=["AB"])
                  op("dve", lambda e, s=s, mv=mv: e.tensor_copy(out=ABv[:, s, 1, :], in_=mv(0)), reads=["modT"], writes=["AB"])
                  op("dve", lambda e, s=s, mv=mv: e.tensor_tensor(out=ABv[:, s, 2, :], in0=mv(16), in1=PL("npost"), op=ALU.mult), reads=["modT", "pl"], writes=["AB"])
                  op("dve", lambda e, s=s, mv=mv: e.scalar_tensor_tensor(out=ABv[:, s, 3, :], in0=mv(32), scalar=1.0, in1=PL("nfpre"), op0=ALU.add, op1=ALU.mult),
                     reads=["modT", "pl"], writes=["AB"])
                  op("dve", lambda e, s=s, mv=mv: e.tensor_copy(out=ABv[:, s, 4, :], in_=mv(24)), reads=["modT"], writes=["AB"])
                  op("dve", lambda e, s=s, mv=mv: e.tensor_tensor(out=ABv[:, s, 5, :], in0=mv(40), in1=PL("nfpost"), op=ALU.mult), reads=["modT", "pl"], writes=["AB"])
                  for vi, vsrc in enumerate((2, 5)):
                      for half in range(2):
                          pg_, pgk = PSF()
                          for c4 in range(4):
                              k = half * 4 + c4
                              op("dve", lambda e, s=s, vsrc=vsrc, k=k: e.tensor_scalar(out=dg[:, :], in0=C("ident"), scalar1=ABv[:, s, vsrc, k:k + 1], scalar2=None, op0=ALU.mult),
                                 reads=["AB", "constf"], writes=["dg"])
                              op("pe", lambda e, pg_=pg_, c4=c4: e.matmul(pg_[:, c4 * 128:(c4 + 1) * 128], lhsT=C("ones"), rhs=dg[:, :], start=True, stop=True),
                                 reads=["dg", "constf"], writes=[pgk])
                          op("act", lambda e, s=s, vi=vi, half=half, pg_=pg_: e.activation(out=Gv[:, s, vi, half * 512:(half + 1) * 512], in_=pg_[:, :], func=AF.Copy),
                             reads=[pgk], writes=["Grep"])
                          REL(pgk)
              _chk("mod")
              op("dve", lambda e: e.tensor_copy(out=GUb[:, :], in_=PL("GU", slice(0, 64))), reads=["pl"], writes=["GUb"])
              op("pool", lambda e: e.memset(cbrowb[:, :], 0.0), writes=["cbrowb"])
              op("pool", lambda e, l=l: e.dma_start(out=cbrowb[0:1, :], in_=cbrow_d[l, :, :]), writes=["cbrowb"], dma=True)
              op("act", lambda e: e.activation(out=Arep[:, :], in_=PL("alog"), func=AF.Exp), reads=["pl"], writes=["Arep"])
              op("dve", lambda e: e.tensor_scalar(out=Arep[:, :], in0=Arep[:, :], scalar1=-1.0, scalar2=None, op0=ALU.mult), reads=["Arep"], writes=["Arep"])
              cwv = PL("convw").rearrange("p (a k) -> p a k", a=8)
              for ct in range(8):
                  for k in range(5):
                      op("dve", lambda e, ct=ct, k=k: e.tensor_scalar(out=cdv[:, ct, k, :], in0=C("ident"), scalar1=cwv[:, ct, k:k + 1], scalar2=None, op0=ALU.mult),
                         reads=["pl", "constf"], writes=["cdiag"])
              op("sp", lambda e, l=l: e.dma_start(out=w_in_s, in_=wb_in[l, :, :].rearrange("(k p) c -> p k c", p=128)),
                 reads=W8("win", l), writes=["wbig"], dma=True)
              op("pool", lambda e: e.memset(lrT[:, :], 1.0), writes=["lrT"])
              op("pool", lambda e: e.memset(ktg[:, :], 0.0), writes=["ktg"])
              op("pool", lambda e: e.memset(kTz[:, :], 0.0), writes=["kTz"])
              for nm, tns in (("S_g", S_g), ("S_s", S_s), ("S_r", S_r), ("Sb_g", Sb_g), ("Sb_s", Sb_s), ("Sb_r", Sb_r)):
                  op("pool", lambda e, tns=tns: e.memset(tns[:, :], 0.0), writes=[nm])

              _chk("derived")
              barrier()
              def ssd_tile(t):
                  seg = 0 if t < NCT else 1
                  u = xrv[t % 3]
                  uk = "xraw%d" % (t % 3)
                  px, pxk = PSF(); pB, pBk = PSF(); pBC, pBCk = PSF()
                  for ct in range(6):
                      o_ = px[:, ct * 128:(ct + 1) * 128] if ct < 4 else pB[:, (ct - 4) * 128:(ct - 3) * 128]
                      ok_ = pxk if ct < 4 else pBk
                      for k in range(5):
                          op("pe", lambda e, o_=o_, ct=ct, k=k: e.matmul(o_, lhsT=u[:, ct, k:k + 128], rhs=cdv[:, ct, k, :], start=(k == 0), stop=False),
                             reads=[uk, "cdiag"], writes=[ok_])
                      op("pe", lambda e, o_=o_, ct=ct: e.matmul(o_, lhsT=onesb[0:64, :], rhs=cbrowb[0:64, ct * 128:(ct + 1) * 128], start=False, stop=True, tile_position=(0, 0)),
                         reads=["onesb", "cbrowb"], writes=[ok_])
                  for idx, ct in enumerate((4, 5, 6, 7)):
                      for k in range(5):
                          op("pe", lambda e, idx=idx, ct=ct, k=k: e.matmul(pBC[:, idx * 128:(idx + 1) * 128], lhsT=cdv[:, ct, k, :], rhs=u[:, ct, k:k + 128], start=(k == 0), stop=(k == 4)),
                             reads=[uk, "cdiag"], writes=[pBCk])
                  op("act", lambda e: e.activation(out=xs[:, :], in_=px[:, :], func=AF.Silu), reads=[pxk], writes=["xs"])
                  op("act", lambda e: e.activation(out=Btok[:, :], in_=pB[:, 0:256], func=AF.Silu), reads=[pBk], writes=["Btok"])
                  for idx, ct in enumerate((4, 5, 6, 7)):
                      a0 = PLC["convb"][0]
                      op("act", lambda e, idx=idx, ct=ct, a0=a0: e.activation(out=BCTv[:, idx, :], in_=pBC[:, idx * 128:(idx + 1) * 128], func=AF.Silu, bias=pl[:, a0 + ct:a0 + ct + 1]),
                         reads=[pBCk, "pl"], writes=["BCT"])
                  REL(pxk, pBk, pBCk)
                  op("pool", lambda e: e.tensor_copy(out=stT[:, 0:512], in_=xs[:, :]), reads=["xs"], writes=["stT"])
                  op("pool", lambda e: e.tensor_copy(out=stT[:, 512:768], in_=BCT[:, 256:512]), reads=["BCT"], writes=["stT"])
                  _chk("S%da" % t)
                  dr = dtraw[t % 2]; drk = "dtraw%d" % (t % 2)
                  op("act", lambda e: e.activation(out=dte[:, :], in_=dr[:, :], func=AF.Exp), reads=[drk], writes=["dte"])
                  op("act", lambda e: e.activation(out=dtv[:, :], in_=dte[:, :], func=AF.Ln, bias=C("one")), reads=["dte", "constf"], writes=["dtv"])
                  op("dve", lambda e: e.tensor_tensor(out=lgt[:, :], in0=dtv[:, :], in1=Arep[:, :], op=ALU.mult), reads=["dtv", "Arep"], writes=["lgt"])
                  pg2, pg2k = PSF()
                  for d in range(2):
                      U = C("maskf") if d == 0 else C("maskb")
                      SLm = C("slf") if d == 0 else C("slb")
                      op("pe", lambda e, d=d, U=U: e.matmul(pg2[:, d * 8:(d + 1) * 8], lhsT=U, rhs=lgt[:, d * 8:(d + 1) * 8], start=True, stop=True), reads=["lgt", "constf"], writes=[pg2k])
                      op("pe", lambda e, d=d, SLm=SLm: e.matmul(pg2[:, 16 + d * 8:16 + (d + 1) * 8], lhsT=SLm, rhs=lgt[:, d * 8:(d + 1) * 8], start=True, stop=True), reads=["lgt", "constf"], writes=[pg2k])
                      op("pe", lambda e, d=d, U=U: e.matmul(pg2[0:8, 64 + d * 128:64 + (d + 1) * 128], lhsT=lgt[:, d * 8:(d + 1) * 8], rhs=U, start=True, stop=True), reads=["lgt", "constf"], writes=[pg2k])
                  op("pe", lambda e: e.matmul(pg2[:, 32:48], lhsT=C("ones"), rhs=lgt[:, :], start=True, stop=True), reads=["lgt", "constf"], writes=[pg2k])
                  op("dve", lambda e: e.tensor_scalar(out=negG[:, :], in0=pg2[:, 0:16], scalar1=-1.0, scalar2=None, op0=ALU.mult), reads=[pg2k], writes=["negG"])
                  op("act", lambda e: e.activation(out=EGi[:, :], in_=pg2[:, 0:16], func=AF.Exp), reads=[pg2k], writes=["EGi"])
                  op("act", lambda e: e.activation(out=wexp[:, :], in_=pg2[:, 16:32], func=AF.Exp), reads=[pg2k], writes=["wexp"])
                  op("act", lambda e: e.activation(out=decrep[:, :], in_=pg2[:, 32:48], func=AF.Exp), reads=[pg2k], writes=["decrep"])
                  op("dve", lambda e: e.tensor_copy(out=gts[:, :], in_=pg2[0:8, 64:320]), reads=[pg2k], writes=["gts"])
                  REL(pg2k)
                  sl = t % 4
                  op("sp", lambda e, sl=sl: e.dma_start(out=gscr_d[sl, :, :], in_=gts[:, :]), reads=["gts"], writes=[("gscr", sl)], dma=True)
                  op("sp", lambda e, sl=sl: e.dma_start(out=Grp[:, :], in_=gscr_d[sl:sl + 1, :, :].rearrange("o h c -> o (h c)").to_broadcast([128, 2048])),
                     reads=[("gscr", sl)], writes=["Grp"], dma=True)
                  for d in range(2):
                      for h in range(8):
                          op("dve", lambda e, d=d, h=h: e.tensor_scalar(out=Grpv[:, h, d, :], in0=Grpv[:, h, d, :], scalar1=negG[:, d * 8 + h:d * 8 + h + 1], scalar2=0.0, op0=ALU.add, op1=ALU.min),
                             reads=["Grp", "negG"], writes=["Grp"])
                          op("act", lambda e, d=d, h=h: e.activation(out=Lmv[:, d, h, :], in_=Grpv[:, h, d, :], func=AF.Exp),
                             reads=["Grp"], writes=["Lm"])
                  _chk("S%db" % t)
                  pcb, pcbk = PSF()
                  for g in range(2):
                      op("pe", lambda e, g=g: e.matmul(pcb[:, g * 128:(g + 1) * 128], lhsT=BCTv[:, g, :], rhs=BCTv[:, 2 + g, :], start=True, stop=True), reads=["BCT"], writes=[pcbk])
                  for d in range(2):
                      mk = C("maskf") if d == 0 else C("maskb")
                      op("dve", lambda e, d=d, mk=mk: e.tensor_tensor(out=CBmv[:, d, :, :], in0=pcb[:, 0:256].rearrange("p (g i) -> p g i", g=2),
                                                                      in1=mk.unsqueeze(1).to_broadcast([128, 2, 128]), op=ALU.mult), reads=[pcbk, "constf"], writes=["CBm"])
                  REL(pcbk)
                  for d in range(2):
                      for g in range(2):
                          op("dve", lambda e, d=d, g=g: e.scalar_tensor_tensor(out=Psv[:, d, 4 * g:4 * g + 4, :], in0=Lmv[:, d, 4 * g:4 * g + 4, :], scalar=1.0,
                                                                               in1=CBmv[:, d, g, :].unsqueeze(1).to_broadcast([128, 4, 128]), op0=ALU.min, op1=ALU.mult),
                             reads=["Lm", "CBm"], writes=["Ps"])
                      op("dve", lambda e, d=d: e.tensor_tensor(out=vts[:, d * 512:(d + 1) * 512].rearrange("p (h c) -> p h c", h=8), in0=xs[:, :].rearrange("p (h c) -> p h c", h=8),
                                                               in1=dtv[:, d * 8:(d + 1) * 8].unsqueeze(2).to_broadcast([128, 8, 64]), op=ALU.mult), reads=["xs", "dtv"], writes=["vts"])
                      op("dve", lambda e, d=d: e.tensor_tensor(out=xss[:, d * 512:(d + 1) * 512].rearrange("p (h c) -> p h c", h=8), in0=vts[:, d * 512:(d + 1) * 512].rearrange("p (h c) -> p h c", h=8),
                                                               in1=wexp[:, d * 8:(d + 1) * 8].unsqueeze(2).to_broadcast([128, 8, 64]), op=ALU.mult), reads=["vts", "wexp"], writes=["xss"])
                  _chk("S%dc" % t)
                  po, pok = PSF(); pi, pik = PSF()
                  for h in range(8):
                      for d in range(2):
                          op("pe", lambda e, h=h, d=d: e.matmul(po[:, h * 64:(h + 1) * 64], lhsT=Psv[:, d, h, :], rhs=vts[:, d * 512 + h * 64:d * 512 + (h + 1) * 64], start=(d == 0), stop=(d == 1)),
                             reads=["Ps", "vts"], writes=[pok])
                  for g in range(2):
                      op("pe", lambda e, g=g: e.matmul(pi[:, g * 256:(g + 1) * 256], lhsT=BCTv[:, 2 + g, :], rhs=Sb_s[:, g * 256:(g + 1) * 256], start=True, stop=True), reads=["BCT", "Sb_s"], writes=[pik])
                  op("dve", lambda e: e.tensor_tensor(out=tmpS[:, :].rearrange("p (h c) -> p h c", h=8), in0=pi[:, :].rearrange("p (h c) -> p h c", h=8),
                                                      in1=EGi[:, 0:8].unsqueeze(2).to_broadcast([128, 8, 64]), op=ALU.mult), reads=[pik, "EGi"], writes=["tmpS"])
                  op("dve", lambda e: e.tensor_tensor(out=stS[:, 0:512], in0=tmpS[:, :], in1=po[:, :], op=ALU.add), reads=["tmpS", pok], writes=["stS"])
                  REL(pok, pik)
                  op("pool", lambda e: e.tensor_copy(out=stS[:, 512:520], in_=EGi[:, 8:16]), reads=["EGi"], writes=["stS"])
                  op("pool", lambda e: e.tensor_copy(out=stS[:, 520:528], in_=decrep[:, 8:16]), reads=["decrep"], writes=["stS"])
                  pds = []
                  for d in range(2):
                      pd_, pdk = PSF()
                      pds.append((pd_, pdk))
                      for g in range(2):
                          op("pe", lambda e, d=d, g=g, pd_=pd_: e.matmul(pd_[:, g * 256:(g + 1) * 256], lhsT=Btok[:, g * 128:(g + 1) * 128], rhs=xss[:, d * 512 + g * 256:d * 512 + (g + 1) * 256], start=True, stop=True),
                             reads=["Btok", "xss"], writes=[pdk])
                  op("act", lambda e: e.activation(out=stS[:, 528:1040], in_=pds[1][0][:, :], func=AF.Copy), reads=[pds[1][1]], writes=["stS"])
                  op("dve", lambda e: e.tensor_tensor(out=tmpS[:, :].rearrange("p (h c) -> p h c", h=8), in0=S_s[:, :].rearrange("p (h c) -> p h c", h=8),
                                                      in1=decrep[:, 0:8].unsqueeze(2).to_broadcast([128, 8, 64]), op=ALU.mult), reads=["S_s", "decrep"], writes=["tmpS"])
                  op("dve", lambda e: e.tensor_tensor(out=S_s[:, :], in0=tmpS[:, :], in1=pds[0][0][:, :], op=ALU.add), reads=["tmpS", pds[0][1]], writes=["S_s"])
                  REL(pds[0][1], pds[1][1])
                  op("act", lambda e: e.activation(out=Sb_s[:, :], in_=S_s[:, :], func=AF.Copy), reads=["S_s"], writes=["Sb_s"])
                  op("sp", lambda e, t=t: e.dma_start(out=sS_d[t, :, :], in_=stS[:, :]), reads=["stS"], writes=[("sS", t)], dma=True)
                  op("sp", lambda e, t=t: e.dma_start(out=sT_d[t, :, :], in_=stT[:, :]), reads=["stT"], writes=[("sT", t)], dma=True)

              for t in range(NT):
                  seg = 0 if t < NCT else 1
                  op("sp", lambda e, t=t: e.dma_start(out=xt[:, :], in_=res_d[t * 128:(t + 1) * 128, :]), reads=[("res", t)], writes=["xt"], dma=True)
                  if seg == 1:
                      op("sp", lambda e, t=t: e.dma_start(out=rope_t[:, :], in_=rope_d[(t - NCT) * 128:(t - NCT + 1) * 128, :]), writes=["rope_t"], dma=True)
                  op("pool", lambda e: e.memset(st[:, 0:1], 0.0), writes=["st0"])
                  op("act", lambda e: e.activation(out=junk[:, :], in_=xt[:, :], func=AF.Square, accum_out=st[:, 0:1]), reads=["xt", "st0"], writes=["junk", "st0"])
                  rstd_from(st[:, 0:1], st[:, 1:2], D, ["st0"], ["st1"])
                  op("dve", lambda e: e.tensor_scalar(out=xh[:, :], in0=xt[:, :], scalar1=st[:, 1:2], scalar2=None, op0=ALU.mult), reads=["xt", "st1"], writes=["xh"])
                  pT, pTk = PSB()
                  for k in range(8):
                      op("pe", lambda e, k=k: e.transpose(pT[:, k * 128:(k + 1) * 128], xh[:, k * 128:(k + 1) * 128], identb[:, :]), reads=["xh", "identb"], writes=[pTk])
                  for k in range(8):
                      op("act", lambda e, k=k, seg=seg: e.activation(out=hTv[:, k, :], in_=pT[:, k * 128:(k + 1) * 128], func=AF.Identity,
                                                                    scale=ABv[:, seg, 0, k:k + 1], bias=ABv[:, seg, 1, k:k + 1]), reads=[pTk, "AB"], writes=["hT"])
                  REL(pTk)
                  _chk("A%da" % t)
                  pA, pAk = PSF()
                  for g in range(4):
                      for k in range(8):
                          op("pe", lambda e, g=g, k=k: e.matmul(pA[0:64, g * 128:(g + 1) * 128], lhsT=w_in_s[:, k, FM_QK + g * 64:FM_QK + (g + 1) * 64], rhs=hTv[:, k, :], start=(k == 0), stop=(k == 7)),
                             reads=["wbig", "hT"], writes=[pAk])
                  pL, pLk = PSF()
                  for k in range(8):
                      op("pe", lambda e, k=k: e.matmul(pL[0:64, 0:128], lhsT=w_in_s[:, k, FM_LR:FM_LR + 64], rhs=hTv[:, k, :], start=(k == 0), stop=(k == 7)), reads=["wbig", "hT"], writes=[pLk])
                  op("act", lambda e: e.activation(out=lrT[0:16, :], in_=pL[0:16, 0:128], func=AF.Copy), reads=[pLk], writes=["lrT"])
                  op("act", lambda e: e.activation(out=lrT[32:48, :], in_=pL[32:48, 0:128], func=AF.Copy), reads=[pLk], writes=["lrT"])
                  REL(pLk)
                  cur = xrv[t % 3]; curk = "xraw%d" % (t % 3)
                  prv = xrv[(t - 1) % 3]; prvk = "xraw%d" % ((t - 1) % 3)
                  first_in_seg = (t == 0 or t == NCT)
                  last_in_seg = (t == NCT - 1 or t == NT - 1)
                  pXs = []
                  for hx in range(2):
                      pX, pXk = PSF()
                      pXs.append((pX, pXk))
                      for c4 in range(4):
                          ct = hx * 4 + c4
                          for k in range(8):
                              op("pe", lambda e, pX=pX, c4=c4, ct=ct, k=k: e.matmul(pX[:, c4 * 128:(c4 + 1) * 128], lhsT=w_in_s[:, k, FM_X + ct * 128:FM_X + (ct + 1) * 128], rhs=hTv[:, k, :], start=(k == 0), stop=(k == 7)),
                                 reads=["wbig", "hT"], writes=[pXk])
                  for hx in range(2):
                      pX, pXk = pXs[hx]
                      pv = pX[:, :].rearrange("p (a t) -> p a t", a=4)
                      op("act", lambda e, hx=hx, pv=pv: e.activation(out=cur[:, hx * 4:(hx + 1) * 4, 2:130], in_=pv, func=AF.Copy), reads=[pXk], writes=[curk])
                      if first_in_seg:
                          op("pool", lambda e, hx=hx: e.memset(cur[:, hx * 4:(hx + 1) * 4, 0:2], 0.0), writes=[curk])
                      else:
                          op("dve", lambda e, hx=hx, pv=pv: e.tensor_copy(out=prv[:, hx * 4:(hx + 1) * 4, 130:132], in_=pv[:, :, 0:2]), reads=[pXk], writes=[prvk])
                          op("pool", lambda e, hx=hx: e.tensor_copy(out=cur[:, hx * 4:(hx + 1) * 4, 0:2], in_=prv[:, hx * 4:(hx + 1) * 4, 128:130]), reads=[prvk], writes=[curk])
                      if last_in_seg:
                          op("pool", lambda e, hx=hx: e.memset(cur[:, hx * 4:(hx + 1) * 4, 130:132], 0.0), writes=[curk])
                      REL(pXk)
                  banks = {}
                  for nm, c0, w in (("A", TM_A, 512), ("B", TM_B, 512), ("C", TM_C, 512), ("D", TM_D, 512), ("E", TM_E, 256)):
                      pb_, pbk = PSF()
                      banks[nm] = (pb_, pbk)
                      for k in range(8):
                          op("pe", lambda e, pb_=pb_, c0=c0, w=w, k=k: e.matmul(pb_[:, 0:w], lhsT=hTv[:, k, :], rhs=w_in_s[:, k, c0:c0 + w], start=(k == 0), stop=(k == 7)),
                             reads=["wbig", "hT"], writes=[pbk])
                  bA, bAk = banks["A"]; bB, bBk = banks["B"]; bC, bCk = banks["C"]; bD, bDk = banks["D"]; bE, bEk = banks["E"]
                  op("act", lambda e: e.activation(out=vg[:, :], in_=bA[:, 128:384], func=AF.Copy), reads=[bAk], writes=["vg"])
                  op("dve", lambda e, t=t: e.tensor_tensor(out=dtraw[t % 2][:, :], in0=bA[:, 384:400], in1=PL("dtb"), op=ALU.add), reads=[bAk, "pl"], writes=["dtraw%d" % (t % 2)])
                  op("act", lambda e: e.activation(out=stB[:, 0:512], in_=bB[:, :], func=AF.Copy), reads=[bBk], writes=["stB"])
                  op("act", lambda e: e.activation(out=stB[:, 512:1024], in_=bC[:, :], func=AF.Copy), reads=[bCk], writes=["stB"])
                  op("act", lambda e: e.activation(out=vr[:, :], in_=bE[:, 0:256], func=AF.Copy), reads=[bEk], writes=["vr"])
                  REL(bBk, bCk, bEk)
                  if seg == 1:
                      for which, src0, dst, tb0 in ((0, 0, qr, 0), (1, 256, kr, 128)):
                          sv = bD[:, src0:src0 + 256].rearrange("p (h s c) -> p h s c", h=4, s=2)
                          tv = ropetmp[:, 0:256].rearrange("p (h s c) -> p h s c", h=4, s=2)
                          tv2 = ropetmp[:, 256:512].rearrange("p (h s c) -> p h s c", h=4, s=2)
                          cosv = rope_t[:, tb0:tb0 + 64].rearrange("p (s c) -> p s c", s=2)
                          sinv = rope_t[:, tb0 + 64:tb0 + 128].rearrange("p (s c) -> p s c", s=2)
                          op("dve", lambda e, sv=sv, tv=tv, cosv=cosv: e.tensor_tensor(out=tv, in0=sv, in1=cosv.unsqueeze(1).to_broadcast([128, 4, 2, 32]), op=ALU.mult), reads=[bDk, "rope_t"], writes=["ropetmp"])
                          op("dve", lambda e, sv=sv, tv2=tv2, sinv=sinv: e.tensor_tensor(out=tv2[:, :, 0, :], in0=sv[:, :, 1, :], in1=sinv[:, 0, :].unsqueeze(1).to_broadcast([128, 4, 32]), op=ALU.mult), reads=[bDk, "rope_t"], writes=["ropetmp2"])
                          op("dve", lambda e, sv=sv, tv2=tv2, sinv=sinv: e.tensor_tensor(out=tv2[:, :, 1, :], in0=sv[:, :, 0, :], in1=sinv[:, 1, :].unsqueeze(1).to_broadcast([128, 4, 32]), op=ALU.mult), reads=[bDk, "rope_t"], writes=["ropetmp2"])
                          op("dve", lambda e, dst=dst: e.tensor_tensor(out=dst[:, :], in0=ropetmp[:, 0:256], in1=ropetmp[:, 256:512], op=ALU.add), reads=["ropetmp", "ropetmp2"], writes=["qr" if which == 0 else "kr"])
                          _chk("R%dw%d" % (t, which))
                  else:
                      op("act", lambda e: e.activation(out=qr[:, :], in_=bD[:, 0:256], func=AF.Copy, scale=0.125), reads=[bDk], writes=["qr"])
                      op("act", lambda e: e.activation(out=kr[:, :], in_=bD[:, 256:512], func=AF.Copy), reads=[bDk], writes=["kr"])
                  REL(bDk)
                  _chk("A%db" % t)
                  pz, pzk = PSF()
                  op("pe", lambda e: e.matmul(pz[:, 0:256], lhsT=lrT[0:64, :], rhs=GUb[0:64, :], start=True, stop=True, tile_position=(0, 0)), reads=["lrT", "GUb"], writes=[pzk])
                  _chk("A%db0" % t)
                  op("act", lambda e: e.activation(out=ez[:, :], in_=pz[:, 0:256], func=AF.Exp, scale=-1.0), reads=[pzk], writes=["ez"])
                  REL(pzk)
                  _chk("A%db0e" % t)
                  op("act", lambda e: e.activation(out=spt[:, :], in_=ez[:, :], func=AF.Ln, bias=C("one")), reads=["ez", "constf"], writes=["spt"])
                  _chk("A%db1" % t)
                  pGs = []
                  for d in range(2):
                      pG, pGk = PSF()
                      pGs.append((pG, pGk))
                      R = C("Rf") if d == 0 else C("Rb")
                      for p_ in range(2):
                          op("pe", lambda e, pG=pG, d=d, p_=p_, R=R: e.matmul(pG[0:64, p_ * 129:(p_ + 1) * 129], lhsT=spt[:, d * 128 + p_ * 64:d * 128 + (p_ + 1) * 64], rhs=R, start=True, stop=True),
                             reads=["spt", "constf"], writes=[pGk])
                  pD, pDk = PSF()
                  for d in range(2):
                      Lc = C("Lf") if d == 0 else C("Lb")
                      op("pe", lambda e, d=d, Lc=Lc: e.matmul(pD[:, d * 128:(d + 1) * 128], lhsT=Lc, rhs=spt[:, d * 128:(d + 1) * 128], start=True, stop=True), reads=["spt", "constf"], writes=[pDk])
                  for d in range(2):
                      pG, pGk = pGs[d]
                      gv = pG[0:64, 0:258].rearrange("q (p i) -> q p i", p=2)
                      op("act", lambda e, d=d, gv=gv: e.activation(out=EGv[:, d, :, :], in_=gv[:, :, 0:128], func=AF.Exp, bias=C("lnqs", slice(0, 64))), reads=[pGk, "constf"], writes=["EG"])
                      op("act", lambda e, d=d, gv=gv: e.activation(out=EGnv[:, d, :, :], in_=gv[:, :, 0:128], func=AF.Exp, scale=-1.0), reads=[pGk], writes=["EGn"])
                      op("act", lambda e, d=d, gv=gv: e.activation(out=gdec[:, d * 2:(d + 1) * 2], in_=gv[:, :, 128], func=AF.Exp), reads=[pGk], writes=["gdec"])
                  op("act", lambda e: e.activation(out=ED[:, :], in_=pD[:, 0:256], func=AF.Exp), reads=[pDk], writes=["ED"])
                  REL(pGs[0][1], pGs[1][1], pDk)
                  _chk("A%db2" % t)
                  qv_ = pA[0:64, 0:256].rearrange("q (p i) -> q p i", p=2)
                  kv_ = pA[0:64, 256:512].rearrange("q (p i) -> q p i", p=2)
                  for d in range(2):
                      op("dve", lambda e, d=d: e.tensor_tensor(out=qtgv[:, d, :, :], in0=qv_, in1=EGv[:, d, :, :], op=ALU.mult), reads=[pAk, "EG"], writes=["qtg"])
                      for hh in range(2):
                          op("dve", lambda e, d=d, hh=hh: e.tensor_tensor(out=ktgv[32 * hh:32 * hh + 32, d, hh, :, :], in0=kv_[32 * hh:32 * hh + 32, :, :], in1=EGnv[32 * hh:32 * hh + 32, d, :, :], op=ALU.mult), reads=[pAk, "EGn"], writes=["ktg"])
                      op("dve", lambda e, d=d: e.tensor_tensor(out=ksg[:, d * 128:(d + 1) * 128], in0=bA[:, 0:128], in1=ED[:, d * 128:(d + 1) * 128], op=ALU.mult), reads=[bAk, "ED"], writes=["ksg"])
                  REL(pAk, bAk)
                  _chk("A%db2d" % t)
                  for d in range(2):
                      mk = C("maskf") if d == 0 else C("maskb")
                      for hh in range(2):
                          pS, pSk = PSF()
                          for p_ in range(2):
                              op("pe", lambda e, pS=pS, d=d, p_=p_, hh=hh: e.matmul(pS[:, p_ * 128:(p_ + 1) * 128], lhsT=ktgv[:, d, hh, p_, :], rhs=qtgv[:, d, p_, :], start=True, stop=True, tile_position=(0, 0)),
                                 reads=["ktg", "qtg"], writes=[pSk])
                          op("dve", lambda e, pS=pS, d=d, hh=hh, mk=mk: e.tensor_tensor(out=Pgv[:, d, hh, :, :], in0=pS[:, 0:256].rearrange("p (b i) -> p b i", b=2),
                                                                                in1=mk.unsqueeze(1).to_broadcast([128, 2, 128]), op=ALU.mult), reads=[pSk, "constf"], writes=["Pg"])
                          REL(pSk)
                  _chk("A%db3" % t)
                  pO, pOk = PSF()
                  for p_ in range(2):
                      op("pe", lambda e, p_=p_: e.matmul(pO[:, p_ * 128:(p_ + 1) * 128], lhsT=qtgv[:, 0, p_, :], rhs=Sb_g[:, p_ * 128:(p_ + 1) * 128], start=True, stop=False, skip_group_check=True, tile_position=(0, 0)),
                         reads=["qtg", "Sb_g"], writes=[pOk])
                      for h in (2 * p_, 2 * p_ + 1):
                          for d in range(2):
                              op("pe", lambda e, h=h, d=d: e.matmul(pO[:, h * 64:(h + 1) * 64], lhsT=Pgv[:, d, h % 2, h // 2, :], rhs=vg[:, h * 64:(h + 1) * 64], start=False, stop=(d == 1 and h == 2 * p_ + 1), skip_group_check=True),
                                 reads=["Pg", "vg"], writes=[pOk])
                  op("act", lambda e: e.activation(out=stA[:, 0:256], in_=pO[:, 0:256], func=AF.Copy), reads=[pOk], writes=["stA"])
                  REL(pOk)
                  _chk("A%db4" % t)
                  pDS, pDSk = PSF()
                  for d in range(2):
                      for p_ in range(2):
                          op("pe", lambda e, d=d, p_=p_: e.matmul(pDS[0:64, d * 256 + p_ * 128:d * 256 + (p_ + 1) * 128], lhsT=ksg[:, d * 128 + p_ * 64:d * 128 + (p_ + 1) * 64], rhs=vg[:, p_ * 128:(p_ + 1) * 128], start=True, stop=True),
                             reads=["ksg", "vg"], writes=[pDSk])
                  for p_ in range(2):
                      op("dve", lambda e, p_=p_: e.scalar_tensor_tensor(out=S_g[:, p_ * 128:(p_ + 1) * 128], in0=S_g[:, p_ * 128:(p_ + 1) * 128], scalar=gdec[:, p_:p_ + 1],
                                                                        in1=pDS[0:64, p_ * 128:(p_ + 1) * 128], op0=ALU.mult, op1=ALU.add), reads=["S_g", "gdec", pDSk], writes=["S_g"])
                  op("dve", lambda e: e.tensor_tensor(out=Sb_g[:, :].rearrange("q (p c) -> q p c", p=2), in0=S_g[:, :].rearrange("q (p c) -> q p c", p=2),
                                                      in1=C("bmg", slice(0, 64)).unsqueeze(1).to_broadcast([64, 2, 128]), op=ALU.mult), reads=["S_g", "constf"], writes=["Sb_g"])
                  op("act", lambda e: e.activation(out=stA[0:64, 768:1024], in_=pDS[0:64, 256:512], func=AF.Copy), reads=[pDSk], writes=["stA"])
                  REL(pDSk)
                  op("pool", lambda e: e.tensor_copy(out=stA[0:64, 1024:1026], in_=gdec[:, 2:4]), reads=["gdec"], writes=["stA"])
                  op("pool", lambda e: e.tensor_copy(out=stB[0:64, 1280:1536], in_=qtg[:, 256:512]), reads=["qtg"], writes=["stB"])
                  _chk("A%dc" % t)
                  pT2, pT2k = PSB()
                  for a_ in range(4):
                      srct = qr if a_ < 2 else kr
                      op("pe", lambda e, a_=a_, srct=srct: e.transpose(pT2[:, a_ * 128:(a_ + 1) * 128], srct[:, (a_ % 2) * 128:(a_ % 2 + 1) * 128], identb[:, :]), reads=["qr", "kr", "identb"], writes=[pT2k])
                  op("act", lambda e: e.activation(out=qkT[:, 0:256], in_=pT2[:, 0:256], func=AF.Copy), reads=[pT2k], writes=["qkT"])
                  for hh in range(2):
                      op("act", lambda e, hh=hh: e.activation(out=kTzv[64 * hh:64 * hh + 64, hh, :, :], in_=pT2[64 * hh:64 * hh + 64, 256:512].rearrange("q (p i) -> q p i", p=2), func=AF.Copy), reads=[pT2k], writes=["kTz"])
                  REL(pT2k)
                  op("pool", lambda e: e.tensor_copy(out=stB[:, 1024:1280], in_=qkT[:, 0:256]), reads=["qkT"], writes=["stB"])
                  for hh in range(2):
                      pSr, pSrk = PSF()
                      for p_ in range(2):
                          op("pe", lambda e, pSr=pSr, p_=p_, hh=hh: e.matmul(pSr[:, p_ * 128:(p_ + 1) * 128], lhsT=kTzv[:, hh, p_, :], rhs=qkTv[:, p_, :], start=True, stop=True), reads=["qkT", "kTz"], writes=[pSrk])
                      a0 = CF["retM"][0]
                      op("dve", lambda e, pSr=pSr, hh=hh, a0=a0: e.tensor_tensor(out=Pr[:, hh * 256:(hh + 1) * 256], in0=pSr[:, 0:256], in1=constf[:, a0 + hh * 256:a0 + (hh + 1) * 256], op=ALU.mult), reads=[pSrk, "constf"], writes=["Pr"])
                      REL(pSrk)
                  pOr, pOrk = PSF()
                  for h in range(4):
                      op("pe", lambda e, h=h: e.matmul(pOr[:, h * 64:(h + 1) * 64], lhsT=Prv[:, h % 2, h // 2, :], rhs=vr[:, h * 64:(h + 1) * 64], start=True, stop=True), reads=["Pr", "vr"], writes=[pOrk])
                  for p_ in range(2):
                      op("pe", lambda e, p_=p_: e.matmul(pOr[:, 256 + p_ * 128:256 + (p_ + 1) * 128], lhsT=qkTv[:, p_, :], rhs=Sb_r[:, p_ * 128:(p_ + 1) * 128], start=True, stop=True), reads=["qkT", "Sb_r"], writes=[pOrk])
                  op("dve", lambda e: e.tensor_tensor(out=tmpS[:, 0:256].rearrange("p (h c) -> p h c", h=4), in0=pOr[:, 256:512].rearrange("p (h c) -> p h c", h=4),
                                                      in1=C("retEQf").unsqueeze(2).to_broadcast([128, 4, 64]), op=ALU.mult), reads=[pOrk, "constf"], writes=["tmpS"])
                  op("dve", lambda e: e.tensor_tensor(out=stA[:, 256:512], in0=tmpS[:, 0:256], in1=pOr[:, 0:256], op=ALU.add), reads=["tmpS", pOrk], writes=["stA"])
                  REL(pOrk)
                  for d in range(2):
                      Wc = C("retWf") if d == 0 else C("retWb")
                      op("dve", lambda e, d=d, Wc=Wc: e.tensor_tensor(out=vtl[:, d * 256:(d + 1) * 256].rearrange("p (h c) -> p h c", h=4), in0=vr[:, :].rearrange("p (h c) -> p h c", h=4),
                                                                      in1=Wc.unsqueeze(2).to_broadcast([128, 4, 64]), op=ALU.mult), reads=["vr", "constf"], writes=["vtl"])
                  pDr, pDrk = PSF()
                  for d in range(2):
                      for p_ in range(2):
                          op("pe", lambda e, d=d, p_=p_: e.matmul(pDr[:, d * 256 + p_ * 128:d * 256 + (p_ + 1) * 128], lhsT=kr[:, p_ * 128:(p_ + 1) * 128], rhs=vtl[:, d * 256 + p_ * 128:d * 256 + (p_ + 1) * 128], start=True, stop=True),
                             reads=["kr", "vtl"], writes=[pDrk])
                  op("dve", lambda e: e.tensor_tensor(out=tmpS[:, 256:512].rearrange("p (h c) -> p h c", h=4), in0=S_r[:, :].rearrange("p (h c) -> p h c", h=4),
                                                      in1=C("retdec").unsqueeze(2).to_broadcast([128, 4, 64]), op=ALU.mult), reads=["S_r", "constf"], writes=["tmpS"])
                  op("dve", lambda e: e.tensor_tensor(out=S_r[:, :], in0=tmpS[:, 256:512], in1=pDr[:, 0:256], op=ALU.add), reads=["tmpS", pDrk], writes=["S_r"])
                  op("dve", lambda e: e.tensor_tensor(out=Sb_r[:, :].rearrange("p (a c) -> p a c", a=2), in0=S_r[:, :].rearrange("p (a c) -> p a c", a=2),
                                                      in1=C("bmr").unsqueeze(1).to_broadcast([128, 2, 128]), op=ALU.mult), reads=["S_r", "constf"], writes=["Sb_r"])
                  op("act", lambda e: e.activation(out=stA[:, 512:768], in_=pDr[:, 256:512], func=AF.Copy), reads=[pDrk], writes=["stA"])
                  REL(pDrk)
                  op("sp", lambda e, t=t: e.dma_start(out=sA_d[t, :, :], in_=stA[:, :]), reads=["stA"], writes=[("sA", t)], dma=True)
                  op("sp", lambda e, t=t: e.dma_start(out=sB_d[t, :, :], in_=stB[:, :]), reads=["stB"], writes=[("sB", t)], dma=True)
                  _chk("A%dd" % t)
                  if not first_in_seg:
                      ssd_tile(t - 1)
                  if last_in_seg:
                      ssd_tile(t)
                  _chk("A%d" % t)

              _chk("A")
              barrier()
              op("sp", lambda e, l=l: e.dma_start(out=w_out_s, in_=wb_out[l, :, :].rearrange("(k p) c -> p k c", p=128)), reads=W8("wout", l), writes=["wbig"], dma=True)
              op("sp", lambda e, l=l: e.dma_start(out=w2_s, in_=wb_2[l, :, :].rearrange("(j p) c -> p j c", p=128)), reads=W22("w2", l), writes=["wbig"], dma=True)
              for nm, tns in (("S_g", S_g), ("S_s", S_s), ("S_r", S_r), ("Sb_g", Sb_g), ("Sb_s", Sb_s), ("Sb_r", Sb_r)):
                  op("pool", lambda e, tns=tns: e.memset(tns[:, :], 0.0), writes=[nm])
              order = list(range(NCT - 1, -1, -1)) + list(range(NT - 1, NCT - 1, -1))
              for t in order:
                  seg = 0 if t < NCT else 1
                  op("sp", lambda e, t=t: e.dma_start(out=stA[:, :], in_=sA_d[t, :, :]), reads=[("sA", t)], writes=["stA"], dma=True)
                  op("sp", lambda e, t=t: e.dma_start(out=stB[:, :], in_=sB_d[t, :, :]), reads=[("sB", t)], writes=["stB"], dma=True)
                  op("sp", lambda e, t=t: e.dma_start(out=stS[:, :], in_=sS_d[t, :, :]), reads=[("sS", t)], writes=["stS"], dma=True)
                  op("sp", lambda e, t=t: e.dma_start(out=stT[:, :], in_=sT_d[t, :, :]), reads=[("sT", t)], writes=["stT"], dma=True)
                  pI, pIk = PSF(); pIs, pIsk = PSF()
                  for p_ in range(2):
                      op("pe", lambda e, p_=p_: e.matmul(pI[:, p_ * 128:(p_ + 1) * 128], lhsT=stB[0:64, 1280 + p_ * 128:1280 + (p_ + 1) * 128], rhs=Sb_g[:, p_ * 128:(p_ + 1) * 128], start=True, stop=True, tile_position=(0, 0)), reads=["stB", "Sb_g"], writes=[pIk])
                      op("pe", lambda e, p_=p_: e.matmul(pI[:, 256 + p_ * 128:256 + (p_ + 1) * 128], lhsT=stB[:, 1024 + p_ * 128:1024 + (p_ + 1) * 128], rhs=Sb_r[:, p_ * 128:(p_ + 1) * 128], start=True, stop=True), reads=["stB", "Sb_r"], writes=[pIk])
                  for g in range(2):
                      op("pe", lambda e, g=g: e.matmul(pIs[:, g * 256:(g + 1) * 256], lhsT=stT[:, 512 + g * 128:512 + (g + 1) * 128], rhs=Sb_s[:, g * 256:(g + 1) * 256], start=True, stop=True), reads=["stT", "Sb_s"], writes=[pIsk])
                  op("dve", lambda e: e.tensor_tensor(out=Oall[:, 0:256], in0=stA[:, 0:256], in1=pI[:, 0:256], op=ALU.add), reads=["stA", pIk], writes=["Og"])
                  op("dve", lambda e: e.tensor_tensor(out=tmpS[:, :].rearrange("p (h c) -> p h c", h=8), in0=pIs[:, :].rearrange("p (h c) -> p h c", h=8),
                                                      in1=stS[:, 512:520].unsqueeze(2).to_broadcast([128, 8, 64]), op=ALU.mult), reads=[pIsk, "stS"], writes=["tmpS"])
                  op("dve", lambda e: e.tensor_tensor(out=Oall[:, 256:768], in0=tmpS[:, :], in1=stS[:, 0:512], op=ALU.add), reads=["tmpS", "stS"], writes=["Os"])
                  op("dve", lambda e: e.tensor_tensor(out=fsq[:, 0:256].rearrange("p (h c) -> p h c", h=4), in0=pI[:, 256:512].rearrange("p (h c) -> p h c", h=4),
                                                      in1=C("retEQb").unsqueeze(2).to_broadcast([128, 4, 64]), op=ALU.mult), reads=[pIk, "constf"], writes=["fsq"])
                  op("dve", lambda e: e.tensor_tensor(out=Oall[:, 768:1024], in0=fsq[:, 0:256], in1=stA[:, 256:512], op=ALU.add), reads=["fsq", "stA"], writes=["Or"])
                  REL(pIk, pIsk)
                  for p_ in range(2):
                      op("dve", lambda e, p_=p_: e.scalar_tensor_tensor(out=S_g[:, p_ * 128:(p_ + 1) * 128], in0=S_g[:, p_ * 128:(p_ + 1) * 128], scalar=stA[0:64, 1024 + p_:1025 + p_],
                                                                        in1=stA[0:64, 768 + p_ * 128:768 + (p_ + 1) * 128], op0=ALU.mult, op1=ALU.add), reads=["S_g", "stA", pIk], writes=["S_g"])
                  op("dve", lambda e: e.tensor_tensor(out=Sb_g[:, :].rearrange("q (p c) -> q p c", p=2), in0=S_g[:, :].rearrange("q (p c) -> q p c", p=2),
                                                      in1=C("bmg", slice(0, 64)).unsqueeze(1).to_broadcast([64, 2, 128]), op=ALU.mult), reads=["S_g", "constf"], writes=["Sb_g"])
                  op("dve", lambda e: e.tensor_tensor(out=tmpS[:, :].rearrange("p (h c) -> p h c", h=8), in0=S_s[:, :].rearrange("p (h c) -> p h c", h=8),
                                                      in1=stS[:, 520:528].unsqueeze(2).to_broadcast([128, 8, 64]), op=ALU.mult), reads=["S_s", "stS", "Os", pIsk], writes=["tmpS"])
                  op("dve", lambda e: e.tensor_tensor(out=S_s[:, :], in0=tmpS[:, :], in1=stS[:, 528:1040], op=ALU.add), reads=["tmpS", "stS"], writes=["S_s"])
                  op("act", lambda e: e.activation(out=Sb_s[:, :], in_=S_s[:, :], func=AF.Copy), reads=["S_s"], writes=["Sb_s"])
                  op("dve", lambda e: e.tensor_tensor(out=fsq[:, 256:512].rearrange("p (h c) -> p h c", h=4), in0=S_r[:, :].rearrange("p (h c) -> p h c", h=4),
                                                      in1=C("retdec").unsqueeze(2).to_broadcast([128, 4, 64]), op=ALU.mult), reads=["S_r", "constf", pIk], writes=["fsq2"])
                  op("dve", lambda e: e.tensor_tensor(out=S_r[:, :], in0=fsq[:, 256:512], in1=stA[:, 512:768], op=ALU.add), reads=["fsq2", "stA"], writes=["S_r"])
                  op("dve", lambda e: e.tensor_tensor(out=Sb_r[:, :].rearrange("p (a c) -> p a c", a=2), in0=S_r[:, :].rearrange("p (a c) -> p a c", a=2),
                                                      in1=C("bmr").unsqueeze(1).to_broadcast([128, 2, 128]), op=ALU.mult), reads=["S_r", "constf"], writes=["Sb_r"])
                  _chk("Bs%d" % t)
                  if last and seg == 0:
                      continue
                  op("dve", lambda e: e.tensor_tensor(out=fsq[:, 0:256], in0=Oall[:, 0:256], in1=Oall[:, 0:256], op=ALU.mult), reads=["Og", "Or"], writes=["fsq"])
                  op("dve", lambda e: e.tensor_reduce(out=st[:, 4:8], in_=fsq[:, 0:256].rearrange("p (h c) -> p h c", h=4), axis=AX.X, op=ALU.add), reads=["fsq"], writes=["st4"])
                  rstd_from(st[:, 4:8], st[:, 8:12], 64, ["st4"], ["st8"])
                  op("dve", lambda e: e.tensor_tensor(out=fsq[:, 0:256].rearrange("p (h c) -> p h c", h=4), in0=Oall[:, 0:256].rearrange("p (h c) -> p h c", h=4),
                                                      in1=st[:, 8:12].unsqueeze(2).to_broadcast([128, 4, 64]), op=ALU.mult), reads=["Og", "st8"], writes=["fsq"])
                  op("dve", lambda e: e.tensor_tensor(out=fsq[:, 0:256], in0=fsq[:, 0:256], in1=PL("glan"), op=ALU.mult), reads=["fsq", "pl"], writes=["fsq"])
                  op("act", lambda e: e.activation(out=sil[:, 0:256], in_=stB[:, 0:256], func=AF.Silu), reads=["stB"], writes=["sil"])
                  op("dve", lambda e: e.tensor_tensor(out=mixed[:, 0:256], in0=fsq[:, 0:256], in1=sil[:, 0:256], op=ALU.mult), reads=["fsq", "sil"], writes=["mixed"])
                  op("dve", lambda e: e.tensor_tensor(out=fsq[:, :], in0=stT[:, 0:512], in1=PL("ssdd"), op=ALU.mult), reads=["stT", "pl", "mixed"], writes=["fsq"])
                  op("dve", lambda e: e.tensor_tensor(out=fsq[:, :], in0=fsq[:, :], in1=Oall[:, 256:768], op=ALU.add), reads=["fsq", "Os"], writes=["fsq"])
                  op("act", lambda e: e.activation(out=sil[:, :], in_=stB[:, 512:1024], func=AF.Silu), reads=["stB", "mixed"], writes=["sil"])
                  op("dve", lambda e: e.tensor_tensor(out=fsq[:, :], in0=fsq[:, :], in1=sil[:, :], op=ALU.mult), reads=["fsq", "sil"], writes=["fsq"])
                  op("pool", lambda e: e.memset(st[:, 12:13], 0.0), writes=["st12"])
                  op("act", lambda e: e.activation(out=sil[:, :], in_=fsq[:, :], func=AF.Square, accum_out=st[:, 12:13]), reads=["fsq", "st12"], writes=["sil", "st12"])
                  rstd_from(st[:, 12:13], st[:, 13:14], 512, ["st12"], ["st13"])
                  op("dve", lambda e: e.scalar_tensor_tensor(out=mixed[:, 256:768], in0=fsq[:, :], scalar=st[:, 13:14], in1=PL("ssdn"), op0=ALU.mult, op1=ALU.mult), reads=["fsq", "st13", "pl"], writes=["mixed"])
                  op("dve", lambda e: e.tensor_reduce(out=st[:, 16:20], in_=Oall[:, 768:1024].rearrange("p (h c) -> p h c", h=4), axis=AX.X, op=ALU.add), reads=["Or"], writes=["st16"])
                  op("dve", lambda e: e.tensor_scalar(out=st[:, 16:20], in0=st[:, 16:20], scalar1=1.0 / 64, scalar2=None, op0=ALU.mult), reads=["st16"], writes=["st16"])
                  op("dve", lambda e: e.tensor_tensor(out=fsq[:, 0:256].rearrange("p (h c) -> p h c", h=4), in0=Oall[:, 768:1024].rearrange("p (h c) -> p h c", h=4),
                                                      in1=st[:, 16:20].unsqueeze(2).to_broadcast([128, 4, 64]), op=ALU.subtract), reads=["Or", "st16", "mixed"], writes=["fsq"])
                  op("dve", lambda e: e.tensor_tensor(out=fsq[:, 256:512], in0=fsq[:, 0:256], in1=fsq[:, 0:256], op=ALU.mult), reads=["fsq"], writes=["fsqb"])
                  op("dve", lambda e: e.tensor_reduce(out=st[:, 20:24], in_=fsq[:, 256:512].rearrange("p (h c) -> p h c", h=4), axis=AX.X, op=ALU.add), reads=["fsqb"], writes=["st20"])
                  rstd_from(st[:, 20:24], st[:, 24:28], 64, ["st20"], ["st24"])
                  op("dve", lambda e: e.tensor_tensor(out=fsq[:, 0:256].rearrange("p (h c) -> p h c", h=4), in0=fsq[:, 0:256].rearrange("p (h c) -> p h c", h=4),
                                                      in1=st[:, 24:28].unsqueeze(2).to_broadcast([128, 4, 64]), op=ALU.mult), reads=["fsq", "st24", "fsqb"], writes=["fsq"])
                  op("dve", lambda e: e.tensor_tensor(out=fsq[:, 0:256], in0=fsq[:, 0:256], in1=PL("retn"), op=ALU.mult), reads=["fsq", "pl"], writes=["fsq"])
                  op("act", lambda e: e.activation(out=sil[:, 0:256], in_=stB[:, 256:512], func=AF.Silu), reads=["stB", "mixed"], writes=["sil"])
                  op("dve", lambda e: e.tensor_tensor(out=mixed[:, 768:1024], in0=fsq[:, 0:256], in1=sil[:, 0:256], op=ALU.mult), reads=["fsq", "sil"], writes=["mixed"])
                  pT, pTk = PSB()
                  for k in range(8):
                      op("pe", lambda e, k=k: e.transpose(pT[:, k * 128:(k + 1) * 128], mixed[:, k * 128:(k + 1) * 128], identb[:, :]), reads=["mixed", "identb"], writes=[pTk])
                  op("act", lambda e: e.activation(out=mixT[:, :], in_=pT[:, :], func=AF.Copy), reads=[pTk], writes=["mixT"])
                  REL(pTk)
                  op("sp", lambda e, t=t: e.dma_start(out=xt[:, :], in_=res_d[t * 128:(t + 1) * 128, :]), reads=[("res", t)], writes=["xt"], dma=True)

                  def resid_update(wsel, nK, lhs_of, vi, tag):
                      pys = []
                      op("pool", lambda e: e.memset(st[:, 28:30], 0.0), writes=["st28"])
                      for half in range(2):
                          py, pyk = PSF()
                          pys.append((py, pyk))
                          for k in range(nK):
                              op("pe", lambda e, py=py, k=k, half=half: e.matmul(py[:, :], lhsT=lhs_of(k), rhs=wsel[:, k, half * 512:(half + 1) * 512], start=(k == 0), stop=(k == nK - 1)),
                                 reads=["wbig", tag], writes=[pyk])
                          op("act", lambda e, py=py, half=half: e.activation(out=junk[:, half * 512:(half + 1) * 512], in_=py[:, :], func=AF.Square, accum_out=st[:, 28 + half:29 + half]),
                             reads=[pyk, "st28"], writes=["junk", "st28"])
                      op("dve", lambda e: e.tensor_tensor(out=st[:, 30:31], in0=st[:, 28:29], in1=st[:, 29:30], op=ALU.add), reads=["st28"], writes=["st30"])
                      rstd_from(st[:, 30:31], st[:, 31:32], D, ["st30"], ["st31"])
                      for half in range(2):
                          py, pyk = pys[half]
                          op("dve", lambda e, py=py, half=half: e.scalar_tensor_tensor(out=junk[:, half * 512:(half + 1) * 512], in0=py[:, :], scalar=st[:, 31:32],
                                                                                       in1=Gv[:, seg, vi, half * 512:(half + 1) * 512], op0=ALU.mult, op1=ALU.mult), reads=[pyk, "st31", "Grep", "junk"], writes=["junk"])
                          REL(pyk)
                      op("dve", lambda e: e.tensor_tensor(out=xt[:, :], in0=xt[:, :], in1=junk[:, :], op=ALU.add), reads=["xt", "junk"], writes=["xt"])

                  resid_update(w_out_s, 8, lambda k: mixTv[:, k, :], 0, "mixT")
                  pos = order.index(t); idx = pos % 2
                  op("pool", lambda e: e.memset(st[:, 0:1], 0.0), writes=["st0"])
                  op("act", lambda e: e.activation(out=junk[:, :], in_=xt[:, :], func=AF.Square, accum_out=st[:, 0:1]), reads=["xt", "st0"], writes=["junk", "st0"])
                  rstd_from(st[:, 0:1], st[:, 1:2], D, ["st0"], ["st1"])
                  op("dve", lambda e: e.tensor_scalar(out=xh[:, :], in0=xt[:, :], scalar1=st[:, 1:2], scalar2=None, op0=ALU.mult), reads=["xt", "st1"], writes=["xh"])
                  pT, pTk = PSB()
                  for k in range(8):
                      op("pe", lambda e, k=k: e.transpose(pT[:, k * 128:(k + 1) * 128], xh[:, k * 128:(k + 1) * 128], identb[:, :]), reads=["xh", "identb"], writes=[pTk])
                  for k in range(8):
                      op("act", lambda e, k=k, seg=seg, idx=idx: e.activation(out=hT2v[:, k, idx * 128:(idx + 1) * 128], in_=pT[:, k * 128:(k + 1) * 128], func=AF.Identity,
                                                                    scale=ABv[:, seg, 3, k:k + 1], bias=ABv[:, seg, 4, k:k + 1]), reads=[pTk, "AB"], writes=["hT"])
                  REL(pTk)
                  if idx == 0:
                      op("sp", lambda e, t=t: e.dma_start(out=res_d[t * 128:(t + 1) * 128, :], in_=xt[:, :]), reads=["xt"], writes=[("res", t)], dma=True)
                      _chk("B%d" % t)
                      continue
                  t_prev = order[pos - 1]
                  for j in range(NJ):
                      wg_ = wgu[j % 4]; wgk = "wgu%d" % (j % 4)
                      wgv = wg_[:, :].rearrange("p (k c) -> p k c", k=8)
                      op("sp", lambda e, j=j, wg_=wg_, l=l: e.dma_start(out=wg_[:, :], in_=wb_13[l, j, :, :]), reads=W8("w13", l), writes=[wgk], dma=True)
                      pgu, pguk = PSF()
                      for k in range(8):
                          op("pe", lambda e, pgu=pgu, wgv=wgv, k=k: e.matmul(pgu[:, 0:256], lhsT=wgv[:, k, 0:128], rhs=hT2v[:, k, :], start=(k == 0), stop=(k == 7)), reads=[wgk, "hT"], writes=[pguk])
                      for k in range(8):
                          op("pe", lambda e, pgu=pgu, wgv=wgv, k=k: e.matmul(pgu[:, 256:512], lhsT=wgv[:, k, 128:256], rhs=hT2v[:, k, :], start=(k == 0), stop=(k == 7)), reads=[wgk, "hT"], writes=[pguk])
                      op("act", lambda e, pgu=pgu: e.activation(out=sgt[:, :], in_=pgu[:, 0:256], func=AF.Silu), reads=[pguk], writes=["sgt"])
                      op("dve", lambda e, pgu=pgu, j=j: e.tensor_tensor(out=actTv[:, j, :], in0=sgt[:, 0:128], in1=pgu[:, 256:384], op=ALU.mult), reads=["sgt", pguk], writes=["actT"])
                      op("dve", lambda e, pgu=pgu, j=j: e.tensor_tensor(out=actTbv[:, j, :], in0=sgt[:, 128:256], in1=pgu[:, 384:512], op=ALU.mult), reads=["sgt", pguk], writes=["actTb"])
                      REL(pguk)
                  for which_, tt in ((1, t), (0, t_prev)):
                      if which_ == 0:
                          op("sp", lambda e, tt=tt: e.dma_start(out=xt[:, :], in_=res_d[tt * 128:(tt + 1) * 128, :]), reads=[("res", tt)], writes=["xt"], dma=True)
                          resid_update(w2_s, NJ, lambda k: actTv[:, k, :], 1, "actT")
                      else:
                          resid_update(w2_s, NJ, lambda k: actTbv[:, k, :], 1, "actTb")
                      op("sp", lambda e, tt=tt: e.dma_start(out=res_d[tt * 128:(tt + 1) * 128, :], in_=xt[:, :]), reads=["xt"], writes=[("res", tt)], dma=True)
                      if last and seg == 1:
                          fo = op("sp", lambda e, tt=tt: e.dma_start(out=out_d[(tt - NCT) * 128:(tt - NCT + 1) * 128, :], in_=xt[:, :]), reads=["xt"], writes=[("out", tt)], dma=True)
                          finals.append(fo)
                  _chk("B%d" % t)
        except _Stop:
            fo = op("sp", lambda e: e.dma_start(out=out_d[0:128, :], in_=xt[:, :]), reads=["xt"], writes=[("out", -1)], dma=True)
            finals.append(fo)
        lastop = {}
        for o_ in P.ops:
            lastop[(o_["eng"], o_["dma"])] = o_["idx"]
        for v_ in lastop.values():
            if v_ not in finals:
                finals.append(v_)
        P.emit(final_wait_ops=finals)
        global LASTP
        LASTP = P
    return nc


finals = []


def kernel(**inp):
    global finals
    finals = []
    inp = {k: np.asarray(v) for k, v in inp.items()}
    depth = DEPTH
    cm = _colmap()
    w_in = inp["w_in"]
    w_in_r = np.where(cm[None, None, :] >= 0, w_in[:, :, np.maximum(cm, 0)], np.float32(0)).astype(np.float32)
    constf = _host_consts()
    rope = _host_rope()
    pl = np.stack([_host_pl(inp, l) for l in range(depth)], 0)
    nc = build_nc(depth)
    in_maps = []
    for b in range(4):
        xin = np.concatenate([inp["ctx"][b], inp["x"][b]], 0).astype(np.float32)
        cv = np.zeros((128, 16), np.float32)
        cv[:, 0::2] = _fm(inp["c_ctx"], 8)
        cv[:, 1::2] = _fm(inp["c"][b], 8)
        in_maps.append({"xin": xin, "cvec": cv, "constf": constf, "rope": rope, "pl": pl, "cbrow": np.ascontiguousarray(inp["ssd_conv_b"][:depth, None, 0:768]).astype(np.float32), "w_in": w_in_r[:depth],
                        "w_out": inp["w_out"][:depth], "w13": inp["ffn_w13"][:depth], "w2": inp["ffn_w2"][:depth], "ada_w": inp["ada_w"][:depth]})
    res = run_bass_kernel_spmd(nc, in_maps, core_ids=[0, 1, 2, 3])
    return np.stack([np.asarray(r["out"], np.float32) for r in res.results], 0)
```

```python
import concourse.bass as bass
import concourse.mybir as mybir

F32 = mybir.dt.float32
BF16 = mybir.dt.bfloat16
AF = mybir.ActivationFunctionType
ALU = mybir.AluOpType
AX = mybir.AxisListType

ENGINES = ("sp", "act", "pool", "dve", "pe")
NDMASEM = 12


import types


def _freeze(fn):
    if fn.__closure__ is None:
        return fn
    cells = []
    for c in fn.__closure__:
        try:
            cells.append(types.CellType(c.cell_contents))
        except ValueError:
            cells.append(c)
    return types.FunctionType(fn.__code__, fn.__globals__, fn.__name__, fn.__defaults__, tuple(cells))


class Prog:
    def __init__(self, nc):
        self.nc = nc
        self.ops = []
        self.last_w = {}
        self.readers = {}

    def op(self, eng, fn, reads=(), writes=(), dma=False):
        i = len(self.ops)
        deps = set()
        raw = set()
        for r in reads:
            if r in self.last_w:
                deps.add(self.last_w[r])
                raw.add(self.last_w[r])
            if isinstance(r, str) and r.startswith("ps"):
                for q in self.readers.get(r, ()):
                    if self.ops[q]["eng"] != eng:
                        deps.add(q)
        for w in writes:
            if w in self.last_w:
                deps.add(self.last_w[w])
            for q in self.readers.get(w, ()):
                deps.add(q)
        for r in reads:
            self.readers.setdefault(r, []).append(i)
        for w in writes:
            self.last_w[w] = i
            self.readers[w] = []
        deps.discard(i)
        self.ops.append(dict(eng=eng, fn=_freeze(fn), deps=deps, dma=dma, idx=i))
        return i

    def emit(self, final_wait_ops=()):
        nc = self.nc
        ops = self.ops
        needed = set()
        for o in ops:
            for d in o["deps"]:
                do = ops[d]
                if do["eng"] == "pe" and o["eng"] == "pe" and not do["dma"]:
                    continue
                needed.add(d)
        for d in final_wait_ops:
            needed.add(d)
        cnt = {e: 0 for e in ENGINES}
        dcnt = {e: 0 for e in ENGINES}
        for o in ops:
            e = o["eng"]
            if o["dma"]:
                n = dcnt[e]
                dcnt[e] += 1
                o["dma_n"] = n
            elif o["idx"] in needed:
                cnt[e] += 1
                o["ticket"] = cnt[e]
        self.cnt = cnt
        import contextlib
        with contextlib.ExitStack() as es:
            sems = {e: es.enter_context(nc.semaphore("s_" + e)) for e in ENGINES}
            dsems = {e: [es.enter_context(nc.semaphore("d_%s_%d" % (e, k))) for k in range(NDMASEM)]
                     for e in ENGINES if dcnt[e] > 0}
            block = es.enter_context(nc.Block())
            per_eng = {e: [o for o in ops if o["eng"] == e] for e in ENGINES}

            def run(e, engobj, extra_final=False):
                waited = {}
                for o in per_eng[e]:
                    waits = []
                    for d in sorted(o["deps"]):
                        do = ops[d]
                        if do["dma"]:
                            n = do["dma_n"]
                            waits.append((dsems[do["eng"]][n % NDMASEM], 16 * (n // NDMASEM + 1), ("d", do["eng"], n % NDMASEM)))
                        else:
                            if do["eng"] == "pe" and e == "pe":
                                continue
                            waits.append((sems[do["eng"]], do["ticket"], ("c", do["eng"])))
                    if o["dma"]:
                        n = o["dma_n"]
                        if n >= NDMASEM:
                            waits.append((dsems[e][n % NDMASEM], 16 * (n // NDMASEM), ("d", e, n % NDMASEM)))
                    for sem, val, key in waits:
                        if waited.get(key, 0) >= val:
                            continue
                        waited[key] = val
                        engobj.wait_ge(sem, val)
                    ins = o["fn"](engobj)
                    if o["dma"]:
                        ins.then_inc(dsems[e][o["dma_n"] % NDMASEM], 16)
                    elif "ticket" in o:
                        ins.then_inc(sems[e], 1)
                if extra_final:
                    for qe in ENGINES:
                        for k in range(NDMASEM):
                            c = len(range(k, dcnt[qe], NDMASEM))
                            if c > 0:
                                engobj.wait_ge(dsems[qe][k], 16 * c)
                    for d in final_wait_ops:
                        do = ops[d]
                        if do["dma"]:
                            n = do["dma_n"]
                            engobj.wait_ge(dsems[do["eng"]][n % NDMASEM], 16 * (n // NDMASEM + 1))
                        else:
                            engobj.wait_ge(sems[do["eng"]], do["ticket"])

            @block.sync
            def _(eng):
                run("sp", eng, extra_final=True)

            @block.scalar
            def _(eng):
                run("act", eng)

            @block.gpsimd
            def _(eng):
                run("pool", eng)

            @block.vector
            def _(eng):
                run("dve", eng)

            @block.tensor
            def _(eng):
                run("pe", eng)

import contextlib
import numpy as np
from concourse.bass_utils import run_bass_kernel_spmd

D = 1024
SEQ = 4096
CTX = 256
NTOK = SEQ + CTX
NT = NTOK // 128
NCT = CTX // 128
DEPTH = 4
FH = 2816
NJ = FH // 128
FM_QK, FM_LR, FM_X, TM0 = 0, 256, 320, 1344
TM_A, TM_B, TM_C, TM_D, TM_E = TM0, TM0 + 512, TM0 + 1024, TM0 + 1536, TM0 + 2048
NCOLS = TM0 + 2304


def _colmap():
    m = -np.ones(NCOLS, dtype=np.int64)
    m[0:256] = np.arange(0, 256)
    m[FM_LR:FM_LR + 16] = np.arange(768, 784)
    m[FM_LR + 32:FM_LR + 48] = np.arange(784, 800)
    m[FM_X:FM_X + 1024] = np.arange(1312, 2336)
    m[TM_A:TM_A + 128] = np.arange(128, 256)
    m[TM_A + 128:TM_A + 384] = np.arange(256, 512)
    m[TM_A + 384:TM_A + 400] = np.arange(2336, 2352)
    m[TM_B:TM_B + 256] = np.arange(512, 768)
    m[TM_B + 256:TM_B + 512] = np.arange(3120, 3376)
    m[TM_C:TM_C + 512] = np.arange(800, 1312)
    m[TM_D:TM_D + 256] = np.arange(2352, 2608)
    m[TM_D + 256:TM_D + 512] = np.arange(2608, 2864)
    m[TM_E:TM_E + 256] = np.arange(2864, 3120)
    return m


class Cols:
    def __init__(self):
        self.off = {}
        self.n = 0

    def add(self, name, w):
        self.off[name] = (self.n, self.n + w)
        self.n += w

    def __getitem__(self, name):
        return self.off[name]


CF = Cols()
for _n, _w in [("eps", 1), ("lnqs", 1), ("one", 1), ("ident", 128), ("maskf", 128), ("maskb", 128),
               ("slf", 128), ("slb", 128), ("Rf", 129), ("Rb", 129), ("Lf", 128), ("Lb", 128), ("ones", 128),
               ("retM", 512), ("retEQf", 4), ("retEQb", 4), ("retWf", 4), ("retWb", 4), ("retdec", 4),
               ("bmg", 128), ("bmr", 128)]:
    CF.add(_n, _w)

PLC = Cols()
for _n, _w in [("adab", 48), ("npre", 8), ("npost", 8), ("nfpre", 8), ("nfpost", 8), ("GU", 256),
               ("glan", 256), ("ssdn", 512), ("retn", 256), ("ssdd", 512), ("convw", 40), ("convb", 8),
               ("dtb", 16), ("alog", 16)]:
    PLC.add(_n, _w)


def _host_consts():
    c = np.zeros((128, CF.n), np.float32)
    def put(name, arr):
        a, b = CF[name]
        c[:, a:b] = np.asarray(arr, np.float32).reshape(128, b - a)
    j = np.arange(128)[:, None]
    i = np.arange(128)[None, :]
    maskf = (j <= i).astype(np.float32)
    maskb = (j >= i).astype(np.float32)
    put("eps", np.full((128, 1), 1e-6))
    put("lnqs", np.full((128, 1), np.log(32.0 ** -0.5)))
    put("one", np.ones((128, 1)))
    put("ident", np.eye(128))
    put("maskf", maskf)
    put("maskb", maskb)
    put("slf", 1 - maskf)
    put("slb", 1 - maskb)
    put("Rf", np.concatenate([maskf, np.ones((128, 1))], 1) * (-1 / 16))
    put("Rb", np.concatenate([maskb, np.ones((128, 1))], 1) * (-1 / 16))
    put("Lf", (1 - maskf) * (-1 / 16))
    put("Lb", (1 - maskb) * (-1 / 16))
    put("ones", np.ones((128, 128)))
    lg = np.log1p(-np.exp2(-5.0 - np.arange(4, dtype=np.float32))).astype(np.float32).astype(np.float64)
    M = np.zeros((128, 4, 128))
    for h in range(4):
        M[:, (h % 2) * 2 + h // 2, :] = np.exp(lg[h] * np.abs(i - j)) * np.where(i == j, 2.0, 1.0)
    put("retM", M)
    tt = np.arange(128)[:, None].astype(np.float64)
    put("retEQf", np.exp(lg[None, :] * (tt + 1)))
    put("retEQb", np.exp(lg[None, :] * (128 - tt)))
    put("retWf", np.exp(lg[None, :] * (127 - tt)))
    put("retWb", np.exp(lg[None, :] * tt))
    put("retdec", np.tile(np.exp(lg * 128)[None, :], (128, 1)))
    p = np.arange(128)[:, None]
    cc = np.arange(128)[None, :]
    put("bmg", ((p // 32) == (cc // 64)).astype(np.float32))
    put("bmr", ((p // 64) == (cc // 64)).astype(np.float32))
    return c


def _host_rope():
    rows = SEQ // 64
    row = np.repeat(np.arange(rows), 64).astype(np.float32)
    col = np.tile(np.arange(64), rows).astype(np.float32)
    inv = (np.float32(10000.0) ** (-np.arange(16, dtype=np.float32) / np.float32(16))).astype(np.float32)
    ang = np.concatenate([row[:, None] * inv, col[:, None] * inv], -1).astype(np.float32)
    cos, sin = np.cos(ang).astype(np.float32), np.sin(ang).astype(np.float32)
    t = np.zeros((SEQ, 256), np.float32)
    t[:, 0:64] = np.concatenate([cos, cos], 1) * 0.125
    t[:, 64:128] = np.concatenate([-sin, sin], 1) * 0.125
    t[:, 128:192] = np.concatenate([cos, cos], 1)
    t[:, 192:256] = np.concatenate([-sin, sin], 1)
    return t


def _fm(v, n):
    return np.asarray(v, np.float32).reshape(n, 128).T


def _host_pl(inp, l):
    a = np.zeros((128, PLC.n), np.float32)
    def put(name, arr):
        s, e = PLC[name]
        a[:, s:e] = np.asarray(arr, np.float32).reshape(128, e - s)
    put("adab", _fm(inp["ada_b"][l], 48))
    put("npre", _fm(inp["norm_mix_pre"][l], 8))
    put("npost", _fm(inp["norm_mix_post"][l], 8))
    put("nfpre", _fm(inp["norm_ffn_pre"][l], 8))
    put("nfpost", _fm(inp["norm_ffn_post"][l], 8))
    gu = np.zeros((128, 256), np.float32)
    gu[0:16, 0:128] = inp["gla_gate_up"][l][0]
    gu[16, 0:128] = inp["gla_gate_b"][l][0]
    gu[32:48, 128:256] = inp["gla_gate_up"][l][1]
    gu[48, 128:256] = inp["gla_gate_b"][l][1]
    put("GU", gu)
    rep = lambda v: np.tile(np.asarray(v, np.float32)[None, :], (128, 1))
    put("glan", rep(inp["gla_norm"][l]))
    put("ssdn", rep(inp["ssd_norm"][l]))
    put("retn", rep(inp["ret_norm"][l]))
    put("ssdd", rep(np.repeat(inp["ssd_d"][l], 64)))
    cw = inp["ssd_conv_w"][l]
    put("convw", cw.reshape(5, 8, 128).transpose(2, 1, 0).reshape(128, 40))
    put("convb", _fm(inp["ssd_conv_b"][l], 8))
    put("dtb", rep(inp["ssd_dt_bias"][l].reshape(16)))
    put("alog", rep(inp["ssd_a_log"][l].reshape(16)))
    return a


STOP = None


class _Stop(Exception):
    pass


def _chk(stage):
    if STOP is not None and STOP == stage:
        raise _Stop()


def build_nc(depth=DEPTH, debug_layers=None):
    nc = bass.Bass("TRN2", target_bir_lowering=False)
    es = contextlib.ExitStack()
    with es:
        def din(name, shape, dt=F32):
            return nc.dram_tensor(name, shape, dt, kind="ExternalInput").ap()
        def dscr(name, shape, dt=F32):
            return nc.dram_tensor(name, shape, dt, kind="Internal").ap()
        xin = din("xin", [NTOK, D])
        cvec = din("cvec", [128, 16])
        constf_d = din("constf", [128, CF.n])
        rope_d = din("rope", [SEQ, 256])
        pl_d = din("pl", [depth, 128, PLC.n])
        cbrow_d = din("cbrow", [depth, 1, 768])
        w_in_d = din("w_in", [depth, D, NCOLS])
        w_out_d = din("w_out", [depth, D, D])
        w13_d = din("w13", [depth, D, 2 * FH])
        w2_d = din("w2", [depth, FH, D])
        ada_d = din("ada_w", [depth, D, 6 * D])
        out_d = nc.dram_tensor("out", [SEQ, D], F32, kind="ExternalOutput").ap()
        res_d = dscr("res", [NTOK, D])
        wb_in = dscr("wb_in", [depth, D, NCOLS], BF16)
        wb_out = dscr("wb_out", [depth, D, D], BF16)
        wb_13 = dscr("wb_13", [depth, NJ, 128, 8 * 256], BF16)
        wb_2 = dscr("wb_2", [depth, FH, D], BF16)
        wb_ada = dscr("wb_ada", [depth, D, 6 * D], BF16)
        sA_d = dscr("sA", [NT, 128, 1026])
        sB_d = dscr("sB", [NT, 128, 1536], BF16)
        sS_d = dscr("sS", [NT, 128, 1040])
        sT_d = dscr("sT", [NT, 128, 768], BF16)
        gscr_d = dscr("gscr", [4, 8, 256])

        def sb(name, shape, dt=F32):
            return es.enter_context(nc.sbuf_tensor("sb_" + name, shape, dt))
        P = Prog(nc)
        constf = sb("constf", [128, CF.n])
        def C(name, rows=slice(0, 128)):
            a, b = CF[name]
            return constf[rows, a:b]
        identb = sb("identb", [128, 128], BF16)
        onesb = sb("onesb", [128, 128], BF16)
        pl = sb("pl", [128, PLC.n])
        def PL(name, rows=slice(0, 128)):
            a, b = PLC[name]
            return pl[rows, a:b]
        GUb = sb("GUb", [64, 256], BF16)
        cbrowb = sb("cbrowb", [64, 768], BF16)
        Arep = sb("Arep", [128, 16])
        cdiag = sb("cdiag", [128, 8 * 5 * 128], BF16)
        cdv = cdiag[:, :].rearrange("p (a k c) -> p a k c", a=8, k=5)
        rope_t = sb("rope_t", [128, 256])
        csil = sb("csil", [128, 16], BF16)
        cve = sb("cve", [128, 16])
        modT = sb("modT", [128, 96])
        modv = modT[:, :].rearrange("p (c s) -> p c s", s=2)
        AB = sb("AB", [128, 2 * 6 * 8])
        ABv = AB[:, :].rearrange("p (s v k) -> p s v k", s=2, v=6)
        Grep_ = sb("Grep_", [128, 2 * 2 * 1024])
        Gv = Grep_[:, :].rearrange("p (s v c) -> p s v c", s=2, v=2)
        dg = sb("dg", [128, 128])
        wbig = sb("wbig", [128, 30720], BF16)
        w_in_s = wbig[:, 0:8 * NCOLS].rearrange("p (k c) -> p k c", k=8)
        w_out_s = wbig[:, 0:8192].rearrange("p (k c) -> p k c", k=8)
        w2_s = wbig[:, 8192:8192 + NJ * 1024].rearrange("p (j c) -> p j c", j=NJ)
        arF = sb("arF", [128, 2560])
        arH = sb("arH", [128, 10336], BF16)
        adaw = sb("adaw", [128, 8 * 512], BF16)
        adawv = adaw[:, :].rearrange("p (k c) -> p k c", k=8)
        wgu = [arH[:, 4864 + i * 2048:4864 + (i + 1) * 2048] for i in range(2)] + [adaw[:, i * 2048:(i + 1) * 2048] for i in range(2)]
        xt = sb("xt", [128, D])
        junk = sb("junk", [128, D])
        xh = sb("xh", [128, D], BF16)
        hT = sb("hT", [128, 2 * D], BF16)
        hTv = hT[:, 0:D].rearrange("p (k t) -> p k t", k=8)
        hT2v = hT[:, :].rearrange("p (k t) -> p k t", k=8)
        st = sb("st", [128, 32])
        lrT = sb("lrT", [64, 128], BF16)
        xraw = [arH[:, 7168 + i * 1056:7168 + (i + 1) * 1056] for i in range(3)]
        xrv = [x_[:, :].rearrange("p (a t) -> p a t", a=8) for x_ in xraw]
        dtraw = [sb("dtraw%d" % i, [128, 16]) for i in range(2)]
        vg = sb("vg", [128, 256], BF16)
        ez = sb("ez", [128, 256])
        spt = sb("spt", [128, 256])
        EG = sb("EG", [64, 2 * 2 * 128])
        EGv = EG[:, :].rearrange("q (d p i) -> q d p i", d=2, p=2)
        EGn = sb("EGn", [64, 512])
        EGnv = EGn[:, :].rearrange("q (d p i) -> q d p i", d=2, p=2)
        gdec = sb("gdec", [64, 4])
        ED = sb("ED", [128, 256])
        qtg = sb("qtg", [64, 512], BF16)
        qtgv = qtg[:, :].rearrange("q (d p i) -> q d p i", d=2, p=2)
        ktg = sb("ktg", [64, 1024], BF16)
        ktgv = ktg[:, :].rearrange("q (d a p i) -> q d a p i", d=2, a=2, p=2)
        kTz = sb("kTz", [128, 512], BF16)
        kTzv = kTz[:, :].rearrange("q (a p i) -> q a p i", a=2, p=2)
        ksg = sb("ksg", [128, 256], BF16)
        Pg = arH[:, 6144:7168]
        Pgv = Pg[:, :].rearrange("p (d a b i) -> p d a b i", d=2, a=2, b=2)
        S_g = sb("S_g", [64, 256]); Sb_g = sb("Sb_g", [64, 256], BF16)
        S_s = sb("S_s", [128, 512]); Sb_s = sb("Sb_s", [128, 512], BF16)
        S_r = sb("S_r", [128, 256]); Sb_r = sb("Sb_r", [128, 256], BF16)
        tmpS = sb("tmpS", [128, 512])
        stA = sb("stA", [128, 1026]); stB = sb("stB", [128, 1536], BF16)
        stS = sb("stS", [128, 1040]); stT = sb("stT", [128, 768], BF16)
        ropetmp = sb("ropetmp", [128, 512])
        qr = sb("qr", [128, 256], BF16); kr = sb("kr", [128, 256], BF16); vr = sb("vr", [128, 256], BF16)
        qkT = sb("qkT", [128, 512], BF16)
        qkTv = qkT[:, :].rearrange("p (a t) -> p a t", a=4)
        Pr = sb("Pr", [128, 512], BF16)
        Prv = Pr[:, :].rearrange("p (a b i) -> p a b i", a=2, b=2)
        vtl = sb("vtl", [128, 512], BF16)
        xs = sb("xs", [128, 512], BF16); Btok = sb("Btok", [128, 256], BF16)
        BCT = sb("BCT", [128, 512], BF16)
        BCTv = BCT[:, :].rearrange("p (a t) -> p a t", a=4)
        dte = sb("dte", [128, 16]); dtv = sb("dtv", [128, 16]); lgt = sb("lgt", [128, 16])
        negG = sb("negG", [128, 16]); wexp = sb("wexp", [128, 16]); decrep = sb("decrep", [128, 16]); EGi = sb("EGi", [128, 16])
        gts = sb("gts", [8, 256])
        Grp = arF[:, 0:2048]
        Grpv = Grp[:, :].rearrange("p (h d i) -> p h d i", h=8, d=2)
        Lm = arH[:, 0:2048]
        Lmv = Lm[:, :].rearrange("p (d h i) -> p d h i", d=2, h=8)
        CBm = arF[:, 2048:2560]
        CBmv = CBm[:, :].rearrange("p (d g i) -> p d g i", d=2, g=2)
        Ps = arH[:, 2048:4096]
        Psv = Ps[:, :].rearrange("p (d h i) -> p d h i", d=2, h=8)
        vts = arH[:, 4096:5120]
        xss = arH[:, 5120:6144]
        Oall = arF[:, 0:1024]
        mixed = arH[:, 0:1024]
        mixT = arH[:, 1024:2048]
        mixTv = mixT[:, :].rearrange("p (k t) -> p k t", k=8)
        fsq = arF[:, 1024:1536]
        sil = arF[:, 1536:2048]
        actT = arH[:, 2048:2048 + NJ * 128]
        actTv = actT[:, :].rearrange("p (j t) -> p j t", j=NJ)
        sgt = arF[:, 2048:2304]
        silb = arH[:, 8960:9984]
        actTb = cdiag[:, 0:NJ * 128]
        actTbv = actTb.rearrange("p (j t) -> p j t", j=NJ)
        psf = [es.enter_context(nc.psum_tensor("psf%d" % i, [128, 512], F32)) for i in range(6)]
        psb = [es.enter_context(nc.psum_tensor("psb%d" % i, [128, 1024], BF16)) for i in range(2)]
        free_f = list(range(6)); free_b = list(range(2))
        def PSF():
            i = free_f.pop(0)
            return psf[i], "psf%d" % i
        def PSB():
            i = free_b.pop(0)
            return psb[i], "psb%d" % i
        def REL(*keys):
            for key in keys:
                (free_f if key.startswith("psf") else free_b).append(int(key[3:]))

        op = P.op
        def bc(ap, shape):
            return ap.to_broadcast(shape)

        ARKEYS = ["Grp", "CBm", "Lm", "Ps", "vts", "xss", "Pg", "xraw0", "xraw1", "xraw2", "Og", "Os", "Or", "fsq", "fsqb", "fsq2",
                  "sil", "sgt", "mixed", "mixT", "actT", "wgu0", "wgu1", "wgu0u", "wgu1u", "wgu2", "wgu3", "wgu2u", "wgu3u", "adaw", "cdiag", "actTb", "silb"]
        bard = sb("bard", [128, 1])
        def barrier():
            op("pool", lambda e: e.memset(bard[:, :], 0.0), reads=[], writes=ARKEYS + ["bard"])
        op("sp", lambda e: e.dma_start(out=constf[:, :], in_=constf_d[:, :]), writes=["constf"], dma=True)
        op("sp", lambda e: e.dma_start(out=cve[:, :], in_=cvec[:, :]), writes=["cve"], dma=True)
        op("dve", lambda e: e.tensor_copy(out=identb[:, :], in_=C("ident")), reads=["constf"], writes=["identb"])
        op("dve", lambda e: e.tensor_copy(out=onesb[:, :], in_=C("ones")), reads=["constf"], writes=["onesb"])
        op("act", lambda e: e.activation(out=csil[:, :], in_=cve[:, :], func=AF.Silu), reads=["cve"], writes=["csil"])
        for nm_, tn_ in (("stA", stA), ("stB", stB), ("stS", stS), ("stT", stT)):
            op("pool", lambda e, tn_=tn_: e.memset(tn_[:, :], 0.0), writes=[nm_])
        for t in range(NT):
            op("sp", lambda e, t=t: e.dma_start(out=res_d[t * 128:(t + 1) * 128, :], in_=xin[t * 128:(t + 1) * 128, :]),
               writes=[("res", t)], dma=True)
        def cast(dst, src, rows, key, l, piece=128):
            for r0 in range(0, rows, piece):
                op("pool", lambda e, r0=r0: e.dma_start(out=dst[l, r0:r0 + piece, :], in_=src[l, r0:r0 + piece, :]),
                   writes=[(key, l, r0 // piece)], dma=True)
        for l in range(depth):
            cast(wb_ada, ada_d, D, "wada", l)
            cast(wb_in, w_in_d, D, "win", l)
            cast(wb_out, w_out_d, D, "wout", l)
            for k_ in range(8):
                for part in range(2):
                    op("pool", lambda e, l=l, k_=k_, part=part: e.dma_start(
                        out=wb_13[l, :, :, k_ * 256 + part * 128:k_ * 256 + (part + 1) * 128],
                        in_=w13_d[l, k_ * 128:(k_ + 1) * 128, part * FH:(part + 1) * FH].rearrange("p (j c) -> j p c", c=128)),
                       writes=[("w13", l, k_)], dma=True)
            cast(wb_2, w2_d, FH, "w2", l)
        W8 = lambda key, l: [(key, l, i) for i in range(8)]
        W22 = lambda key, l: [(key, l, i) for i in range(22)]

        def rstd_from(ss_ap, out_ap, n, reads, writes):
            rows = slice(0, 128)
            op("act", lambda e: e.activation(out=out_ap, in_=ss_ap, func=AF.Ln, scale=1.0 / n, bias=C("eps")),
               reads=list(reads) + ["constf"], writes=writes)
            op("act", lambda e: e.activation(out=out_ap, in_=out_ap, func=AF.Exp, scale=-0.5),
               reads=writes, writes=writes)

        try:
          _chk("prologue")
          for l in range(depth):
              last = (l == depth - 1)
              barrier()
              op("sp", lambda e, l=l: e.dma_start(out=pl[:, :], in_=pl_d[l, :, :]), writes=["pl"], dma=True)
              pm, pmk = PSF()
              for piece in range(12):
                  op("sp", lambda e, l=l, piece=piece: e.dma_start(
                      out=adawv, in_=wb_ada[l, :, piece * 512:(piece + 1) * 512].rearrange("(k p) c -> p k c", p=128)),
                     reads=W8("wada", l), writes=["adaw"], dma=True)
                  for cc in range(4):
                      ch = piece * 4 + cc
                      for k in range(8):
                          op("pe", lambda e, ch=ch, cc=cc, k=k: e.matmul(pm[:, ch * 2:ch * 2 + 2], lhsT=adawv[:, k, cc * 128:(cc + 1) * 128],
                                                                           rhs=csil[:, 2 * k:2 * k + 2],
                                                                           start=(k == 0), stop=(k == 7)),
                             reads=["adaw", "csil"], writes=[pmk])
              op("dve", lambda e: e.tensor_tensor(out=modv, in0=pm[:, 0:96].rearrange("p (c s) -> p c s", s=2),
                                                  in1=PL("adab").unsqueeze(2).to_broadcast([128, 48, 2]), op=ALU.add),
                 reads=[pmk, "pl"], writes=["modT"])
              REL(pmk)
              for s in range(2):
                  def mv(c0, s=s):
                      return modv[:, c0:c0 + 8, s]
                  op("dve", lambda e, s=s, mv=mv: e.scalar_tensor_tensor(out=ABv[:, s, 0, :], in0=mv(8), scalar=1.0, in1=PL("npre"), op0=ALU.add, op1=ALU.mult),
                     reads=["modT", "pl"], writes=["AB"])
                  op("dve", lambda e, s=s, mv=mv: e.tensor_copy(out=ABv[:, s, 1, :], in_=mv(0)), reads=["modT"], writes=["AB"])
                  op("dve", lambda e, s=s, mv=mv: e.tensor_tensor(out=ABv[:, s, 2, :], in0=mv(16), in1=PL("npost"), op=ALU.mult), reads=["modT", "pl"], writes=["AB"])
                  op("dve", lambda e, s=s, mv=mv: e.scalar_tensor_tensor(out=ABv[:, s, 3, :], in0=mv(32), scalar=1.0, in1=PL("nfpre"), op0=ALU.add, op1=ALU.mult),
                     reads=["modT", "pl"], writes=["AB"])
                  op("dve", lambda e, s=s, mv=mv: e.tensor_copy(out=ABv[:, s, 4, :], in_=mv(24)), reads=["modT"], writes=["AB"])
                  op("dve", lambda e, s=s, mv=mv: e.tensor_tensor(out=ABv[:, s, 5, :], in0=mv(40), in1=PL("nfpost"), op=ALU.mult), reads=["modT", "pl"], writes=["AB"])
                  for vi, vsrc in enumerate((2, 5)):
                      for half in range(2):
                          pg_, pgk = PSF()
                          for c4 in range(4):
                              k = half * 4 + c4
                              op("dve", lambda e, s=s, vsrc=vsrc, k=k: e.tensor_scalar(out=dg[:, :], in0=C("ident"), scalar1=ABv[:, s, vsrc, k:k + 1], scalar2=None, op0=ALU.mult),
                                 reads=["AB", "constf"], writes=["dg"])
                              op("pe", lambda e, pg_=pg_, c4=c4: e.matmul(pg_[:, c4 * 128:(c4 + 1) * 128], lhsT=C("ones"), rhs=dg[:, :], start=True, stop=True),
                                 reads=["dg", "constf"], writes=[pgk])
                          op("act", lambda e, s=s, vi=vi, half=half, pg_=pg_: e.activation(out=Gv[:, s, vi, half * 512:(half + 1) * 512], in_=pg_[:, :], func=AF.Copy),
                             reads=[pgk], writes=["Grep"])
                          REL(pgk)
              _chk("mod")
              op("dve", lambda e: e.tensor_copy(out=GUb[:, :], in_=PL("GU", slice(0, 64))), reads=["pl"], writes=["GUb"])
              op("pool", lambda e: e.memset(cbrowb[:, :], 0.0), writes=["cbrowb"])
              op("pool", lambda e, l=l: e.dma_start(out=cbrowb[0:1, :], in_=cbrow_d[l, :, :]), writes=["cbrowb"], dma=True)
              op("act", lambda e: e.activation(out=Arep[:, :], in_=PL("alog"), func=AF.Exp), reads=["pl"], writes=["Arep"])
              op("dve", lambda e: e.tensor_scalar(out=Arep[:, :], in0=Arep[:, :], scalar1=-1.0, scalar2=None, op0=ALU.mult), reads=["Arep"], writes=["Arep"])
              cwv = PL("convw").rearrange("p (a k) -> p a k", a=8)
              for ct in range(8):
                  for k in range(5):
                      op("dve", lambda e, ct=ct, k=k: e.tensor_scalar(out=cdv[:, ct, k, :], in0=C("ident"), scalar1=cwv[:, ct, k:k + 1], scalar2=None, op0=ALU.mult),
                         reads=["pl", "constf"], writes=["cdiag"])
              op("sp", lambda e, l=l: e.dma_start(out=w_in_s, in_=wb_in[l, :, :].rearrange("(k p) c -> p k c", p=128)),
                 reads=W8("win", l), writes=["wbig"], dma=True)
              op("pool", lambda e: e.memset(lrT[:, :], 1.0), writes=["lrT"])
              op("pool", lambda e: e.memset(ktg[:, :], 0.0), writes=["ktg"])
              op("pool", lambda e: e.memset(kTz[:, :], 0.0), writes=["kTz"])
              for nm, tns in (("S_g", S_g), ("S_s", S_s), ("S_r", S_r), ("Sb_g", Sb_g), ("Sb_s", Sb_s), ("Sb_r", Sb_r)):
                  op("pool", lambda e, tns=tns: e.memset(tns[:, :], 0.0), writes=[nm])

              _chk("derived")
              barrier()
              def ssd_tile(t):
                  seg = 0 if t < NCT else 1
                  u = xrv[t % 3]
                  uk = "xraw%d" % (t % 3)
                  px, pxk = PSF(); pB, pBk = PSF(); pBC, pBCk = PSF()
                  for ct in range(6):
                      o_ = px[:, ct * 128:(ct + 1) * 128] if ct < 4 else pB[:, (ct - 4) * 128:(ct - 3) * 128]
                      ok_ = pxk if ct < 4 else pBk
                      for k in range(5):
                          op("pe", lambda e, o_=o_, ct=ct, k=k: e.matmul(o_, lhsT=u[:, ct, k:k + 128], rhs=cdv[:, ct, k, :], start=(k == 0), stop=False),
                             reads=[uk, "cdiag"], writes=[ok_])
                      op("pe", lambda e, o_=o_, ct=ct: e.matmul(o_, lhsT=onesb[0:64, :], rhs=cbrowb[0:64, ct * 128:(ct + 1) * 128], start=False, stop=True, tile_position=(0, 0)),
                         reads=["onesb", "cbrowb"], writes=[ok_])
                  for idx, ct in enumerate((4, 5, 6, 7)):
                      for k in range(5):
                          op("pe", lambda e, idx=idx, ct=ct, k=k: e.matmul(pBC[:, idx * 128:(idx + 1) * 128], lhsT=cdv[:, ct, k, :], rhs=u[:, ct, k:k + 128], start=(k == 0), stop=(k == 4)),
                             reads=[uk, "cdiag"], writes=[pBCk])
                  op("act", lambda e: e.activation(out=xs[:, :], in_=px[:, :], func=AF.Silu), reads=[pxk], writes=["xs"])
                  op("act", lambda e: e.activation(out=Btok[:, :], in_=pB[:, 0:256], func=AF.Silu), reads=[pBk], writes=["Btok"])
                  for idx, ct in enumerate((4, 5, 6, 7)):
                      a0 = PLC["convb"][0]
                      op("act", lambda e, idx=idx, ct=ct, a0=a0: e.activation(out=BCTv[:, idx, :], in_=pBC[:, idx * 128:(idx + 1) * 128], func=AF.Silu, bias=pl[:, a0 + ct:a0 + ct + 1]),
                         reads=[pBCk, "pl"], writes=["BCT"])
                  REL(pxk, pBk, pBCk)
                  op("pool", lambda e: e.tensor_copy(out=stT[:, 0:512], in_=xs[:, :]), reads=["xs"], writes=["stT"])
                  op("pool", lambda e: e.tensor_copy(out=stT[:, 512:768], in_=BCT[:, 256:512]), reads=["BCT"], writes=["stT"])
                  _chk("S%da" % t)
                  dr = dtraw[t % 2]; drk = "dtraw%d" % (t % 2)
                  op("act", lambda e: e.activation(out=dte[:, :], in_=dr[:, :], func=AF.Exp), reads=[drk], writes=["dte"])
                  op("act", lambda e: e.activation(out=dtv[:, :], in_=dte[:, :], func=AF.Ln, bias=C("one")), reads=["dte", "constf"], writes=["dtv"])
                  op("dve", lambda e: e.tensor_tensor(out=lgt[:, :], in0=dtv[:, :], in1=Arep[:, :], op=ALU.mult), reads=["dtv", "Arep"], writes=["lgt"])
                  pg2, pg2k = PSF()
                  for d in range(2):
                      U = C("maskf") if d == 0 else C("maskb")
                      SLm = C("slf") if d == 0 else C("slb")
                      op("pe", lambda e, d=d, U=U: e.matmul(pg2[:, d * 8:(d + 1) * 8], lhsT=U, rhs=lgt[:, d * 8:(d + 1) * 8], start=True, stop=True), reads=["lgt", "constf"], writes=[pg2k])
                      op("pe", lambda e, d=d, SLm=SLm: e.matmul(pg2[:, 16 + d * 8:16 + (d + 1) * 8], lhsT=SLm, rhs=lgt[:, d * 8:(d + 1) * 8], start=True, stop=True), reads=["lgt", "constf"], writes=[pg2k])
                      op("pe", lambda e, d=d, U=U: e.matmul(pg2[0:8, 64 + d * 128:64 + (d + 1) * 128], lhsT=lgt[:, d * 8:(d + 1) * 8], rhs=U, start=True, stop=True), reads=["lgt", "constf"], writes=[pg2k])
                  op("pe", lambda e: e.matmul(pg2[:, 32:48], lhsT=C("ones"), rhs=lgt[:, :], start=True, stop=True), reads=["lgt", "constf"], writes=[pg2k])
                  op("dve", lambda e: e.tensor_scalar(out=negG[:, :], in0=pg2[:, 0:16], scalar1=-1.0, scalar2=None, op0=ALU.mult), reads=[pg2k], writes=["negG"])
                  op("act", lambda e: e.activation(out=EGi[:, :], in_=pg2[:, 0:16], func=AF.Exp), reads=[pg2k], writes=["EGi"])
                  op("act", lambda e: e.activation(out=wexp[:, :], in_=pg2[:, 16:32], func=AF.Exp), reads=[pg2k], writes=["wexp"])
                  op("act", lambda e: e.activation(out=decrep[:, :], in_=pg2[:, 32:48], func=AF.Exp), reads=[pg2k], writes=["decrep"])
                  op("dve", lambda e: e.tensor_copy(out=gts[:, :], in_=pg2[0:8, 64:320]), reads=[pg2k], writes=["gts"])
                  REL(pg2k)
                  sl = t % 4
                  op("sp", lambda e, sl=sl: e.dma_start(out=gscr_d[sl, :, :], in_=gts[:, :]), reads=["gts"], writes=[("gscr", sl)], dma=True)
                  op("sp", lambda e, sl=sl: e.dma_start(out=Grp[:, :], in_=gscr_d[sl:sl + 1, :, :].rearrange("o h c -> o (h c)").to_broadcast([128, 2048])),
                     reads=[("gscr", sl)], writes=["Grp"], dma=True)
                  for d in range(2):
                      for h in range(8):
                          op("dve", lambda e, d=d, h=h: e.tensor_scalar(out=Grpv[:, h, d, :], in0=Grpv[:, h, d, :], scalar1=negG[:, d * 8 + h:d * 8 + h + 1], scalar2=0.0, op0=ALU.add, op1=ALU.min),
                             reads=["Grp", "negG"], writes=["Grp"])
                          op("act", lambda e, d=d, h=h: e.activation(out=Lmv[:, d, h, :], in_=Grpv[:, h, d, :], func=AF.Exp),
                             reads=["Grp"], writes=["Lm"])
                  _chk("S%db" % t)
                  pcb, pcbk = PSF()
                  for g in range(2):
                      op("pe", lambda e, g=g: e.matmul(pcb[:, g * 128:(g + 1) * 128], lhsT=BCTv[:, g, :], rhs=BCTv[:, 2 + g, :], start=True, stop=True), reads=["BCT"], writes=[pcbk])
                  for d in range(2):
                      mk = C("maskf") if d == 0 else C("maskb")
                      op("dve", lambda e, d=d, mk=mk: e.tensor_tensor(out=CBmv[:, d, :, :], in0=pcb[:, 0:256].rearrange("p (g i) -> p g i", g=2),
                                                                      in1=mk.unsqueeze(1).to_broadcast([128, 2, 128]), op=ALU.mult), reads=[pcbk, "constf"], writes=["CBm"])
                  REL(pcbk)
                  for d in range(2):
                      for g in range(2):
                          op("dve", lambda e, d=d, g=g: e.scalar_tensor_tensor(out=Psv[:, d, 4 * g:4 * g + 4, :], in0=Lmv[:, d, 4 * g:4 * g + 4, :], scalar=1.0,
                                                                               in1=CBmv[:, d, g, :].unsqueeze(1).to_broadcast([128, 4, 128]), op0=ALU.min, op1=ALU.mult),
                             reads=["Lm", "CBm"], writes=["Ps"])
                      op("dve", lambda e, d=d: e.tensor_tensor(out=vts[:, d * 512:(d + 1) * 512].rearrange("p (h c) -> p h c", h=8), in0=xs[:, :].rearrange("p (h c) -> p h c", h=8),
                                                               in1=dtv[:, d * 8:(d + 1) * 8].unsqueeze(2).to_broadcast([128, 8, 64]), op=ALU.mult), reads=["xs", "dtv"], writes=["vts"])
                      op("dve", lambda e, d=d: e.tensor_tensor(out=xss[:, d * 512:(d + 1) * 512].rearrange("p (h c) -> p h c", h=8), in0=vts[:, d * 512:(d + 1) * 512].rearrange("p (h c) -> p h c", h=8),
                                                               in1=wexp[:, d * 8:(d + 1) * 8].unsqueeze(2).to_broadcast([128, 8, 64]), op=ALU.mult), reads=["vts", "wexp"], writes=["xss"])
                  _chk("S%dc" % t)
                  po, pok = PSF(); pi, pik = PSF()
                  for h in range(8):
                      for d in range(2):
                          op("pe", lambda e, h=h, d=d: e.matmul(po[:, h * 64:(h + 1) * 64], lhsT=Psv[:, d, h, :], rhs=vts[:, d * 512 + h * 64:d * 512 + (h + 1) * 64], start=(d == 0), stop=(d == 1)),
                             reads=["Ps", "vts"], writes=[pok])
                  for g in range(2):
                      op("pe", lambda e, g=g: e.matmul(pi[:, g * 256:(g + 1) * 256], lhsT=BCTv[:, 2 + g, :], rhs=Sb_s[:, g * 256:(g + 1) * 256], start=True, stop=True), reads=["BCT", "Sb_s"], writes=[pik])
                  op("dve", lambda e: e.tensor_tensor(out=tmpS[:, :].rearrange("p (h c) -> p h c", h=8), in0=pi[:, :].rearrange("p (h c) -> p h c", h=8),
                                                      in1=EGi[:, 0:8].unsqueeze(2).to_broadcast([128, 8, 64]), op=ALU.mult), reads=[pik, "EGi"], writes=["tmpS"])
                  op("dve", lambda e: e.tensor_tensor(out=stS[:, 0:512], in0=tmpS[:, :], in1=po[:, :], op=ALU.add), reads=["tmpS", pok], writes=["stS"])
                  REL(pok, pik)
                  op("pool", lambda e: e.tensor_copy(out=stS[:, 512:520], in_=EGi[:, 8:16]), reads=["EGi"], writes=["stS"])
                  op("pool", lambda e: e.tensor_copy(out=stS[:, 520:528], in_=decrep[:, 8:16]), reads=["decrep"], writes=["stS"])
                  pds = []
                  for d in range(2):
                      pd_, pdk = PSF()
                      pds.append((pd_, pdk))
                      for g in range(2):
                          op("pe", lambda e, d=d, g=g, pd_=pd_: e.matmul(pd_[:, g * 256:(g + 1) * 256], lhsT=Btok[:, g * 128:(g + 1) * 128], rhs=xss[:, d * 512 + g * 256:d * 512 + (g + 1) * 256], start=True, stop=True),
                             reads=["Btok", "xss"], writes=[pdk])
                  op("act", lambda e: e.activation(out=stS[:, 528:1040], in_=pds[1][0][:, :], func=AF.Copy), reads=[pds[1][1]], writes=["stS"])
                  op("dve", lambda e: e.tensor_tensor(out=tmpS[:, :].rearrange("p (h c) -> p h c", h=8), in0=S_s[:, :].rearrange("p (h c) -> p h c", h=8),
                                                      in1=decrep[:, 0:8].unsqueeze(2).to_broadcast([128, 8, 64]), op=ALU.mult), reads=["S_s", "decrep"], writes=["tmpS"])
                  op("dve", lambda e: e.tensor_tensor(out=S_s[:, :], in0=tmpS[:, :], in1=pds[0][0][:, :], op=ALU.add), reads=["tmpS", pds[0][1]], writes=["S_s"])
                  REL(pds[0][1], pds[1][1])
                  op("act", lambda e: e.activation(out=Sb_s[:, :], in_=S_s[:, :], func=AF.Copy), reads=["S_s"], writes=["Sb_s"])
                  op("sp", lambda e, t=t: e.dma_start(out=sS_d[t, :, :], in_=stS[:, :]), reads=["stS"], writes=[("sS", t)], dma=True)
                  op("sp", lambda e, t=t: e.dma_start(out=sT_d[t, :, :], in_=stT[:, :]), reads=["stT"], writes=[("sT", t)], dma=True)

              def load_tile(t):
                  op("sp", lambda e, t=t: e.dma_start(out=xt[:, :], in_=res_d[t * 128:(t + 1) * 128, :]), reads=[("res", t)], writes=["xt"], dma=True)
                  if t >= NCT:
                      op("sp", lambda e, t=t: e.dma_start(out=rope_t[:, :], in_=rope_d[(t - NCT) * 128:(t - NCT + 1) * 128, :]), writes=["rope_t"], dma=True)
              load_tile(0)
              for t in range(NT):
                  seg = 0 if t < NCT else 1
                  op("pool", lambda e: e.memset(st[:, 0:1], 0.0), writes=["st0"])
                  op("act", lambda e: e.activation(out=junk[:, :], in_=xt[:, :], func=AF.Square, accum_out=st[:, 0:1]), reads=["xt", "st0"], writes=["junk", "st0"])
                  rstd_from(st[:, 0:1], st[:, 1:2], D, ["st0"], ["st1"])
                  op("dve", lambda e: e.tensor_scalar(out=xh[:, :], in0=xt[:, :], scalar1=st[:, 1:2], scalar2=None, op0=ALU.mult), reads=["xt", "st1"], writes=["xh"])
                  pT, pTk = PSB()
                  for k in range(8):
                      op("pe", lambda e, k=k: e.transpose(pT[:, k * 128:(k + 1) * 128], xh[:, k * 128:(k + 1) * 128], identb[:, :]), reads=["xh", "identb"], writes=[pTk])
                  for k in range(8):
                      op("act", lambda e, k=k, seg=seg: e.activation(out=hTv[:, k, :], in_=pT[:, k * 128:(k + 1) * 128], func=AF.Identity,
                                                                    scale=ABv[:, seg, 0, k:k + 1], bias=ABv[:, seg, 1, k:k + 1]), reads=[pTk, "AB"], writes=["hT"])
                  REL(pTk)
                  _chk("A%da" % t)
                  pA, pAk = PSF()
                  for g in range(4):
                      for k in range(8):
                          op("pe", lambda e, g=g, k=k: e.matmul(pA[0:64, g * 128:(g + 1) * 128], lhsT=w_in_s[:, k, FM_QK + g * 64:FM_QK + (g + 1) * 64], rhs=hTv[:, k, :], start=(k == 0), stop=(k == 7)),
                             reads=["wbig", "hT"], writes=[pAk])
                  pL, pLk = PSF()
                  for k in range(8):
                      op("pe", lambda e, k=k: e.matmul(pL[0:64, 0:128], lhsT=w_in_s[:, k, FM_LR:FM_LR + 64], rhs=hTv[:, k, :], start=(k == 0), stop=(k == 7)), reads=["wbig", "hT"], writes=[pLk])
                  op("act", lambda e: e.activation(out=lrT[0:16, :], in_=pL[0:16, 0:128], func=AF.Copy), reads=[pLk], writes=["lrT"])
                  op("act", lambda e: e.activation(out=lrT[32:48, :], in_=pL[32:48, 0:128], func=AF.Copy), reads=[pLk], writes=["lrT"])
                  REL(pLk)
                  cur = xrv[t % 3]; curk = "xraw%d" % (t % 3)
                  prv = xrv[(t - 1) % 3]; prvk = "xraw%d" % ((t - 1) % 3)
                  first_in_seg = (t == 0 or t == NCT)
                  last_in_seg = (t == NCT - 1 or t == NT - 1)
                  pXs = []
                  for hx in range(2):
                      pX, pXk = PSF()
                      pXs.append((pX, pXk))
                      for c4 in range(4):
                          ct = hx * 4 + c4
                          for k in range(8):
                              op("pe", lambda e, pX=pX, c4=c4, ct=ct, k=k: e.matmul(pX[:, c4 * 128:(c4 + 1) * 128], lhsT=w_in_s[:, k, FM_X + ct * 128:FM_X + (ct + 1) * 128], rhs=hTv[:, k, :], start=(k == 0), stop=(k == 7)),
                                 reads=["wbig", "hT"], writes=[pXk])
                  for hx in range(2):
                      pX, pXk = pXs[hx]
                      pv = pX[:, :].rearrange("p (a t) -> p a t", a=4)
                      op("act", lambda e, hx=hx, pv=pv: e.activation(out=cur[:, hx * 4:(hx + 1) * 4, 2:130], in_=pv, func=AF.Copy), reads=[pXk], writes=[curk])
                      if first_in_seg:
                          op("pool", lambda e, hx=hx: e.memset(cur[:, hx * 4:(hx + 1) * 4, 0:2], 0.0), writes=[curk])
                      else:
                          op("dve", lambda e, hx=hx, pv=pv: e.tensor_copy(out=prv[:, hx * 4:(hx + 1) * 4, 130:132], in_=pv[:, :, 0:2]), reads=[pXk], writes=[prvk])
                          op("pool", lambda e, hx=hx: e.tensor_copy(out=cur[:, hx * 4:(hx + 1) * 4, 0:2], in_=prv[:, hx * 4:(hx + 1) * 4, 128:130]), reads=[prvk], writes=[curk])
                      if last_in_seg:
                          op("pool", lambda e, hx=hx: e.memset(cur[:, hx * 4:(hx + 1) * 4, 130:132], 0.0), writes=[curk])
                      REL(pXk)
                  banks = {}
                  for nm, c0, w in (("A", TM_A, 512), ("B", TM_B, 512), ("C", TM_C, 512), ("D", TM_D, 512), ("E", TM_E, 256)):
                      pb_, pbk = PSF()
                      banks[nm] = (pb_, pbk)
                      for k in range(8):
                          op("pe", lambda e, pb_=pb_, c0=c0, w=w, k=k: e.matmul(pb_[:, 0:w], lhsT=hTv[:, k, :], rhs=w_in_s[:, k, c0:c0 + w], start=(k == 0), stop=(k == 7)),
                             reads=["wbig", "hT"], writes=[pbk])
                  bA, bAk = banks["A"]; bB, bBk = banks["B"]; bC, bCk = banks["C"]; bD, bDk = banks["D"]; bE, bEk = banks["E"]
                  op("act", lambda e: e.activation(out=vg[:, :], in_=bA[:, 128:384], func=AF.Copy), reads=[bAk], writes=["vg"])
                  op("dve", lambda e, t=t: e.tensor_tensor(out=dtraw[t % 2][:, :], in0=bA[:, 384:400], in1=PL("dtb"), op=ALU.add), reads=[bAk, "pl"], writes=["dtraw%d" % (t % 2)])
                  op("act", lambda e: e.activation(out=stB[:, 0:512], in_=bB[:, :], func=AF.Copy), reads=[bBk], writes=["stB"])
                  op("act", lambda e: e.activation(out=stB[:, 512:1024], in_=bC[:, :], func=AF.Copy), reads=[bCk], writes=["stB"])
                  op("act", lambda e: e.activation(out=vr[:, :], in_=bE[:, 0:256], func=AF.Copy), reads=[bEk], writes=["vr"])
                  REL(bBk, bCk, bEk)
                  if seg == 1:
                      for which, src0, dst, tb0 in ((0, 0, qr, 0), (1, 256, kr, 128)):
                          sv = bD[:, src0:src0 + 256].rearrange("p (h s c) -> p h s c", h=4, s=2)
                          tv = ropetmp[:, 0:256].rearrange("p (h s c) -> p h s c", h=4, s=2)
                          tv2 = ropetmp[:, 256:512].rearrange("p (h s c) -> p h s c", h=4, s=2)
                          cosv = rope_t[:, tb0:tb0 + 64].rearrange("p (s c) -> p s c", s=2)
                          sinv = rope_t[:, tb0 + 64:tb0 + 128].rearrange("p (s c) -> p s c", s=2)
                          op("dve", lambda e, sv=sv, tv=tv, cosv=cosv: e.tensor_tensor(out=tv, in0=sv, in1=cosv.unsqueeze(1).to_broadcast([128, 4, 2, 32]), op=ALU.mult), reads=[bDk, "rope_t"], writes=["ropetmp"])
                          op("dve", lambda e, sv=sv, tv2=tv2, sinv=sinv: e.tensor_tensor(out=tv2[:, :, 0, :], in0=sv[:, :, 1, :], in1=sinv[:, 0, :].unsqueeze(1).to_broadcast([128, 4, 32]), op=ALU.mult), reads=[bDk, "rope_t"], writes=["ropetmp2"])
                          op("dve", lambda e, sv=sv, tv2=tv2, sinv=sinv: e.tensor_tensor(out=tv2[:, :, 1, :], in0=sv[:, :, 0, :], in1=sinv[:, 1, :].unsqueeze(1).to_broadcast([128, 4, 32]), op=ALU.mult), reads=[bDk, "rope_t"], writes=["ropetmp2"])
                          op("dve", lambda e, dst=dst: e.tensor_tensor(out=dst[:, :], in0=ropetmp[:, 0:256], in1=ropetmp[:, 256:512], op=ALU.add), reads=["ropetmp", "ropetmp2"], writes=["qr" if which == 0 else "kr"])
                          _chk("R%dw%d" % (t, which))
                  else:
                      op("act", lambda e: e.activation(out=qr[:, :], in_=bD[:, 0:256], func=AF.Copy, scale=0.125), reads=[bDk], writes=["qr"])
                      op("act", lambda e: e.activation(out=kr[:, :], in_=bD[:, 256:512], func=AF.Copy), reads=[bDk], writes=["kr"])
                  REL(bDk)
                  if t + 1 < NT:
                      load_tile(t + 1)
                  _chk("A%db" % t)
                  pz, pzk = PSF()
                  op("pe", lambda e: e.matmul(pz[:, 0:256], lhsT=lrT[0:64, :], rhs=GUb[0:64, :], start=True, stop=True, tile_position=(0, 0)), reads=["lrT", "GUb"], writes=[pzk])
                  _chk("A%db0" % t)
                  op("act", lambda e: e.activation(out=ez[:, :], in_=pz[:, 0:256], func=AF.Exp, scale=-1.0), reads=[pzk], writes=["ez"])
                  REL(pzk)
                  _chk("A%db0e" % t)
                  op("act", lambda e: e.activation(out=spt[:, :], in_=ez[:, :], func=AF.Ln, bias=C("one")), reads=["ez", "constf"], writes=["spt"])
                  _chk("A%db1" % t)
                  pGs = []
                  for d in range(2):
                      pG, pGk = PSF()
                      pGs.append((pG, pGk))
                      R = C("Rf") if d == 0 else C("Rb")
                      for p_ in range(2):
                          op("pe", lambda e, pG=pG, d=d, p_=p_, R=R: e.matmul(pG[0:64, p_ * 129:(p_ + 1) * 129], lhsT=spt[:, d * 128 + p_ * 64:d * 128 + (p_ + 1) * 64], rhs=R, start=True, stop=True),
                             reads=["spt", "constf"], writes=[pGk])
                  pD, pDk = PSF()
                  for d in range(2):
                      Lc = C("Lf") if d == 0 else C("Lb")
                      op("pe", lambda e, d=d, Lc=Lc: e.matmul(pD[:, d * 128:(d + 1) * 128], lhsT=Lc, rhs=spt[:, d * 128:(d + 1) * 128], start=True, stop=True), reads=["spt", "constf"], writes=[pDk])
                  for d in range(2):
                      pG, pGk = pGs[d]
                      gv = pG[0:64, 0:258].rearrange("q (p i) -> q p i", p=2)
                      op("act", lambda e, d=d, gv=gv: e.activation(out=EGv[:, d, :, :], in_=gv[:, :, 0:128], func=AF.Exp, bias=C("lnqs", slice(0, 64))), reads=[pGk, "constf"], writes=["EG"])
                      op("act", lambda e, d=d, gv=gv: e.activation(out=EGnv[:, d, :, :], in_=gv[:, :, 0:128], func=AF.Exp, scale=-1.0), reads=[pGk], writes=["EGn"])
                      op("act", lambda e, d=d, gv=gv: e.activation(out=gdec[:, d * 2:(d + 1) * 2], in_=gv[:, :, 128], func=AF.Exp), reads=[pGk], writes=["gdec"])
                  op("act", lambda e: e.activation(out=ED[:, :], in_=pD[:, 0:256], func=AF.Exp), reads=[pDk], writes=["ED"])
                  REL(pGs[0][1], pGs[1][1], pDk)
                  _chk("A%db2" % t)
                  qv_ = pA[0:64, 0:256].rearrange("q (p i) -> q p i", p=2)
                  kv_ = pA[0:64, 256:512].rearrange("q (p i) -> q p i", p=2)
                  for d in range(2):
                      op("dve", lambda e, d=d: e.tensor_tensor(out=qtgv[:, d, :, :], in0=qv_, in1=EGv[:, d, :, :], op=ALU.mult), reads=[pAk, "EG"], writes=["qtg"])
                      for hh in range(2):
                          op("dve", lambda e, d=d, hh=hh: e.tensor_tensor(out=ktgv[32 * hh:32 * hh + 32, d, hh, :, :], in0=kv_[32 * hh:32 * hh + 32, :, :], in1=EGnv[32 * hh:32 * hh + 32, d, :, :], op=ALU.mult), reads=[pAk, "EGn"], writes=["ktg"])
                      op("dve", lambda e, d=d: e.tensor_tensor(out=ksg[:, d * 128:(d + 1) * 128], in0=bA[:, 0:128], in1=ED[:, d * 128:(d + 1) * 128], op=ALU.mult), reads=[bAk, "ED"], writes=["ksg"])
                  REL(pAk, bAk)
                  _chk("A%db2d" % t)
                  for d in range(2):
                      mk = C("maskf") if d == 0 else C("maskb")
                      for hh in range(2):
                          pS, pSk = PSF()
                          for p_ in range(2):
                              op("pe", lambda e, pS=pS, d=d, p_=p_, hh=hh: e.matmul(pS[:, p_ * 128:(p_ + 1) * 128], lhsT=ktgv[:, d, hh, p_, :], rhs=qtgv[:, d, p_, :], start=True, stop=True, tile_position=(0, 0)),
                                 reads=["ktg", "qtg"], writes=[pSk])
                          op("dve", lambda e, pS=pS, d=d, hh=hh, mk=mk: e.tensor_tensor(out=Pgv[:, d, hh, :, :], in0=pS[:, 0:256].rearrange("p (b i) -> p b i", b=2),
                                                                                in1=mk.unsqueeze(1).to_broadcast([128, 2, 128]), op=ALU.mult), reads=[pSk, "constf"], writes=["Pg"])
                          REL(pSk)
                  _chk("A%db3" % t)
                  pO, pOk = PSF()
                  for p_ in range(2):
                      op("pe", lambda e, p_=p_: e.matmul(pO[:, p_ * 128:(p_ + 1) * 128], lhsT=qtgv[:, 0, p_, :], rhs=Sb_g[:, p_ * 128:(p_ + 1) * 128], start=True, stop=False, skip_group_check=True, tile_position=(0, 0)),
                         reads=["qtg", "Sb_g"], writes=[pOk])
                      for h in (2 * p_, 2 * p_ + 1):
                          for d in range(2):
                              op("pe", lambda e, h=h, d=d: e.matmul(pO[:, h * 64:(h + 1) * 64], lhsT=Pgv[:, d, h % 2, h // 2, :], rhs=vg[:, h * 64:(h + 1) * 64], start=False, stop=(d == 1 and h == 2 * p_ + 1), skip_group_check=True),
                                 reads=["Pg", "vg"], writes=[pOk])
                  op("act", lambda e: e.activation(out=stA[:, 0:256], in_=pO[:, 0:256], func=AF.Copy), reads=[pOk], writes=["stA"])
                  REL(pOk)
                  _chk("A%db4" % t)
                  pDS, pDSk = PSF()
                  for d in range(2):
                      for p_ in range(2):
                          op("pe", lambda e, d=d, p_=p_: e.matmul(pDS[0:64, d * 256 + p_ * 128:d * 256 + (p_ + 1) * 128], lhsT=ksg[:, d * 128 + p_ * 64:d * 128 + (p_ + 1) * 64], rhs=vg[:, p_ * 128:(p_ + 1) * 128], start=True, stop=True),
                             reads=["ksg", "vg"], writes=[pDSk])
                  for p_ in range(2):
                      op("dve", lambda e, p_=p_: e.scalar_tensor_tensor(out=S_g[:, p_ * 128:(p_ + 1) * 128], in0=S_g[:, p_ * 128:(p_ + 1) * 128], scalar=gdec[:, p_:p_ + 1],
                                                                        in1=pDS[0:64, p_ * 128:(p_ + 1) * 128], op0=ALU.mult, op1=ALU.add), reads=["S_g", "gdec", pDSk], writes=["S_g"])
                  op("dve", lambda e: e.tensor_tensor(out=Sb_g[:, :].rearrange("q (p c) -> q p c", p=2), in0=S_g[:, :].rearrange("q (p c) -> q p c", p=2),
                                                      in1=C("bmg", slice(0, 64)).unsqueeze(1).to_broadcast([64, 2, 128]), op=ALU.mult), reads=["S_g", "constf"], writes=["Sb_g"])
                  op("act", lambda e: e.activation(out=stA[0:64, 768:1024], in_=pDS[0:64, 256:512], func=AF.Copy), reads=[pDSk], writes=["stA"])
                  REL(pDSk)
                  op("pool", lambda e: e.tensor_copy(out=stA[0:64, 1024:1026], in_=gdec[:, 2:4]), reads=["gdec"], writes=["stA"])
                  op("pool", lambda e: e.tensor_copy(out=stB[0:64, 1280:1536], in_=qtg[:, 256:512]), reads=["qtg"], writes=["stB"])
                  _chk("A%dc" % t)
                  pT2, pT2k = PSB()
                  for a_ in range(4):
                      srct = qr if a_ < 2 else kr
                      op("pe", lambda e, a_=a_, srct=srct: e.transpose(pT2[:, a_ * 128:(a_ + 1) * 128], srct[:, (a_ % 2) * 128:(a_ % 2 + 1) * 128], identb[:, :]), reads=["qr", "kr", "identb"], writes=[pT2k])
                  op("act", lambda e: e.activation(out=qkT[:, 0:256], in_=pT2[:, 0:256], func=AF.Copy), reads=[pT2k], writes=["qkT"])
                  for hh in range(2):
                      op("act", lambda e, hh=hh: e.activation(out=kTzv[64 * hh:64 * hh + 64, hh, :, :], in_=pT2[64 * hh:64 * hh + 64, 256:512].rearrange("q (p i) -> q p i", p=2), func=AF.Copy), reads=[pT2k], writes=["kTz"])
                  REL(pT2k)
                  op("pool", lambda e: e.tensor_copy(out=stB[:, 1024:1280], in_=qkT[:, 0:256]), reads=["qkT"], writes=["stB"])
                  for hh in range(2):
                      pSr, pSrk = PSF()
                      for p_ in range(2):
                          op("pe", lambda e, pSr=pSr, p_=p_, hh=hh: e.matmul(pSr[:, p_ * 128:(p_ + 1) * 128], lhsT=kTzv[:, hh, p_, :], rhs=qkTv[:, p_, :], start=True, stop=True), reads=["qkT", "kTz"], writes=[pSrk])
                      a0 = CF["retM"][0]
                      op("dve", lambda e, pSr=pSr, hh=hh, a0=a0: e.tensor_tensor(out=Pr[:, hh * 256:(hh + 1) * 256], in0=pSr[:, 0:256], in1=constf[:, a0 + hh * 256:a0 + (hh + 1) * 256], op=ALU.mult), reads=[pSrk, "constf"], writes=["Pr"])
                      REL(pSrk)
                  pOr, pOrk = PSF()
                  for h in range(4):
                      op("pe", lambda e, h=h: e.matmul(pOr[:, h * 64:(h + 1) * 64], lhsT=Prv[:, h % 2, h // 2, :], rhs=vr[:, h * 64:(h + 1) * 64], start=True, stop=True), reads=["Pr", "vr"], writes=[pOrk])
                  for p_ in range(2):
                      op("pe", lambda e, p_=p_: e.matmul(pOr[:, 256 + p_ * 128:256 + (p_ + 1) * 128], lhsT=qkTv[:, p_, :], rhs=Sb_r[:, p_ * 128:(p_ + 1) * 128], start=True, stop=True), reads=["qkT", "Sb_r"], writes=[pOrk])
                  op("dve", lambda e: e.tensor_tensor(out=tmpS[:, 0:256].rearrange("p (h c) -> p h c", h=4), in0=pOr[:, 256:512].rearrange("p (h c) -> p h c", h=4),
                                                      in1=C("retEQf").unsqueeze(2).to_broadcast([128, 4, 64]), op=ALU.mult), reads=[pOrk, "constf"], writes=["tmpS"])
                  op("dve", lambda e: e.tensor_tensor(out=stA[:, 256:512], in0=tmpS[:, 0:256], in1=pOr[:, 0:256], op=ALU.add), reads=["tmpS", pOrk], writes=["stA"])
                  REL(pOrk)
                  for d in range(2):
                      Wc = C("retWf") if d == 0 else C("retWb")
                      op("dve", lambda e, d=d, Wc=Wc: e.tensor_tensor(out=vtl[:, d * 256:(d + 1) * 256].rearrange("p (h c) -> p h c", h=4), in0=vr[:, :].rearrange("p (h c) -> p h c", h=4),
                                                                      in1=Wc.unsqueeze(2).to_broadcast([128, 4, 64]), op=ALU.mult), reads=["vr", "constf"], writes=["vtl"])
                  pDr, pDrk = PSF()
                  for d in range(2):
                      for p_ in range(2):
                          op("pe", lambda e, d=d, p_=p_: e.matmul(pDr[:, d * 256 + p_ * 128:d * 256 + (p_ + 1) * 128], lhsT=kr[:, p_ * 128:(p_ + 1) * 128], rhs=vtl[:, d * 256 + p_ * 128:d * 256 + (p_ + 1) * 128], start=True, stop=True),
                             reads=["kr", "vtl"], writes=[pDrk])
                  op("dve", lambda e: e.tensor_tensor(out=tmpS[:, 256:512].rearrange("p (h c) -> p h c", h=4), in0=S_r[:, :].rearrange("p (h c) -> p h c", h=4),
                                                      in1=C("retdec").unsqueeze(2).to_broadcast([128, 4, 64]), op=ALU.mult), reads=["S_r", "constf"], writes=["tmpS"])
                  op("dve", lambda e: e.tensor_tensor(out=S_r[:, :], in0=tmpS[:, 256:512], in1=pDr[:, 0:256], op=ALU.add), reads=["tmpS", pDrk], writes=["S_r"])
                  op("dve", lambda e: e.tensor_tensor(out=Sb_r[:, :].rearrange("p (a c) -> p a c", a=2), in0=S_r[:, :].rearrange("p (a c) -> p a c", a=2),
                                                      in1=C("bmr").unsqueeze(1).to_broadcast([128, 2, 128]), op=ALU.mult), reads=["S_r", "constf"], writes=["Sb_r"])
                  op("act", lambda e: e.activation(out=stA[:, 512:768], in_=pDr[:, 256:512], func=AF.Copy), reads=[pDrk], writes=["stA"])
                  REL(pDrk)
                  op("sp", lambda e, t=t: e.dma_start(out=sA_d[t, :, :], in_=stA[:, :]), reads=["stA"], writes=[("sA", t)], dma=True)
                  op("sp", lambda e, t=t: e.dma_start(out=sB_d[t, :, :], in_=stB[:, :]), reads=["stB"], writes=[("sB", t)], dma=True)
                  _chk("A%dd" % t)
                  if not first_in_seg:
                      ssd_tile(t - 1)
                  if last_in_seg:
                      ssd_tile(t)
                  _chk("A%d" % t)

              _chk("A")
              barrier()
              op("sp", lambda e, l=l: e.dma_start(out=w_out_s, in_=wb_out[l, :, :].rearrange("(k p) c -> p k c", p=128)), reads=W8("wout", l), writes=["wbig"], dma=True)
              op("sp", lambda e, l=l: e.dma_start(out=w2_s, in_=wb_2[l, :, :].rearrange("(j p) c -> p j c", p=128)), reads=W22("w2", l), writes=["wbig"], dma=True)
              for nm, tns in (("S_g", S_g), ("S_s", S_s), ("S_r", S_r), ("Sb_g", Sb_g), ("Sb_s", Sb_s), ("Sb_r", Sb_r)):
                  op("pool", lambda e, tns=tns: e.memset(tns[:, :], 0.0), writes=[nm])
              order = list(range(NCT - 1, -1, -1)) + list(range(NT - 1, NCT - 1, -1))
              def load_staging(t):
                  op("sp", lambda e, t=t: e.dma_start(out=stA[:, :], in_=sA_d[t, :, :]), reads=[("sA", t)], writes=["stA"], dma=True)
                  op("sp", lambda e, t=t: e.dma_start(out=stB[:, :], in_=sB_d[t, :, :]), reads=[("sB", t)], writes=["stB"], dma=True)
                  op("sp", lambda e, t=t: e.dma_start(out=stS[:, :], in_=sS_d[t, :, :]), reads=[("sS", t)], writes=["stS"], dma=True)
                  op("sp", lambda e, t=t: e.dma_start(out=stT[:, :], in_=sT_d[t, :, :]), reads=[("sT", t)], writes=["stT"], dma=True)
              load_staging(order[0])
              for t in order:
                  seg = 0 if t < NCT else 1
                  nxt = order[order.index(t) + 1] if order.index(t) + 1 < len(order) else None
                  op("act", lambda e: e.activation(out=silb[:, :], in_=stB[:, 0:1024], func=AF.Silu), reads=["stB"], writes=["silb"])
                  pI, pIk = PSF(); pIs, pIsk = PSF()
                  for p_ in range(2):
                      op("pe", lambda e, p_=p_: e.matmul(pI[:, p_ * 128:(p_ + 1) * 128], lhsT=stB[0:64, 1280 + p_ * 128:1280 + (p_ + 1) * 128], rhs=Sb_g[:, p_ * 128:(p_ + 1) * 128], start=True, stop=True, tile_position=(0, 0)), reads=["stB", "Sb_g"], writes=[pIk])
                      op("pe", lambda e, p_=p_: e.matmul(pI[:, 256 + p_ * 128:256 + (p_ + 1) * 128], lhsT=stB[:, 1024 + p_ * 128:1024 + (p_ + 1) * 128], rhs=Sb_r[:, p_ * 128:(p_ + 1) * 128], start=True, stop=True), reads=["stB", "Sb_r"], writes=[pIk])
                  for g in range(2):
                      op("pe", lambda e, g=g: e.matmul(pIs[:, g * 256:(g + 1) * 256], lhsT=stT[:, 512 + g * 128:512 + (g + 1) * 128], rhs=Sb_s[:, g * 256:(g + 1) * 256], start=True, stop=True), reads=["stT", "Sb_s"], writes=[pIsk])
                  op("dve", lambda e: e.tensor_tensor(out=Oall[:, 0:256], in0=stA[:, 0:256], in1=pI[:, 0:256], op=ALU.add), reads=["stA", pIk], writes=["Og"])
                  op("dve", lambda e: e.tensor_tensor(out=tmpS[:, :].rearrange("p (h c) -> p h c", h=8), in0=pIs[:, :].rearrange("p (h c) -> p h c", h=8),
                                                      in1=stS[:, 512:520].unsqueeze(2).to_broadcast([128, 8, 64]), op=ALU.mult), reads=[pIsk, "stS"], writes=["tmpS"])
                  op("dve", lambda e: e.tensor_tensor(out=Oall[:, 256:768], in0=tmpS[:, :], in1=stS[:, 0:512], op=ALU.add), reads=["tmpS", "stS"], writes=["Os"])
                  op("dve", lambda e: e.tensor_tensor(out=fsq[:, 0:256].rearrange("p (h c) -> p h c", h=4), in0=pI[:, 256:512].rearrange("p (h c) -> p h c", h=4),
                                                      in1=C("retEQb").unsqueeze(2).to_broadcast([128, 4, 64]), op=ALU.mult), reads=[pIk, "constf"], writes=["fsq"])
                  op("dve", lambda e: e.tensor_tensor(out=Oall[:, 768:1024], in0=fsq[:, 0:256], in1=stA[:, 256:512], op=ALU.add), reads=["fsq", "stA"], writes=["Or"])
                  REL(pIk, pIsk)
                  for p_ in range(2):
                      op("dve", lambda e, p_=p_: e.scalar_tensor_tensor(out=S_g[:, p_ * 128:(p_ + 1) * 128], in0=S_g[:, p_ * 128:(p_ + 1) * 128], scalar=stA[0:64, 1024 + p_:1025 + p_],
                                                                        in1=stA[0:64, 768 + p_ * 128:768 + (p_ + 1) * 128], op0=ALU.mult, op1=ALU.add), reads=["S_g", "stA", pIk], writes=["S_g"])
                  op("dve", lambda e: e.tensor_tensor(out=Sb_g[:, :].rearrange("q (p c) -> q p c", p=2), in0=S_g[:, :].rearrange("q (p c) -> q p c", p=2),
                                                      in1=C("bmg", slice(0, 64)).unsqueeze(1).to_broadcast([64, 2, 128]), op=ALU.mult), reads=["S_g", "constf"], writes=["Sb_g"])
                  op("dve", lambda e: e.tensor_tensor(out=tmpS[:, :].rearrange("p (h c) -> p h c", h=8), in0=S_s[:, :].rearrange("p (h c) -> p h c", h=8),
                                                      in1=stS[:, 520:528].unsqueeze(2).to_broadcast([128, 8, 64]), op=ALU.mult), reads=["S_s", "stS", "Os", pIsk], writes=["tmpS"])
                  op("dve", lambda e: e.tensor_tensor(out=S_s[:, :], in0=tmpS[:, :], in1=stS[:, 528:1040], op=ALU.add), reads=["tmpS", "stS"], writes=["S_s"])
                  op("act", lambda e: e.activation(out=Sb_s[:, :], in_=S_s[:, :], func=AF.Copy), reads=["S_s"], writes=["Sb_s"])
                  op("dve", lambda e: e.tensor_tensor(out=fsq[:, 256:512].rearrange("p (h c) -> p h c", h=4), in0=S_r[:, :].rearrange("p (h c) -> p h c", h=4),
                                                      in1=C("retdec").unsqueeze(2).to_broadcast([128, 4, 64]), op=ALU.mult), reads=["S_r", "constf", pIk], writes=["fsq2"])
                  op("dve", lambda e: e.tensor_tensor(out=S_r[:, :], in0=fsq[:, 256:512], in1=stA[:, 512:768], op=ALU.add), reads=["fsq2", "stA"], writes=["S_r"])
                  op("dve", lambda e: e.tensor_tensor(out=Sb_r[:, :].rearrange("p (a c) -> p a c", a=2), in0=S_r[:, :].rearrange("p (a c) -> p a c", a=2),
                                                      in1=C("bmr").unsqueeze(1).to_broadcast([128, 2, 128]), op=ALU.mult), reads=["S_r", "constf"], writes=["Sb_r"])
                  _chk("Bs%d" % t)
                  if last and seg == 0:
                      if nxt is not None:
                          load_staging(nxt)
                      continue
                  op("dve", lambda e: e.tensor_tensor(out=fsq[:, 0:256], in0=Oall[:, 0:256], in1=Oall[:, 0:256], op=ALU.mult), reads=["Og", "Or"], writes=["fsq"])
                  op("dve", lambda e: e.tensor_reduce(out=st[:, 4:8], in_=fsq[:, 0:256].rearrange("p (h c) -> p h c", h=4), axis=AX.X, op=ALU.add), reads=["fsq"], writes=["st4"])
                  rstd_from(st[:, 4:8], st[:, 8:12], 64, ["st4"], ["st8"])
                  op("dve", lambda e: e.tensor_tensor(out=fsq[:, 0:256].rearrange("p (h c) -> p h c", h=4), in0=Oall[:, 0:256].rearrange("p (h c) -> p h c", h=4),
                                                      in1=st[:, 8:12].unsqueeze(2).to_broadcast([128, 4, 64]), op=ALU.mult), reads=["Og", "st8"], writes=["fsq"])
                  op("dve", lambda e: e.tensor_tensor(out=fsq[:, 0:256], in0=fsq[:, 0:256], in1=PL("glan"), op=ALU.mult), reads=["fsq", "pl"], writes=["fsq"])
                  op("dve", lambda e: e.tensor_tensor(out=mixed[:, 0:256], in0=fsq[:, 0:256], in1=silb[:, 0:256], op=ALU.mult), reads=["fsq", "silb"], writes=["mixed"])
                  op("dve", lambda e: e.tensor_tensor(out=fsq[:, :], in0=stT[:, 0:512], in1=PL("ssdd"), op=ALU.mult), reads=["stT", "pl", "mixed"], writes=["fsq"])
                  op("dve", lambda e: e.tensor_tensor(out=fsq[:, :], in0=fsq[:, :], in1=Oall[:, 256:768], op=ALU.add), reads=["fsq", "Os"], writes=["fsq"])
                  op("dve", lambda e: e.tensor_tensor(out=fsq[:, :], in0=fsq[:, :], in1=silb[:, 512:1024], op=ALU.mult), reads=["fsq", "silb"], writes=["fsq"])
                  op("pool", lambda e: e.memset(st[:, 12:13], 0.0), writes=["st12"])
                  op("act", lambda e: e.activation(out=sil[:, :], in_=fsq[:, :], func=AF.Square, accum_out=st[:, 12:13]), reads=["fsq", "st12"], writes=["sil", "st12"])
                  rstd_from(st[:, 12:13], st[:, 13:14], 512, ["st12"], ["st13"])
                  op("dve", lambda e: e.scalar_tensor_tensor(out=mixed[:, 256:768], in0=fsq[:, :], scalar=st[:, 13:14], in1=PL("ssdn"), op0=ALU.mult, op1=ALU.mult), reads=["fsq", "st13", "pl"], writes=["mixed"])
                  op("dve", lambda e: e.tensor_reduce(out=st[:, 16:20], in_=Oall[:, 768:1024].rearrange("p (h c) -> p h c", h=4), axis=AX.X, op=ALU.add), reads=["Or"], writes=["st16"])
                  op("dve", lambda e: e.tensor_scalar(out=st[:, 16:20], in0=st[:, 16:20], scalar1=1.0 / 64, scalar2=None, op0=ALU.mult), reads=["st16"], writes=["st16"])
                  op("dve", lambda e: e.tensor_tensor(out=fsq[:, 0:256].rearrange("p (h c) -> p h c", h=4), in0=Oall[:, 768:1024].rearrange("p (h c) -> p h c", h=4),
                                                      in1=st[:, 16:20].unsqueeze(2).to_broadcast([128, 4, 64]), op=ALU.subtract), reads=["Or", "st16", "mixed"], writes=["fsq"])
                  op("dve", lambda e: e.tensor_tensor(out=fsq[:, 256:512], in0=fsq[:, 0:256], in1=fsq[:, 0:256], op=ALU.mult), reads=["fsq"], writes=["fsqb"])
                  op("dve", lambda e: e.tensor_reduce(out=st[:, 20:24], in_=fsq[:, 256:512].rearrange("p (h c) -> p h c", h=4), axis=AX.X, op=ALU.add), reads=["fsqb"], writes=["st20"])
                  rstd_from(st[:, 20:24], st[:, 24:28], 64, ["st20"], ["st24"])
                  op("dve", lambda e: e.tensor_tensor(out=fsq[:, 0:256].rearrange("p (h c) -> p h c", h=4), in0=fsq[:, 0:256].rearrange("p (h c) -> p h c", h=4),
                                                      in1=st[:, 24:28].unsqueeze(2).to_broadcast([128, 4, 64]), op=ALU.mult), reads=["fsq", "st24", "fsqb"], writes=["fsq"])
                  op("dve", lambda e: e.tensor_tensor(out=fsq[:, 0:256], in0=fsq[:, 0:256], in1=PL("retn"), op=ALU.mult), reads=["fsq", "pl"], writes=["fsq"])
                  op("dve", lambda e: e.tensor_tensor(out=mixed[:, 768:1024], in0=fsq[:, 0:256], in1=silb[:, 256:512], op=ALU.mult), reads=["fsq", "silb"], writes=["mixed"])
                  if nxt is not None:
                      load_staging(nxt)
                  pT, pTk = PSB()
                  for k in range(8):
                      op("pe", lambda e, k=k: e.transpose(pT[:, k * 128:(k + 1) * 128], mixed[:, k * 128:(k + 1) * 128], identb[:, :]), reads=["mixed", "identb"], writes=[pTk])
                  op("act", lambda e: e.activation(out=mixT[:, :], in_=pT[:, :], func=AF.Copy), reads=[pTk], writes=["mixT"])
                  REL(pTk)
                  op("sp", lambda e, t=t: e.dma_start(out=xt[:, :], in_=res_d[t * 128:(t + 1) * 128, :]), reads=[("res", t)], writes=["xt"], dma=True)

                  def resid_update(wsel, nK, lhs_of, vi, tag):
                      pys = []
                      op("pool", lambda e: e.memset(st[:, 28:30], 0.0), writes=["st28"])
                      for half in range(2):
                          py, pyk = PSF()
                          pys.append((py, pyk))
                          for k in range(nK):
                              op("pe", lambda e, py=py, k=k, half=half: e.matmul(py[:, :], lhsT=lhs_of(k), rhs=wsel[:, k, half * 512:(half + 1) * 512], start=(k == 0), stop=(k == nK - 1)),
                                 reads=["wbig", tag], writes=[pyk])
                          op("act", lambda e, py=py, half=half: e.activation(out=junk[:, half * 512:(half + 1) * 512], in_=py[:, :], func=AF.Square, accum_out=st[:, 28 + half:29 + half]),
                             reads=[pyk, "st28"], writes=["junk", "st28"])
                      op("dve", lambda e: e.tensor_tensor(out=st[:, 30:31], in0=st[:, 28:29], in1=st[:, 29:30], op=ALU.add), reads=["st28"], writes=["st30"])
                      rstd_from(st[:, 30:31], st[:, 31:32], D, ["st30"], ["st31"])
                      for half in range(2):
                          py, pyk = pys[half]
                          op("dve", lambda e, py=py, half=half: e.scalar_tensor_tensor(out=junk[:, half * 512:(half + 1) * 512], in0=py[:, :], scalar=st[:, 31:32],
                                                                                       in1=Gv[:, seg, vi, half * 512:(half + 1) * 512], op0=ALU.mult, op1=ALU.mult), reads=[pyk, "st31", "Grep", "junk"], writes=["junk"])
                          REL(pyk)
                      op("dve", lambda e: e.tensor_tensor(out=xt[:, :], in0=xt[:, :], in1=junk[:, :], op=ALU.add), reads=["xt", "junk"], writes=["xt"])

                  resid_update(w_out_s, 8, lambda k: mixTv[:, k, :], 0, "mixT")
                  pos = order.index(t); idx = pos % 2
                  op("pool", lambda e: e.memset(st[:, 0:1], 0.0), writes=["st0"])
                  op("act", lambda e: e.activation(out=junk[:, :], in_=xt[:, :], func=AF.Square, accum_out=st[:, 0:1]), reads=["xt", "st0"], writes=["junk", "st0"])
                  rstd_from(st[:, 0:1], st[:, 1:2], D, ["st0"], ["st1"])
                  op("dve", lambda e: e.tensor_scalar(out=xh[:, :], in0=xt[:, :], scalar1=st[:, 1:2], scalar2=None, op0=ALU.mult), reads=["xt", "st1"], writes=["xh"])
                  pT, pTk = PSB()
                  for k in range(8):
                      op("pe", lambda e, k=k: e.transpose(pT[:, k * 128:(k + 1) * 128], xh[:, k * 128:(k + 1) * 128], identb[:, :]), reads=["xh", "identb"], writes=[pTk])
                  for k in range(8):
                      op("act", lambda e, k=k, seg=seg, idx=idx: e.activation(out=hT2v[:, k, idx * 128:(idx + 1) * 128], in_=pT[:, k * 128:(k + 1) * 128], func=AF.Identity,
                                                                    scale=ABv[:, seg, 3, k:k + 1], bias=ABv[:, seg, 4, k:k + 1]), reads=[pTk, "AB"], writes=["hT"])
                  REL(pTk)
                  if idx == 0:
                      op("sp", lambda e, t=t: e.dma_start(out=res_d[t * 128:(t + 1) * 128, :], in_=xt[:, :]), reads=["xt"], writes=[("res", t)], dma=True)
                      _chk("B%d" % t)
                      continue
                  t_prev = order[pos - 1]
                  for j in range(NJ):
                      wg_ = wgu[j % 4]; wgk = "wgu%d" % (j % 4)
                      wgv = wg_[:, :].rearrange("p (k c) -> p k c", k=8)
                      op("sp", lambda e, j=j, wg_=wg_, l=l: e.dma_start(out=wg_[:, :], in_=wb_13[l, j, :, :]), reads=W8("w13", l), writes=[wgk], dma=True)
                      pgu, pguk = PSF()
                      for k in range(8):
                          op("pe", lambda e, pgu=pgu, wgv=wgv, k=k: e.matmul(pgu[:, 0:256], lhsT=wgv[:, k, 0:128], rhs=hT2v[:, k, :], start=(k == 0), stop=(k == 7)), reads=[wgk, "hT"], writes=[pguk])
                      for k in range(8):
                          op("pe", lambda e, pgu=pgu, wgv=wgv, k=k: e.matmul(pgu[:, 256:512], lhsT=wgv[:, k, 128:256], rhs=hT2v[:, k, :], start=(k == 0), stop=(k == 7)), reads=[wgk, "hT"], writes=[pguk])
                      op("act", lambda e, pgu=pgu: e.activation(out=sgt[:, :], in_=pgu[:, 0:256], func=AF.Silu), reads=[pguk], writes=["sgt"])
                      op("dve", lambda e, pgu=pgu, j=j: e.tensor_tensor(out=actTv[:, j, :], in0=sgt[:, 0:128], in1=pgu[:, 256:384], op=ALU.mult), reads=["sgt", pguk], writes=["actT"])
                      op("dve", lambda e, pgu=pgu, j=j: e.tensor_tensor(out=actTbv[:, j, :], in0=sgt[:, 128:256], in1=pgu[:, 384:512], op=ALU.mult), reads=["sgt", pguk], writes=["actTb"])
                      REL(pguk)
                  for which_, tt in ((1, t), (0, t_prev)):
                      if which_ == 0:
                          op("sp", lambda e, tt=tt: e.dma_start(out=xt[:, :], in_=res_d[tt * 128:(tt + 1) * 128, :]), reads=[("res", tt)], writes=["xt"], dma=True)
                          resid_update(w2_s, NJ, lambda k: actTv[:, k, :], 1, "actT")
                      else:
                          resid_update(w2_s, NJ, lambda k: actTbv[:, k, :], 1, "actTb")
                      op("sp", lambda e, tt=tt: e.dma_start(out=res_d[tt * 128:(tt + 1) * 128, :], in_=xt[:, :]), reads=["xt"], writes=[("res", tt)], dma=True)
                      if last and seg == 1:
                          fo = op("sp", lambda e, tt=tt: e.dma_start(out=out_d[(tt - NCT) * 128:(tt - NCT + 1) * 128, :], in_=xt[:, :]), reads=["xt"], writes=[("out", tt)], dma=True)
                          finals.append(fo)
                  _chk("B%d" % t)
        except _Stop:
            fo = op("sp", lambda e: e.dma_start(out=out_d[0:128, :], in_=xt[:, :]), reads=["xt"], writes=[("out", -1)], dma=True)
            finals.append(fo)
        lastop = {}
        for o_ in P.ops:
            lastop[(o_["eng"], o_["dma"])] = o_["idx"]
        for v_ in lastop.values():
            if v_ not in finals:
                finals.append(v_)
        P.emit(final_wait_ops=finals)
        global LASTP
        LASTP = P
    return nc


finals = []


def kernel(**inp):
    global finals
    finals = []
    inp = {k: np.asarray(v) for k, v in inp.items()}
    depth = DEPTH
    cm = _colmap()
    w_in = inp["w_in"]
    w_in_r = np.where(cm[None, None, :] >= 0, w_in[:, :, np.maximum(cm, 0)], np.float32(0)).astype(np.float32)
    constf = _host_consts()
    rope = _host_rope()
    pl = np.stack([_host_pl(inp, l) for l in range(depth)], 0)
    nc = build_nc(depth)
    in_maps = []
    for b in range(4):
        xin = np.concatenate([inp["ctx"][b], inp["x"][b]], 0).astype(np.float32)
        cv = np.zeros((128, 16), np.float32)
        cv[:, 0::2] = _fm(inp["c_ctx"], 8)
        cv[:, 1::2] = _fm(inp["c"][b], 8)
        in_maps.append({"xin": xin, "cvec": cv, "constf": constf, "rope": rope, "pl": pl, "cbrow": np.ascontiguousarray(inp["ssd_conv_b"][:depth, None, 0:768]).astype(np.float32), "w_in": w_in_r[:depth],
                        "w_out": inp["w_out"][:depth], "w13": inp["ffn_w13"][:depth], "w2": inp["ffn_w2"][:depth], "ada_w": inp["ada_w"][:depth]})
    res = run_bass_kernel_spmd(nc, in_maps, core_ids=[0, 1, 2, 3])
    return np.stack([np.asarray(r["out"], np.float32) for r in res.results], 0)
```

```python
import concourse.bass as bass
import concourse.mybir as mybir

F32 = mybir.dt.float32
BF16 = mybir.dt.bfloat16
AF = mybir.ActivationFunctionType
ALU = mybir.AluOpType
AX = mybir.AxisListType

ENGINES = ("sp", "act", "pool", "dve", "pe")
NDMASEM = 12


import types


def _freeze(fn):
    if fn.__closure__ is None:
        return fn
    cells = []
    for c in fn.__closure__:
        try:
            cells.append(types.CellType(c.cell_contents))
        except ValueError:
            cells.append(c)
    return types.FunctionType(fn.__code__, fn.__globals__, fn.__name__, fn.__defaults__, tuple(cells))


class Prog:
    def __init__(self, nc):
        self.nc = nc
        self.ops = []
        self.last_w = {}
        self.readers = {}

    def op(self, eng, fn, reads=(), writes=(), dma=False):
        i = len(self.ops)
        deps = set()
        raw = set()
        for r in reads:
            if r in self.last_w:
                deps.add(self.last_w[r])
                raw.add(self.last_w[r])
            if isinstance(r, str) and r.startswith("ps"):
                for q in self.readers.get(r, ()):
                    if self.ops[q]["eng"] != eng:
                        deps.add(q)
        for w in writes:
            if w in self.last_w:
                deps.add(self.last_w[w])
            for q in self.readers.get(w, ()):
                deps.add(q)
        for r in reads:
            self.readers.setdefault(r, []).append(i)
        for w in writes:
            self.last_w[w] = i
            self.readers[w] = []
        deps.discard(i)
        self.ops.append(dict(eng=eng, fn=_freeze(fn), deps=deps, dma=dma, idx=i))
        return i

    def emit(self, final_wait_ops=()):
        nc = self.nc
        ops = self.ops
        needed = set()
        for o in ops:
            for d in o["deps"]:
                do = ops[d]
                if do["eng"] == "pe" and o["eng"] == "pe" and not do["dma"]:
                    continue
                needed.add(d)
        for d in final_wait_ops:
            needed.add(d)
        cnt = {e: 0 for e in ENGINES}
        dcnt = {e: 0 for e in ENGINES}
        for o in ops:
            e = o["eng"]
            if o["dma"]:
                n = dcnt[e]
                dcnt[e] += 1
                o["dma_n"] = n
            elif o["idx"] in needed:
                cnt[e] += 1
                o["ticket"] = cnt[e]
        self.cnt = cnt
        import contextlib
        with contextlib.ExitStack() as es:
            sems = {e: es.enter_context(nc.semaphore("s_" + e)) for e in ENGINES}
            dsems = {e: [es.enter_context(nc.semaphore("d_%s_%d" % (e, k))) for k in range(NDMASEM)]
                     for e in ENGINES if dcnt[e] > 0}
            block = es.enter_context(nc.Block())
            per_eng = {e: [o for o in ops if o["eng"] == e] for e in ENGINES}

            def run(e, engobj, extra_final=False):
                waited = {}
                for o in per_eng[e]:
                    waits = []
                    for d in sorted(o["deps"]):
                        do = ops[d]
                        if do["dma"]:
                            n = do["dma_n"]
                            waits.append((dsems[do["eng"]][n % NDMASEM], 16 * (n // NDMASEM + 1), ("d", do["eng"], n % NDMASEM)))
                        else:
                            if do["eng"] == "pe" and e == "pe":
                                continue
                            waits.append((sems[do["eng"]], do["ticket"], ("c", do["eng"])))
                    if o["dma"]:
                        n = o["dma_n"]
                        if n >= NDMASEM:
                            waits.append((dsems[e][n % NDMASEM], 16 * (n // NDMASEM), ("d", e, n % NDMASEM)))
                    for sem, val, key in waits:
                        if waited.get(key, 0) >= val:
                            continue
                        waited[key] = val
                        engobj.wait_ge(sem, val)
                    ins = o["fn"](engobj)
                    if o["dma"]:
                        ins.then_inc(dsems[e][o["dma_n"] % NDMASEM], 16)
                    elif "ticket" in o:
                        ins.then_inc(sems[e], 1)
                if extra_final:
                    for qe in ENGINES:
                        for k in range(NDMASEM):
                            c = len(range(k, dcnt[qe], NDMASEM))
                            if c > 0:
                                engobj.wait_ge(dsems[qe][k], 16 * c)
                    for d in final_wait_ops:
                        do = ops[d]
                        if do["dma"]:
                            n = do["dma_n"]
                            engobj.wait_ge(dsems[do["eng"]][n % NDMASEM], 16 * (n // NDMASEM + 1))
                        else:
                            engobj.wait_ge(sems[do["eng"]], do["ticket"])

            @block.sync
            def _(eng):
                run("sp", eng, extra_final=True)

            @block.scalar
            def _(eng):
                run("act", eng)

            @block.gpsimd
            def _(eng):
                run("pool", eng)

            @block.vector
            def _(eng):
                run("dve", eng)

            @block.tensor
            def _(eng):
                run("pe", eng)

import contextlib
import numpy as np
from concourse.bass_utils import run_bass_kernel_spmd

D = 1024
SEQ = 4096
CTX = 256
NTOK = SEQ + CTX
NT = NTOK // 128
NCT = CTX // 128
DEPTH = 4
FH = 2816
NJ = FH // 128
FM_QK, FM_LR, FM_X, TM0 = 0, 256, 320, 1344
TM_A, TM_B, TM_C, TM_D, TM_E = TM0, TM0 + 512, TM0 + 1024, TM0 + 1536, TM0 + 2048
NCOLS = TM0 + 2304


def _colmap():
    m = -np.ones(NCOLS, dtype=np.int64)
    m[0:256] = np.arange(0, 256)
    m[FM_LR:FM_LR + 16] = np.arange(768, 784)
    m[FM_LR + 32:FM_LR + 48] = np.arange(784, 800)
    m[FM_X:FM_X + 1024] = np.arange(1312, 2336)
    m[TM_A:TM_A + 128] = np.arange(128, 256)
    m[TM_A + 128:TM_A + 384] = np.arange(256, 512)
    m[TM_A + 384:TM_A + 400] = np.arange(2336, 2352)
    m[TM_B:TM_B + 256] = np.arange(512, 768)
    m[TM_B + 256:TM_B + 512] = np.arange(3120, 3376)
    m[TM_C:TM_C + 512] = np.arange(800, 1312)
    m[TM_D:TM_D + 256] = np.arange(2352, 2608)
    m[TM_D + 256:TM_D + 512] = np.arange(2608, 2864)
    m[TM_E:TM_E + 256] = np.arange(2864, 3120)
    return m


class Cols:
    def __init__(self):
        self.off = {}
        self.n = 0

    def add(self, name, w):
        self.off[name] = (self.n, self.n + w)
        self.n += w

    def __getitem__(self, name):
        return self.off[name]


CF = Cols()
for _n, _w in [("eps", 1), ("lnqs", 1), ("one", 1), ("ident", 128), ("maskf", 128), ("maskb", 128),
               ("slf", 128), ("slb", 128), ("Rf", 129), ("Rb", 129), ("Lf", 128), ("Lb", 128), ("ones", 128),
               ("retM", 512), ("retEQf", 4), ("retEQb", 4), ("retWf", 4), ("retWb", 4), ("retdec", 4),
               ("bmg", 128), ("bmr", 128)]:
    CF.add(_n, _w)

PLC = Cols()
for _n, _w in [("adab", 48), ("npre", 8), ("npost", 8), ("nfpre", 8), ("nfpost", 8), ("GU", 256),
               ("glan", 256), ("ssdn", 512), ("retn", 256), ("ssdd", 512), ("convw", 40), ("convb", 8),
               ("dtb", 16), ("alog", 16)]:
    PLC.add(_n, _w)


def _host_consts():
    c = np.zeros((128, CF.n), np.float32)
    def put(name, arr):
        a, b = CF[name]
        c[:, a:b] = np.asarray(arr, np.float32).reshape(128, b - a)
    j = np.arange(128)[:, None]
    i = np.arange(128)[None, :]
    maskf = (j <= i).astype(np.float32)
    maskb = (j >= i).astype(np.float32)
    put("eps", np.full((128, 1), 1e-6))
    put("lnqs", np.full((128, 1), np.log(32.0 ** -0.5)))
    put("one", np.ones((128, 1)))
    put("ident", np.eye(128))
    put("maskf", maskf)
    put("maskb", maskb)
    put("slf", 1 - maskf)
    put("slb", 1 - maskb)
    put("Rf", np.concatenate([maskf, np.ones((128, 1))], 1) * (-1 / 16))
    put("Rb", np.concatenate([maskb, np.ones((128, 1))], 1) * (-1 / 16))
    put("Lf", (1 - maskf) * (-1 / 16))
    put("Lb", (1 - maskb) * (-1 / 16))
    put("ones", np.ones((128, 128)))
    lg = np.log1p(-np.exp2(-5.0 - np.arange(4, dtype=np.float32))).astype(np.float32).astype(np.float64)
    M = np.zeros((128, 4, 128))
    for h in range(4):
        M[:, (h % 2) * 2 + h // 2, :] = np.exp(lg[h] * np.abs(i - j)) * np.where(i == j, 2.0, 1.0)
    put("retM", M)
    tt = np.arange(128)[:, None].astype(np.float64)
    put("retEQf", np.exp(lg[None, :] * (tt + 1)))
    put("retEQb", np.exp(lg[None, :] * (128 - tt)))
    put("retWf", np.exp(lg[None, :] * (127 - tt)))
    put("retWb", np.exp(lg[None, :] * tt))
    put("retdec", np.tile(np.exp(lg * 128)[None, :], (128, 1)))
    p = np.arange(128)[:, None]
    cc = np.arange(128)[None, :]
    put("bmg", ((p // 32) == (cc // 64)).astype(np.float32))
    put("bmr", ((p // 64) == (cc // 64)).astype(np.float32))
    return c


def _host_rope():
    rows = SEQ // 64
    row = np.repeat(np.arange(rows), 64).astype(np.float32)
    col = np.tile(np.arange(64), rows).astype(np.float32)
    inv = (np.float32(10000.0) ** (-np.arange(16, dtype=np.float32) / np.float32(16))).astype(np.float32)
    ang = np.concatenate([row[:, None] * inv, col[:, None] * inv], -1).astype(np.float32)
    cos, sin = np.cos(ang).astype(np.float32), np.sin(ang).astype(np.float32)
    t = np.zeros((SEQ, 256), np.float32)
    t[:, 0:64] = np.concatenate([cos, cos], 1) * 0.125
    t[:, 64:128] = np.concatenate([-sin, sin], 1) * 0.125
    t[:, 128:192] = np.concatenate([cos, cos], 1)
    t[:, 192:256] = np.concatenate([-sin, sin], 1)
    return t


def _fm(v, n):
    return np.asarray(v, np.float32).reshape(n, 128).T


def _host_pl(inp, l):
    a = np.zeros((128, PLC.n), np.float32)
    def put(name, arr):
        s, e = PLC[name]
        a[:, s:e] = np.asarray(arr, np.float32).reshape(128, e - s)
    put("adab", _fm(inp["ada_b"][l], 48))
    put("npre", _fm(inp["norm_mix_pre"][l], 8))
    put("npost", _fm(inp["norm_mix_post"][l], 8))
    put("nfpre", _fm(inp["norm_ffn_pre"][l], 8))
    put("nfpost", _fm(inp["norm_ffn_post"][l], 8))
    gu = np.zeros((128, 256), np.float32)
    gu[0:16, 0:128] = inp["gla_gate_up"][l][0]
    gu[16, 0:128] = inp["gla_gate_b"][l][0]
    gu[32:48, 128:256] = inp["gla_gate_up"][l][1]
    gu[48, 128:256] = inp["gla_gate_b"][l][1]
    put("GU", gu)
    rep = lambda v: np.tile(np.asarray(v, np.float32)[None, :], (128, 1))
    put("glan", rep(inp["gla_norm"][l]))
    put("ssdn", rep(inp["ssd_norm"][l]))
    put("retn", rep(inp["ret_norm"][l]))
    put("ssdd", rep(np.repeat(inp["ssd_d"][l], 64)))
    cw = inp["ssd_conv_w"][l]
    put("convw", cw.reshape(5, 8, 128).transpose(2, 1, 0).reshape(128, 40))
    put("convb", _fm(inp["ssd_conv_b"][l], 8))
    put("dtb", rep(inp["ssd_dt_bias"][l].reshape(16)))
    put("alog", rep(inp["ssd_a_log"][l].reshape(16)))
    return a


STOP = None


class _Stop(Exception):
    pass


def _chk(stage):
    if STOP is not None and STOP == stage:
        raise _Stop()


def build_nc(depth=DEPTH, debug_layers=None):
    nc = bass.Bass("TRN2", target_bir_lowering=False)
    es = contextlib.ExitStack()
    with es:
        def din(name, shape, dt=F32):
            return nc.dram_tensor(name, shape, dt, kind="ExternalInput").ap()
        def dscr(name, shape, dt=F32):
            return nc.dram_tensor(name, shape, dt, kind="Internal").ap()
        xin = din("xin", [NTOK, D])
        cvec = din("cvec", [128, 16])
        constf_d = din("constf", [128, CF.n])
        rope_d = din("rope", [SEQ, 256])
        pl_d = din("pl", [depth, 128, PLC.n])
        cbrow_d = din("cbrow", [depth, 1, 768])
        w_in_d = din("w_in", [depth, D, NCOLS])
        w_out_d = din("w_out", [depth, D, D])
        w13_d = din("w13", [depth, D, 2 * FH])
        w2_d = din("w2", [depth, FH, D])
        ada_d = din("ada_w", [depth, D, 6 * D])
        out_d = nc.dram_tensor("out", [SEQ, D], F32, kind="ExternalOutput").ap()
        res_d = dscr("res", [NTOK, D])
        wb_in = dscr("wb_in", [depth, D, NCOLS], BF16)
        wb_out = dscr("wb_out", [depth, D, D], BF16)
        wb_13 = dscr("wb_13", [depth, NJ, 128, 8 * 256], BF16)
        wb_2 = dscr("wb_2", [depth, FH, D], BF16)
        wb_ada = dscr("wb_ada", [depth, D, 6 * D], BF16)
        sA_d = dscr("sA", [NT, 128, 1026])
        sB_d = dscr("sB", [NT, 128, 1536], BF16)
        sS_d = dscr("sS", [NT, 128, 1040])
        sT_d = dscr("sT", [NT, 128, 768], BF16)
        gscr_d = dscr("gscr", [4, 8, 256])

        def sb(name, shape, dt=F32):
            return es.enter_context(nc.sbuf_tensor("sb_" + name, shape, dt))
        P = Prog(nc)
        constf = sb("constf", [128, CF.n])
        def C(name, rows=slice(0, 128)):
            a, b = CF[name]
            return constf[rows, a:b]
        identb = sb("identb", [128, 128], BF16)
        onesb = sb("onesb", [128, 128], BF16)
        pl = sb("pl", [128, PLC.n])
        def PL(name, rows=slice(0, 128)):
            a, b = PLC[name]
            return pl[rows, a:b]
        GUb = sb("GUb", [64, 256], BF16)
        cbrowb = sb("cbrowb", [64, 768], BF16)
        Arep = sb("Arep", [128, 16])
        cdiag = sb("cdiag", [128, 8 * 5 * 128], BF16)
        cdv = cdiag[:, :].rearrange("p (a k c) -> p a k c", a=8, k=5)
        rope_t = sb("rope_t", [128, 256])
        csil = sb("csil", [128, 16], BF16)
        cve = sb("cve", [128, 16])
        modT = sb("modT", [128, 96])
        modv = modT[:, :].rearrange("p (c s) -> p c s", s=2)
        AB = sb("AB", [128, 2 * 6 * 8])
        ABv = AB[:, :].rearrange("p (s v k) -> p s v k", s=2, v=6)
        Grep_ = sb("Grep_", [128, 2 * 2 * 1024])
        Gv = Grep_[:, :].rearrange("p (s v c) -> p s v c", s=2, v=2)
        dg = sb("dg", [128, 128])
        wbig = sb("wbig", [128, 30720], BF16)
        w_in_s = wbig[:, 0:8 * NCOLS].rearrange("p (k c) -> p k c", k=8)
        w_out_s = wbig[:, 0:8192].rearrange("p (k c) -> p k c", k=8)
        w2_s = wbig[:, 8192:8192 + NJ * 1024].rearrange("p (j c) -> p j c", j=NJ)
        arF = sb("arF", [128, 2560])
        arH = sb("arH", [128, 10336], BF16)
        adaw = sb("adaw", [128, 8 * 512], BF16)
        adawv = adaw[:, :].rearrange("p (k c) -> p k c", k=8)
        wgu = [arH[:, 4864 + i * 2048:4864 + (i + 1) * 2048] for i in range(2)] + [adaw[:, i * 2048:(i + 1) * 2048] for i in range(2)]
        xt = sb("xt", [128, D])
        junk = sb("junk", [128, D])
        xh = sb("xh", [128, D], BF16)
        hT = sb("hT", [128, 2 * D], BF16)
        hTv = hT[:, 0:D].rearrange("p (k t) -> p k t", k=8)
        hT2v = hT[:, :].rearrange("p (k t) -> p k t", k=8)
        st = sb("st", [128, 32])
        lrT = sb("lrT", [64, 128], BF16)
        xraw = [arH[:, 7168 + i * 1056:7168 + (i + 1) * 1056] for i in range(3)]
        xrv = [x_[:, :].rearrange("p (a t) -> p a t", a=8) for x_ in xraw]
        dtraw = [sb("dtraw%d" % i, [128, 16]) for i in range(2)]
        vg = sb("vg", [128, 256], BF16)
        ez = sb("ez", [128, 256])
        spt = sb("spt", [128, 256])
        EG = sb("EG", [64, 2 * 2 * 128])
        EGv = EG[:, :].rearrange("q (d p i) -> q d p i", d=2, p=2)
        EGn = sb("EGn", [64, 512])
        EGnv = EGn[:, :].rearrange("q (d p i) -> q d p i", d=2, p=2)
        gdec = sb("gdec", [64, 4])
        ED = sb("ED", [128, 256])
        qtg = sb("qtg", [64, 512], BF16)
        qtgv = qtg[:, :].rearrange("q (d p i) -> q d p i", d=2, p=2)
        ktg = sb("ktg", [64, 1024], BF16)
        ktgv = ktg[:, :].rearrange("q (d a p i) -> q d a p i", d=2, a=2, p=2)
        kTz = sb("kTz", [128, 512], BF16)
        kTzv = kTz[:, :].rearrange("q (a p i) -> q a p i", a=2, p=2)
        ksg = sb("ksg", [128, 256], BF16)
        Pg = arH[:, 6144:7168]
        Pgv = Pg[:, :].rearrange("p (d a b i) -> p d a b i", d=2, a=2, b=2)
        S_g = sb("S_g", [64, 256]); Sb_g = sb("Sb_g", [64, 256], BF16)
        S_s = sb("S_s", [128, 512]); Sb_s = sb("Sb_s", [128, 512], BF16)
        S_r = sb("S_r", [128, 256]); Sb_r = sb("Sb_r", [128, 256], BF16)
        tmpS = sb("tmpS", [128, 512])
        stA = sb("stA", [128, 1026]); stB = sb("stB", [128, 1536], BF16)
        stS = sb("stS", [128, 1040]); stT = sb("stT", [128, 768], BF16)
        ropetmp = sb("ropetmp", [128, 512])
        qr = sb("qr", [128, 256], BF16); kr = sb("kr", [128, 256], BF16); vr = sb("vr", [128, 256], BF16)
        qkT = sb("qkT", [128, 512], BF16)
        qkTv = qkT[:, :].rearrange("p (a t) -> p a t", a=4)
        Pr = sb("Pr", [128, 512], BF16)
        Prv = Pr[:, :].rearrange("p (a b i) -> p a b i", a=2, b=2)
        vtl = sb("vtl", [128, 512], BF16)
        xs = sb("xs", [128, 512], BF16); Btok = sb("Btok", [128, 256], BF16)
        BCT = sb("BCT", [128, 512], BF16)
        BCTv = BCT[:, :].rearrange("p (a t) -> p a t", a=4)
        dte = sb("dte", [128, 16]); dtv = sb("dtv", [128, 16]); lgt = sb("lgt", [128, 16])
        negG = sb("negG", [128, 16]); wexp = sb("wexp", [128, 16]); decrep = sb("decrep", [128, 16]); EGi = sb("EGi", [128, 16])
        gts = sb("gts", [8, 256])
        Grp = arF[:, 0:2048]
        Grpv = Grp[:, :].rearrange("p (h d i) -> p h d i", h=8, d=2)
        Lm = arH[:, 0:2048]
        Lmv = Lm[:, :].rearrange("p (d h i) -> p d h i", d=2, h=8)
        CBm = arF[:, 2048:2560]
        CBmv = CBm[:, :].rearrange("p (d g i) -> p d g i", d=2, g=2)
        Ps = arH[:, 2048:4096]
        Psv = Ps[:, :].rearrange("p (d h i) -> p d h i", d=2, h=8)
        vts = arH[:, 4096:5120]
        xss = arH[:, 5120:6144]
        Oall = arF[:, 0:1024]
        mixed = arH[:, 0:1024]
        mixT = arH[:, 1024:2048]
        mixTv = mixT[:, :].rearrange("p (k t) -> p k t", k=8)
        fsq = arF[:, 1024:1536]
        sil = arF[:, 1536:2048]
        actT = arH[:, 2048:2048 + NJ * 128]
        actTv = actT[:, :].rearrange("p (j t) -> p j t", j=NJ)
        sgt = arF[:, 2048:2304]
        silb = arH[:, 8960:9984]
        actTb = cdiag[:, 0:NJ * 128]
        actTbv = actTb.rearrange("p (j t) -> p j t", j=NJ)
        psf = [es.enter_context(nc.psum_tensor("psf%d" % i, [128, 512], F32)) for i in range(6)]
        psb = [es.enter_context(nc.psum_tensor("psb%d" % i, [128, 1024], BF16)) for i in range(2)]
        free_f = list(range(6)); free_b = list(range(2))
        def PSF():
            i = free_f.pop(0)
            return psf[i], "psf%d" % i
        def PSB():
            i = free_b.pop(0)
            return psb[i], "psb%d" % i
        def REL(*keys):
            for key in keys:
                (free_f if key.startswith("psf") else free_b).append(int(key[3:]))

        op = P.op
        def bc(ap, shape):
            return ap.to_broadcast(shape)

        ARKEYS = ["Grp", "CBm", "Lm", "Ps", "vts", "xss", "Pg", "xraw0", "xraw1", "xraw2", "Og", "Os", "Or", "fsq", "fsqb", "fsq2",
                  "sil", "sgt", "mixed", "mixT", "actT", "wgu0", "wgu1", "wgu0u", "wgu1u", "wgu2", "wgu3", "wgu2u", "wgu3u", "adaw", "cdiag", "actTb", "silb"]
        bard = sb("bard", [128, 1])
        def barrier():
            op("pool", lambda e: e.memset(bard[:, :], 0.0), reads=[], writes=ARKEYS + ["bard"])
        op("sp", lambda e: e.dma_start(out=constf[:, :], in_=constf_d[:, :]), writes=["constf"], dma=True)
        op("sp", lambda e: e.dma_start(out=cve[:, :], in_=cvec[:, :]), writes=["cve"], dma=True)
        op("dve", lambda e: e.tensor_copy(out=identb[:, :], in_=C("ident")), reads=["constf"], writes=["identb"])
        op("dve", lambda e: e.tensor_copy(out=onesb[:, :], in_=C("ones")), reads=["constf"], writes=["onesb"])
        op("act", lambda e: e.activation(out=csil[:, :], in_=cve[:, :], func=AF.Silu), reads=["cve"], writes=["csil"])
        for nm_, tn_ in (("stA", stA), ("stB", stB), ("stS", stS), ("stT", stT)):
            op("pool", lambda e, tn_=tn_: e.memset(tn_[:, :], 0.0), writes=[nm_])
        for t in range(NT):
            op("sp", lambda e, t=t: e.dma_start(out=res_d[t * 128:(t + 1) * 128, :], in_=xin[t * 128:(t + 1) * 128, :]),
               writes=[("res", t)], dma=True)
        def cast(dst, src, rows, key, l, piece=128):
            for r0 in range(0, rows, piece):
                op("pool", lambda e, r0=r0: e.dma_start(out=dst[l, r0:r0 + piece, :], in_=src[l, r0:r0 + piece, :]),
                   writes=[(key, l, r0 // piece)], dma=True)
        for l in range(depth):
            cast(wb_ada, ada_d, D, "wada", l)
            cast(wb_in, w_in_d, D, "win", l)
            cast(wb_out, w_out_d, D, "wout", l)
            for k_ in range(8):
                for part in range(2):
                    op("pool", lambda e, l=l, k_=k_, part=part: e.dma_start(
                        out=wb_13[l, :, :, k_ * 256 + part * 128:k_ * 256 + (part + 1) * 128],
                        in_=w13_d[l, k_ * 128:(k_ + 1) * 128, part * FH:(part + 1) * FH].rearrange("p (j c) -> j p c", c=128)),
                       writes=[("w13", l, k_)], dma=True)
            cast(wb_2, w2_d, FH, "w2", l)
        W8 = lambda key, l: [(key, l, i) for i in range(8)]
        W22 = lambda key, l: [(key, l, i) for i in range(22)]

        def rstd_from(ss_ap, out_ap, n, reads, writes):
            rows = slice(0, 128)
            op("act", lambda e: e.activation(out=out_ap, in_=ss_ap, func=AF.Ln, scale=1.0 / n, bias=C("eps")),
               reads=list(reads) + ["constf"], writes=writes)
            op("act", lambda e: e.activation(out=out_ap, in_=out_ap, func=AF.Exp, scale=-0.5),
               reads=writes, writes=writes)

        try:
          _chk("prologue")
          for l in range(depth):
              last = (l == depth - 1)
              barrier()
              op("sp", lambda e, l=l: e.dma_start(out=pl[:, :], in_=pl_d[l, :, :]), writes=["pl"], dma=True)
              pm, pmk = PSF()
              for piece in range(12):
                  op("sp", lambda e, l=l, piece=piece: e.dma_start(
                      out=adawv, in_=wb_ada[l, :, piece * 512:(piece + 1) * 512].rearrange("(k p) c -> p k c", p=128)),
                     reads=W8("wada", l), writes=["adaw"], dma=True)
                  for cc in range(4):
                      ch = piece * 4 + cc
                      for k in range(8):
                          op("pe", lambda e, ch=ch, cc=cc, k=k: e.matmul(pm[:, ch * 2:ch * 2 + 2], lhsT=adawv[:, k, cc * 128:(cc + 1) * 128],
                                                                           rhs=csil[:, 2 * k:2 * k + 2],
                                                                           start=(k == 0), stop=(k == 7)),
                             reads=["adaw", "csil"], writes=[pmk])
              op("dve", lambda e: e.tensor_tensor(out=modv, in0=pm[:, 0:96].rearrange("p (c s) -> p c s", s=2),
                                                  in1=PL("adab").unsqueeze(2).to_broadcast([128, 48, 2]), op=ALU.add),
                 reads=[pmk, "pl"], writes=["modT"])
              REL(pmk)
              for s in range(2):
                  def mv(c0, s=s):
                      return modv[:, c0:c0 + 8, s]
                  op("dve", lambda e, s=s, mv=mv: e.scalar_tensor_tensor(out=ABv[:, s, 0, :], in0=mv(8), scalar=1.0, in1=PL("npre"), op0=ALU.add, op1=ALU.mult),
                     reads=["modT", "pl"], writes=["AB"])
                  op("dve", lambda e, s=s, mv=mv: e.tensor_copy(out=ABv[:, s, 1, :], in_=mv(0)), reads=["modT"], writes=["AB"])
                  op("dve", lambda e, s=s, mv=mv: e.tensor_tensor(out=ABv[:, s, 2, :], in0=mv(16), in1=PL("npost"), op=ALU.mult), reads=["modT", "pl"], writes=["AB"])
                  op("dve", lambda e, s=s, mv=mv: e.scalar_tensor_tensor(out=ABv[:, s, 3, :], in0=mv(32), scalar=1.0, in1=PL("nfpre"), op0=ALU.add, op1=ALU.mult),
                     reads=["modT", "pl"], writes=["AB"])
                  op("dve", lambda e, s=s, mv=mv: e.tensor_copy(out=ABv[:, s, 4, :], in_=mv(24)), reads=["modT"], writes=["AB"])
                  op("dve", lambda e, s=s, mv=mv: e.tensor_tensor(out=ABv[:, s, 5, :], in0=mv(40), in1=PL("nfpost"), op=ALU.mult), reads=["modT", "pl"], writes=["AB"])
                  for vi, vsrc in enumerate((2, 5)):
                      for half in range(2):
                          pg_, pgk = PSF()
                          for c4 in range(4):
                              k = half * 4 + c4
                              op("dve", lambda e, s=s, vsrc=vsrc, k=k: e.tensor_scalar(out=dg[:, :], in0=C("ident"), scalar1=ABv[:, s, vsrc, k:k + 1], scalar2=None, op0=ALU.mult),
                                 reads=["AB", "constf"], writes=["dg"])
                              op("pe", lambda e, pg_=pg_, c4=c4: e.matmul(pg_[:, c4 * 128:(c4 + 1) * 128], lhsT=C("ones"), rhs=dg[:, :], start=True, stop=True),
                                 reads=["dg", "constf"], writes=[pgk])
                          op("act", lambda e, s=s, vi=vi, half=half, pg_=pg_: e.activation(out=Gv[:, s, vi, half * 512:(half + 1) * 512], in_=pg_[:, :], func=AF.Copy),
                             reads=[pgk], writes=["Grep"])
                          REL(pgk)
              _chk("mod")
              op("dve", lambda e: e.tensor_copy(out=GUb[:, :], in_=PL("GU", slice(0, 64))), reads=["pl"], writes=["GUb"])
              op("pool", lambda e: e.memset(cbrowb[:, :], 0.0), writes=["cbrowb"])
              op("pool", lambda e, l=l: e.dma_start(out=cbrowb[0:1, :], in_=cbrow_d[l, :, :]), writes=["cbrowb"], dma=True)
              op("act", lambda e: e.activation(out=Arep[:, :], in_=PL("alog"), func=AF.Exp), reads=["pl"], writes=["Arep"])
              op("dve", lambda e: e.tensor_scalar(out=Arep[:, :], in0=Arep[:, :], scalar1=-1.0, scalar2=None, op0=ALU.mult), reads=["Arep"], writes=["Arep"])
              cwv = PL("convw").rearrange("p (a k) -> p a k", a=8)
              for ct in range(8):
                  for k in range(5):
                      op("dve", lambda e, ct=ct, k=k: e.tensor_scalar(out=cdv[:, ct, k, :], in0=C("ident"), scalar1=cwv[:, ct, k:k + 1], scalar2=None, op0=ALU.mult),
                         reads=["pl", "constf"], writes=["cdiag"])
              op("sp", lambda e, l=l: e.dma_start(out=w_in_s, in_=wb_in[l, :, :].rearrange("(k p) c -> p k c", p=128)),
                 reads=W8("win", l), writes=["wbig"], dma=True)
              op("pool", lambda e: e.memset(lrT[:, :], 1.0), writes=["lrT"])
              op("pool", lambda e: e.memset(ktg[:, :], 0.0), writes=["ktg"])
              op("pool", lambda e: e.memset(kTz[:, :], 0.0), writes=["kTz"])
              for nm, tns in (("S_g", S_g), ("S_s", S_s), ("S_r", S_r), ("Sb_g", Sb_g), ("Sb_s", Sb_s), ("Sb_r", Sb_r)):
                  op("pool", lambda e, tns=tns: e.memset(tns[:, :], 0.0), writes=[nm])

              _chk("derived")
              barrier()
              def ssd_tile(t):
                  seg = 0 if t < NCT else 1
                  u = xrv[t % 3]
                  uk = "xraw%d" % (t % 3)
                  px, pxk = PSF(); pB, pBk = PSF(); pBC, pBCk = PSF()
                  for ct in range(6):
                      o_ = px[:, ct * 128:(ct + 1) * 128] if ct < 4 else pB[:, (ct - 4) * 128:(ct - 3) * 128]
                      ok_ = pxk if ct < 4 else pBk
                      for k in range(5):
                          op("pe", lambda e, o_=o_, ct=ct, k=k: e.matmul(o_, lhsT=u[:, ct, k:k + 128], rhs=cdv[:, ct, k, :], start=(k == 0), stop=False),
                             reads=[uk, "cdiag"], writes=[ok_])
                      op("pe", lambda e, o_=o_, ct=ct: e.matmul(o_, lhsT=onesb[0:64, :], rhs=cbrowb[0:64, ct * 128:(ct + 1) * 128], start=False, stop=True, tile_position=(0, 0)),
                         reads=["onesb", "cbrowb"], writes=[ok_])
                  for idx, ct in enumerate((4, 5, 6, 7)):
                      for k in range(5):
                          op("pe", lambda e, idx=idx, ct=ct, k=k: e.matmul(pBC[:, idx * 128:(idx + 1) * 128], lhsT=cdv[:, ct, k, :], rhs=u[:, ct, k:k + 128], start=(k == 0), stop=(k == 4)),
                             reads=[uk, "cdiag"], writes=[pBCk])
                  op("act", lambda e: e.activation(out=xs[:, :], in_=px[:, :], func=AF.Silu), reads=[pxk], writes=["xs"])
                  op("act", lambda e: e.activation(out=Btok[:, :], in_=pB[:, 0:256], func=AF.Silu), reads=[pBk], writes=["Btok"])
                  for idx, ct in enumerate((4, 5, 6, 7)):
                      a0 = PLC["convb"][0]
                      op("act", lambda e, idx=idx, ct=ct, a0=a0: e.activation(out=BCTv[:, idx, :], in_=pBC[:, idx * 128:(idx + 1) * 128], func=AF.Silu, bias=pl[:, a0 + ct:a0 + ct + 1]),
                         reads=[pBCk, "pl"], writes=["BCT"])
                  REL(pxk, pBk, pBCk)
                  op("pool", lambda e: e.tensor_copy(out=stT[:, 0:512], in_=xs[:, :]), reads=["xs"], writes=["stT"])
                  op("pool", lambda e: e.tensor_copy(out=stT[:, 512:768], in_=BCT[:, 256:512]), reads=["BCT"], writes=["stT"])
                  _chk("S%da" % t)
                  dr = dtraw[t % 2]; drk = "dtraw%d" % (t % 2)
                  op("act", lambda e: e.activation(out=dte[:, :], in_=dr[:, :], func=AF.Exp), reads=[drk], writes=["dte"])
                  op("act", lambda e: e.activation(out=dtv[:, :], in_=dte[:, :], func=AF.Ln, bias=C("one")), reads=["dte", "constf"], writes=["dtv"])
                  op("dve", lambda e: e.tensor_tensor(out=lgt[:, :], in0=dtv[:, :], in1=Arep[:, :], op=ALU.mult), reads=["dtv", "Arep"], writes=["lgt"])
                  pg2, pg2k = PSF()
                  for d in range(2):
                      U = C("maskf") if d == 0 else C("maskb")
                      SLm = C("slf") if d == 0 else C("slb")
                      op("pe", lambda e, d=d, U=U: e.matmul(pg2[:, d * 8:(d + 1) * 8], lhsT=U, rhs=lgt[:, d * 8:(d + 1) * 8], start=True, stop=True), reads=["lgt", "constf"], writes=[pg2k])
                      op("pe", lambda e, d=d, SLm=SLm: e.matmul(pg2[:, 16 + d * 8:16 + (d + 1) * 8], lhsT=SLm, rhs=lgt[:, d * 8:(d + 1) * 8], start=True, stop=True), reads=["lgt", "constf"], writes=[pg2k])
                      op("pe", lambda e, d=d, U=U: e.matmul(pg2[0:8, 64 + d * 128:64 + (d + 1) * 128], lhsT=lgt[:, d * 8:(d + 1) * 8], rhs=U, start=True, stop=True), reads=["lgt", "constf"], writes=[pg2k])
                  op("pe", lambda e: e.matmul(pg2[:, 32:48], lhsT=C("ones"), rhs=lgt[:, :], start=True, stop=True), reads=["lgt", "constf"], writes=[pg2k])
                  op("dve", lambda e: e.tensor_scalar(out=negG[:, :], in0=pg2[:, 0:16], scalar1=-1.0, scalar2=None, op0=ALU.mult), reads=[pg2k], writes=["negG"])
                  op("act", lambda e: e.activation(out=EGi[:, :], in_=pg2[:, 0:16], func=AF.Exp), reads=[pg2k], writes=["EGi"])
                  op("act", lambda e: e.activation(out=wexp[:, :], in_=pg2[:, 16:32], func=AF.Exp), reads=[pg2k], writes=["wexp"])
                  op("act", lambda e: e.activation(out=decrep[:, :], in_=pg2[:, 32:48], func=AF.Exp), reads=[pg2k], writes=["decrep"])
                  op("dve", lambda e: e.tensor_copy(out=gts[:, :], in_=pg2[0:8, 64:320]), reads=[pg2k], writes=["gts"])
                  REL(pg2k)
                  sl = t % 4
                  op("sp", lambda e, sl=sl: e.dma_start(out=gscr_d[sl, :, :], in_=gts[:, :]), reads=["gts"], writes=[("gscr", sl)], dma=True)
                  op("sp", lambda e, sl=sl: e.dma_start(out=Grp[:, :], in_=gscr_d[sl:sl + 1, :, :].rearrange("o h c -> o (h c)").to_broadcast([128, 2048])),
                     reads=[("gscr", sl)], writes=["Grp"], dma=True)
                  for d in range(2):
                      for h in range(8):
                          op("dve", lambda e, d=d, h=h: e.tensor_scalar(out=Grpv[:, h, d, :], in0=Grpv[:, h, d, :], scalar1=negG[:, d * 8 + h:d * 8 + h + 1], scalar2=0.0, op0=ALU.add, op1=ALU.min),
                             reads=["Grp", "negG"], writes=["Grp"])
                          op("act", lambda e, d=d, h=h: e.activation(out=Lmv[:, d, h, :], in_=Grpv[:, h, d, :], func=AF.Exp),
                             reads=["Grp"], writes=["Lm"])
                  _chk("S%db" % t)
                  pcb, pcbk = PSF()
                  for g in range(2):
                      op("pe", lambda e, g=g: e.matmul(pcb[:, g * 128:(g + 1) * 128], lhsT=BCTv[:, g, :], rhs=BCTv[:, 2 + g, :], start=True, stop=True), reads=["BCT"], writes=[pcbk])
                  for d in range(2):
                      mk = C("maskf") if d == 0 else C("maskb")
                      op("dve", lambda e, d=d, mk=mk: e.tensor_tensor(out=CBmv[:, d, :, :], in0=pcb[:, 0:256].rearrange("p (g i) -> p g i", g=2),
                                                                      in1=mk.unsqueeze(1).to_broadcast([128, 2, 128]), op=ALU.mult), reads=[pcbk, "constf"], writes=["CBm"])
                  REL(pcbk)
                  for d in range(2):
                      for g in range(2):
                          op("dve", lambda e, d=d, g=g: e.scalar_tensor_tensor(out=Psv[:, d, 4 * g:4 * g + 4, :], in0=Lmv[:, d, 4 * g:4 * g + 4, :], scalar=1.0,
                                                                               in1=CBmv[:, d, g, :].unsqueeze(1).to_broadcast([128, 4, 128]), op0=ALU.min, op1=ALU.mult),
                             reads=["Lm", "CBm"], writes=["Ps"])
                      op("dve", lambda e, d=d: e.tensor_tensor(out=vts[:, d * 512:(d + 1) * 512].rearrange("p (h c) -> p h c", h=8), in0=xs[:, :].rearrange("p (h c) -> p h c", h=8),
                                                               in1=dtv[:, d * 8:(d + 1) * 8].unsqueeze(2).to_broadcast([128, 8, 64]), op=ALU.mult), reads=["xs", "dtv"], writes=["vts"])
                      op("dve", lambda e, d=d: e.tensor_tensor(out=xss[:, d * 512:(d + 1) * 512].rearrange("p (h c) -> p h c", h=8), in0=vts[:, d * 512:(d + 1) * 512].rearrange("p (h c) -> p h c", h=8),
                                                               in1=wexp[:, d * 8:(d + 1) * 8].unsqueeze(2).to_broadcast([128, 8, 64]), op=ALU.mult), reads=["vts", "wexp"], writes=["xss"])
                  _chk("S%dc" % t)
                  po, pok = PSF(); pi, pik = PSF()
                  for h in range(8):
                      for d in range(2):
                          op("pe", lambda e, h=h, d=d: e.matmul(po[:, h * 64:(h + 1) * 64], lhsT=Psv[:, d, h, :], rhs=vts[:, d * 512 + h * 64:d * 512 + (h + 1) * 64], start=(d == 0), stop=(d == 1)),
                             reads=["Ps", "vts"], writes=[pok])
                  for g in range(2):
                      op("pe", lambda e, g=g: e.matmul(pi[:, g * 256:(g + 1) * 256], lhsT=BCTv[:, 2 + g, :], rhs=Sb_s[:, g * 256:(g + 1) * 256], start=True, stop=True), reads=["BCT", "Sb_s"], writes=[pik])
                  op("dve", lambda e: e.tensor_tensor(out=tmpS[:, :].rearrange("p (h c) -> p h c", h=8), in0=pi[:, :].rearrange("p (h c) -> p h c", h=8),
                                                      in1=EGi[:, 0:8].unsqueeze(2).to_broadcast([128, 8, 64]), op=ALU.mult), reads=[pik, "EGi"], writes=["tmpS"])
                  op("dve", lambda e: e.tensor_tensor(out=stS[:, 0:512], in0=tmpS[:, :], in1=po[:, :], op=ALU.add), reads=["tmpS", pok], writes=["stS"])
                  REL(pok, pik)
                  op("pool", lambda e: e.tensor_copy(out=stS[:, 512:520], in_=EGi[:, 8:16]), reads=["EGi"], writes=["stS"])
                  op("pool", lambda e: e.tensor_copy(out=stS[:, 520:528], in_=decrep[:, 8:16]), reads=["decrep"], writes=["stS"])
                  pds = []
                  for d in range(2):
                      pd_, pdk = PSF()
                      pds.append((pd_, pdk))
                      for g in range(2):
                          op("pe", lambda e, d=d, g=g, pd_=pd_: e.matmul(pd_[:, g * 256:(g + 1) * 256], lhsT=Btok[:, g * 128:(g + 1) * 128], rhs=xss[:, d * 512 + g * 256:d * 512 + (g + 1) * 256], start=True, stop=True),
                             reads=["Btok", "xss"], writes=[pdk])
                  op("act", lambda e: e.activation(out=stS[:, 528:1040], in_=pds[1][0][:, :], func=AF.Copy), reads=[pds[1][1]], writes=["stS"])
                  op("dve", lambda e: e.tensor_tensor(out=tmpS[:, :].rearrange("p (h c) -> p h c", h=8), in0=S_s[:, :].rearrange("p (h c) -> p h c", h=8),
                                                      in1=decrep[:, 0:8].unsqueeze(2).to_broadcast([128, 8, 64]), op=ALU.mult), reads=["S_s", "decrep"], writes=["tmpS"])
                  op("dve", lambda e: e.tensor_tensor(out=S_s[:, :], in0=tmpS[:, :], in1=pds[0][0][:, :], op=ALU.add), reads=["tmpS", pds[0][1]], writes=["S_s"])
                  REL(pds[0][1], pds[1][1])
                  op("act", lambda e: e.activation(out=Sb_s[:, :], in_=S_s[:, :], func=AF.Copy), reads=["S_s"], writes=["Sb_s"])
                  op("sp", lambda e, t=t: e.dma_start(out=sS_d[t, :, :], in_=stS[:, :]), reads=["stS"], writes=[("sS", t)], dma=True)
                  op("sp", lambda e, t=t: e.dma_start(out=sT_d[t, :, :], in_=stT[:, :]), reads=["stT"], writes=[("sT", t)], dma=True)

              def load_tile(t):
                  op("sp", lambda e, t=t: e.dma_start(out=xt[:, :], in_=res_d[t * 128:(t + 1) * 128, :]), reads=[("res", t)], writes=["xt"], dma=True)
                  if t >= NCT:
                      op("sp", lambda e, t=t: e.dma_start(out=rope_t[:, :], in_=rope_d[(t - NCT) * 128:(t - NCT + 1) * 128, :]), writes=["rope_t"], dma=True)
              load_tile(0)
              for t in range(NT):
                  seg = 0 if t < NCT else 1
                  op("pool", lambda e: e.memset(st[:, 0:1], 0.0), writes=["st0"])
                  op("act", lambda e: e.activation(out=junk[:, :], in_=xt[:, :], func=AF.Square, accum_out=st[:, 0:1]), reads=["xt", "st0"], writes=["junk", "st0"])
                  rstd_from(st[:, 0:1], st[:, 1:2], D, ["st0"], ["st1"])
                  op("dve", lambda e: e.tensor_scalar(out=xh[:, :], in0=xt[:, :], scalar1=st[:, 1:2], scalar2=None, op0=ALU.mult), reads=["xt", "st1"], writes=["xh"])
                  pT, pTk = PSB()
                  for k in range(8):
                      op("pe", lambda e, k=k: e.transpose(pT[:, k * 128:(k + 1) * 128], xh[:, k * 128:(k + 1) * 128], identb[:, :]), reads=["xh", "identb"], writes=[pTk])
                  for k in range(8):
                      op("act", lambda e, k=k, seg=seg: e.activation(out=hTv[:, k, :], in_=pT[:, k * 128:(k + 1) * 128], func=AF.Identity,
                                                                    scale=ABv[:, seg, 0, k:k + 1], bias=ABv[:, seg, 1, k:k + 1]), reads=[pTk, "AB"], writes=["hT"])
                  REL(pTk)
                  _chk("A%da" % t)
                  pA, pAk = PSF()
                  for g in range(4):
                      for k in range(8):
                          op("pe", lambda e, g=g, k=k: e.matmul(pA[0:64, g * 128:(g + 1) * 128], lhsT=w_in_s[:, k, FM_QK + g * 64:FM_QK + (g + 1) * 64], rhs=hTv[:, k, :], start=(k == 0), stop=(k == 7)),
                             reads=["wbig", "hT"], writes=[pAk])
                  pL, pLk = PSF()
                  for k in range(8):
                      op("pe", lambda e, k=k: e.matmul(pL[0:64, 0:128], lhsT=w_in_s[:, k, FM_LR:FM_LR + 64], rhs=hTv[:, k, :], start=(k == 0), stop=(k == 7)), reads=["wbig", "hT"], writes=[pLk])
                  op("act", lambda e: e.activation(out=lrT[0:16, :], in_=pL[0:16, 0:128], func=AF.Copy), reads=[pLk], writes=["lrT"])
                  op("act", lambda e: e.activation(out=lrT[32:48, :], in_=pL[32:48, 0:128], func=AF.Copy), reads=[pLk], writes=["lrT"])
                  REL(pLk)
                  cur = xrv[t % 3]; curk = "xraw%d" % (t % 3)
                  prv = xrv[(t - 1) % 3]; prvk = "xraw%d" % ((t - 1) % 3)
                  first_in_seg = (t == 0 or t == NCT)
                  last_in_seg = (t == NCT - 1 or t == NT - 1)
                  pXs = []
                  for hx in range(2):
                      pX, pXk = PSF()
                      pXs.append((pX, pXk))
                      for c4 in range(4):
                          ct = hx * 4 + c4
                          for k in range(8):
                              op("pe", lambda e, pX=pX, c4=c4, ct=ct, k=k: e.matmul(pX[:, c4 * 128:(c4 + 1) * 128], lhsT=w_in_s[:, k, FM_X + ct * 128:FM_X + (ct + 1) * 128], rhs=hTv[:, k, :], start=(k == 0), stop=(k == 7)),
                                 reads=["wbig", "hT"], writes=[pXk])
                  for hx in range(2):
                      pX, pXk = pXs[hx]
                      pv = pX[:, :].rearrange("p (a t) -> p a t", a=4)
                      op("act", lambda e, hx=hx, pv=pv: e.activation(out=cur[:, hx * 4:(hx + 1) * 4, 2:130], in_=pv, func=AF.Copy), reads=[pXk], writes=[curk])
                      if first_in_seg:
                          op("pool", lambda e, hx=hx: e.memset(cur[:, hx * 4:(hx + 1) * 4, 0:2], 0.0), writes=[curk])
                      else:
                          op("dve", lambda e, hx=hx, pv=pv: e.tensor_copy(out=prv[:, hx * 4:(hx + 1) * 4, 130:132], in_=pv[:, :, 0:2]), reads=[pXk], writes=[prvk])
                          op("pool", lambda e, hx=hx: e.tensor_copy(out=cur[:, hx * 4:(hx + 1) * 4, 0:2], in_=prv[:, hx * 4:(hx + 1) * 4, 128:130]), reads=[prvk], writes=[curk])
                      if last_in_seg:
                          op("pool", lambda e, hx=hx: e.memset(cur[:, hx * 4:(hx + 1) * 4, 130:132], 0.0), writes=[curk])
                      REL(pXk)
                  banks = {}
                  for nm, c0, w in (("A", TM_A, 512), ("B", TM_B, 512), ("C", TM_C, 512), ("D", TM_D, 512), ("E", TM_E, 256)):
                      pb_, pbk = PSF()
                      banks[nm] = (pb_, pbk)
                      for k in range(8):
                          op("pe", lambda e, pb_=pb_, c0=c0, w=w, k=k: e.matmul(pb_[:, 0:w], lhsT=hTv[:, k, :], rhs=w_in_s[:, k, c0:c0 + w], start=(k == 0), stop=(k == 7)),
                             reads=["wbig", "hT"], writes=[pbk])
                  bA, bAk = banks["A"]; bB, bBk = banks["B"]; bC, bCk = banks["C"]; bD, bDk = banks["D"]; bE, bEk = banks["E"]
                  op("act", lambda e: e.activation(out=vg[:, :], in_=bA[:, 128:384], func=AF.Copy), reads=[bAk], writes=["vg"])
                  op("dve", lambda e, t=t: e.tensor_tensor(out=dtraw[t % 2][:, :], in0=bA[:, 384:400], in1=PL("dtb"), op=ALU.add), reads=[bAk, "pl"], writes=["dtraw%d" % (t % 2)])
                  op("act", lambda e: e.activation(out=stB[:, 0:512], in_=bB[:, :], func=AF.Copy), reads=[bBk], writes=["stB"])
                  op("act", lambda e: e.activation(out=stB[:, 512:1024], in_=bC[:, :], func=AF.Copy), reads=[bCk], writes=["stB"])
                  op("act", lambda e: e.activation(out=vr[:, :], in_=bE[:, 0:256], func=AF.Copy), reads=[bEk], writes=["vr"])
                  REL(bBk, bCk, bEk)
                  if seg == 1:
                      for which, src0, dst, tb0 in ((0, 0, qr, 0), (1, 256, kr, 128)):
                          sv = bD[:, src0:src0 + 256].rearrange("p (h s c) -> p h s c", h=4, s=2)
                          tv = ropetmp[:, 0:256].rearrange("p (h s c) -> p h s c", h=4, s=2)
                          tv2 = ropetmp[:, 256:512].rearrange("p (h s c) -> p h s c", h=4, s=2)
                          cosv = rope_t[:, tb0:tb0 + 64].rearrange("p (s c) -> p s c", s=2)
                          sinv = rope_t[:, tb0 + 64:tb0 + 128].rearrange("p (s c) -> p s c", s=2)
                          op("dve", lambda e, sv=sv, tv=tv, cosv=cosv: e.tensor_tensor(out=tv, in0=sv, in1=cosv.unsqueeze(1).to_broadcast([128, 4, 2, 32]), op=ALU.mult), reads=[bDk, "rope_t"], writes=["ropetmp"])
                          op("dve", lambda e, sv=sv, tv2=tv2, sinv=sinv: e.tensor_tensor(out=tv2[:, :, 0, :], in0=sv[:, :, 1, :], in1=sinv[:, 0, :].unsqueeze(1).to_broadcast([128, 4, 32]), op=ALU.mult), reads=[bDk, "rope_t"], writes=["ropetmp2"])
                          op("dve", lambda e, sv=sv, tv2=tv2, sinv=sinv: e.tensor_tensor(out=tv2[:, :, 1, :], in0=sv[:, :, 0, :], in1=sinv[:, 1, :].unsqueeze(1).to_broadcast([128, 4, 32]), op=ALU.mult), reads=[bDk, "rope_t"], writes=["ropetmp2"])
                          op("dve", lambda e, dst=dst: e.tensor_tensor(out=dst[:, :], in0=ropetmp[:, 0:256], in1=ropetmp[:, 256:512], op=ALU.add), reads=["ropetmp", "ropetmp2"], writes=["qr" if which == 0 else "kr"])
                          _chk("R%dw%d" % (t, which))
                  else:
                      op("act", lambda e: e.activation(out=qr[:, :], in_=bD[:, 0:256], func=AF.Copy, scale=0.125), reads=[bDk], writes=["qr"])
                      op("act", lambda e: e.activation(out=kr[:, :], in_=bD[:, 256:512], func=AF.Copy), reads=[bDk], writes=["kr"])
                  REL(bDk)
                  if t + 1 < NT:
                      load_tile(t + 1)
                  _chk("A%db" % t)
                  pz, pzk = PSF()
                  op("pe", lambda e: e.matmul(pz[:, 0:256], lhsT=lrT[0:64, :], rhs=GUb[0:64, :], start=True, stop=True, tile_position=(0, 0)), reads=["lrT", "GUb"], writes=[pzk])
                  _chk("A%db0" % t)
                  op("act", lambda e: e.activation(out=ez[:, :], in_=pz[:, 0:256], func=AF.Exp, scale=-1.0), reads=[pzk], writes=["ez"])
                  REL(pzk)
                  _chk("A%db0e" % t)
                  op("act", lambda e: e.activation(out=spt[:, :], in_=ez[:, :], func=AF.Ln, bias=C("one")), reads=["ez", "constf"], writes=["spt"])
                  _chk("A%db1" % t)
                  pGs = []
                  for d in range(2):
                      pG, pGk = PSF()
                      pGs.append((pG, pGk))
                      R = C("Rf") if d == 0 else C("Rb")
                      for p_ in range(2):
                          op("pe", lambda e, pG=pG, d=d, p_=p_, R=R: e.matmul(pG[0:64, p_ * 129:(p_ + 1) * 129], lhsT=spt[:, d * 128 + p_ * 64:d * 128 + (p_ + 1) * 64], rhs=R, start=True, stop=True),
                             reads=["spt", "constf"], writes=[pGk])
                  pD, pDk = PSF()
                  for d in range(2):
                      Lc = C("Lf") if d == 0 else C("Lb")
                      op("pe", lambda e, d=d, Lc=Lc: e.matmul(pD[:, d * 128:(d + 1) * 128], lhsT=Lc, rhs=spt[:, d * 128:(d + 1) * 128], start=True, stop=True), reads=["spt", "constf"], writes=[pDk])
                  for d in range(2):
                      pG, pGk = pGs[d]
                      gv = pG[0:64, 0:258].rearrange("q (p i) -> q p i", p=2)
                      op("act", lambda e, d=d, gv=gv: e.activation(out=EGv[:, d, :, :], in_=gv[:, :, 0:128], func=AF.Exp, bias=C("lnqs", slice(0, 64))), reads=[pGk, "constf"], writes=["EG"])
                      op("act", lambda e, d=d, gv=gv: e.activation(out=EGnv[:, d, :, :], in_=gv[:, :, 0:128], func=AF.Exp, scale=-1.0), reads=[pGk], writes=["EGn"])
                      op("act", lambda e, d=d, gv=gv: e.activation(out=gdec[:, d * 2:(d + 1) * 2], in_=gv[:, :, 128], func=AF.Exp), reads=[pGk], writes=["gdec"])
                  op("act", lambda e: e.activation(out=ED[:, :], in_=pD[:, 0:256], func=AF.Exp), reads=[pDk], writes=["ED"])
                  REL(pGs[0][1], pGs[1][1], pDk)
                  _chk("A%db2" % t)
                  qv_ = pA[0:64, 0:256].rearrange("q (p i) -> q p i", p=2)
                  kv_ = pA[0:64, 256:512].rearrange("q (p i) -> q p i", p=2)
                  for d in range(2):
                      op("dve", lambda e, d=d: e.tensor_tensor(out=qtgv[:, d, :, :], in0=qv_, in1=EGv[:, d, :, :], op=ALU.mult), reads=[pAk, "EG"], writes=["qtg"])
                      for hh in range(2):
                          op("dve", lambda e, d=d, hh=hh: e.tensor_tensor(out=ktgv[32 * hh:32 * hh + 32, d, hh, :, :], in0=kv_[32 * hh:32 * hh + 32, :, :], in1=EGnv[32 * hh:32 * hh + 32, d, :, :], op=ALU.mult), reads=[pAk, "EGn"], writes=["ktg"])
                      op("dve", lambda e, d=d: e.tensor_tensor(out=ksg[:, d * 128:(d + 1) * 128], in0=bA[:, 0:128], in1=ED[:, d * 128:(d + 1) * 128], op=ALU.mult), reads=[bAk, "ED"], writes=["ksg"])
                  REL(pAk, bAk)
                  _chk("A%db2d" % t)
                  for d in range(2):
                      mk = C("maskf") if d == 0 else C("maskb")
                      for hh in range(2):
                          pS, pSk = PSF()
                          for p_ in range(2):
                              op("pe", lambda e, pS=pS, d=d, p_=p_, hh=hh: e.matmul(pS[:, p_ * 128:(p_ + 1) * 128], lhsT=ktgv[:, d, hh, p_, :], rhs=qtgv[:, d, p_, :], start=True, stop=True, tile_position=(0, 0)),
                                 reads=["ktg", "qtg"], writes=[pSk])
                          op("dve", lambda e, pS=pS, d=d, hh=hh, mk=mk: e.tensor_tensor(out=Pgv[:, d, hh, :, :], in0=pS[:, 0:256].rearrange("p (b i) -> p b i", b=2),
                                                                                in1=mk.unsqueeze(1).to_broadcast([128, 2, 128]), op=ALU.mult), reads=[pSk, "constf"], writes=["Pg"])
                          REL(pSk)
                  _chk("A%db3" % t)
                  pO, pOk = PSF()
                  for p_ in range(2):
                      op("pe", lambda e, p_=p_: e.matmul(pO[:, p_ * 128:(p_ + 1) * 128], lhsT=qtgv[:, 0, p_, :], rhs=Sb_g[:, p_ * 128:(p_ + 1) * 128], start=True, stop=False, skip_group_check=True, tile_position=(0, 0)),
                         reads=["qtg", "Sb_g"], writes=[pOk])
                      for h in (2 * p_, 2 * p_ + 1):
                          for d in range(2):
                              op("pe", lambda e, h=h, d=d: e.matmul(pO[:, h * 64:(h + 1) * 64], lhsT=Pgv[:, d, h % 2, h // 2, :], rhs=vg[:, h * 64:(h + 1) * 64], start=False, stop=(d == 1 and h == 2 * p_ + 1), skip_group_check=True),
                                 reads=["Pg", "vg"], writes=[pOk])
                  op("act", lambda e: e.activation(out=stA[:, 0:256], in_=pO[:, 0:256], func=AF.Copy), reads=[pOk], writes=["stA"])
                  REL(pOk)
                  _chk("A%db4" % t)
                  pDS, pDSk = PSF()
                  for d in range(2):
                      for p_ in range(2):
                          op("pe", lambda e, d=d, p_=p_: e.matmul(pDS[0:64, d * 256 + p_ * 128:d * 256 + (p_ + 1) * 128], lhsT=ksg[:, d * 128 + p_ * 64:d * 128 + (p_ + 1) * 64], rhs=vg[:, p_ * 128:(p_ + 1) * 128], start=True, stop=True),
                             reads=["ksg", "vg"], writes=[pDSk])
                  for p_ in range(2):
                      op("dve", lambda e, p_=p_: e.scalar_tensor_tensor(out=S_g[:, p_ * 128:(p_ + 1) * 128], in0=S_g[:, p_ * 128:(p_ + 1) * 128], scalar=gdec[:, p_:p_ + 1],
                                                                        in1=pDS[0:64, p_ * 128:(p_ + 1) * 128], op0=ALU.mult, op1=ALU.add), reads=["S_g", "gdec", pDSk], writes=["S_g"])
                  op("dve", lambda e: e.tensor_tensor(out=Sb_g[:, :].rearrange("q (p c) -> q p c", p=2), in0=S_g[:, :].rearrange("q (p c) -> q p c", p=2),
                                                      in1=C("bmg", slice(0, 64)).unsqueeze(1).to_broadcast([64, 2, 128]), op=ALU.mult), reads=["S_g", "constf"], writes=["Sb_g"])
                  op("act", lambda e: e.activation(out=stA[0:64, 768:1024], in_=pDS[0:64, 256:512], func=AF.Copy), reads=[pDSk], writes=["stA"])
                  REL(pDSk)
                  op("pool", lambda e: e.tensor_copy(out=stA[0:64, 1024:1026], in_=gdec[:, 2:4]), reads=["gdec"], writes=["stA"])
                  op("pool", lambda e: e.tensor_copy(out=stB[0:64, 1280:1536], in_=qtg[:, 256:512]), reads=["qtg"], writes=["stB"])
                  _chk("A%dc" % t)
                  pT2, pT2k = PSB()
                  for a_ in range(4):
                      srct = qr if a_ < 2 else kr
                      op("pe", lambda e, a_=a_, srct=srct: e.transpose(pT2[:, a_ * 128:(a_ + 1) * 128], srct[:, (a_ % 2) * 128:(a_ % 2 + 1) * 128], identb[:, :]), reads=["qr", "kr", "identb"], writes=[pT2k])
                  op("act", lambda e: e.activation(out=qkT[:, 0:256], in_=pT2[:, 0:256], func=AF.Copy), reads=[pT2k], writes=["qkT"])
                  for hh in range(2):
                      op("act", lambda e, hh=hh: e.activation(out=kTzv[64 * hh:64 * hh + 64, hh, :, :], in_=pT2[64 * hh:64 * hh + 64, 256:512].rearrange("q (p i) -> q p i", p=2), func=AF.Copy), reads=[pT2k], writes=["kTz"])
                  REL(pT2k)
                  op("pool", lambda e: e.tensor_copy(out=stB[:, 1024:1280], in_=qkT[:, 0:256]), reads=["qkT"], writes=["stB"])
                  for hh in range(2):
                      pSr, pSrk = PSF()
                      for p_ in range(2):
                          op("pe", lambda e, pSr=pSr, p_=p_, hh=hh: e.matmul(pSr[:, p_ * 128:(p_ + 1) * 128], lhsT=kTzv[:, hh, p_, :], rhs=qkTv[:, p_, :], start=True, stop=True), reads=["qkT", "kTz"], writes=[pSrk])
                      a0 = CF["retM"][0]
                      op("dve", lambda e, pSr=pSr, hh=hh, a0=a0: e.tensor_tensor(out=Pr[:, hh * 256:(hh + 1) * 256], in0=pSr[:, 0:256], in1=constf[:, a0 + hh * 256:a0 + (hh + 1) * 256], op=ALU.mult), reads=[pSrk, "constf"], writes=["Pr"])
                      REL(pSrk)
                  pOr, pOrk = PSF()
                  for h in range(4):
                      op("pe", lambda e, h=h: e.matmul(pOr[:, h * 64:(h + 1) * 64], lhsT=Prv[:, h % 2, h // 2, :], rhs=vr[:, h * 64:(h + 1) * 64], start=True, stop=True), reads=["Pr", "vr"], writes=[pOrk])
                  for p_ in range(2):
                      op("pe", lambda e, p_=p_: e.matmul(pOr[:, 256 + p_ * 128:256 + (p_ + 1) * 128], lhsT=qkTv[:, p_, :], rhs=Sb_r[:, p_ * 128:(p_ + 1) * 128], start=True, stop=True), reads=["qkT", "Sb_r"], writes=[pOrk])
                  op("dve", lambda e: e.tensor_tensor(out=tmpS[:, 0:256].rearrange("p (h c) -> p h c", h=4), in0=pOr[:, 256:512].rearrange("p (h c) -> p h c", h=4),
                                                      in1=C("retEQf").unsqueeze(2).to_broadcast([128, 4, 64]), op=ALU.mult), reads=[pOrk, "constf"], writes=["tmpS"])
                  op("dve", lambda e: e.tensor_tensor(out=stA[:, 256:512], in0=tmpS[:, 0:256], in1=pOr[:, 0:256], op=ALU.add), reads=["tmpS", pOrk], writes=["stA"])
                  REL(pOrk)
                  for d in range(2):
                      Wc = C("retWf") if d == 0 else C("retWb")
                      op("dve", lambda e, d=d, Wc=Wc: e.tensor_tensor(out=vtl[:, d * 256:(d + 1) * 256].rearrange("p (h c) -> p h c", h=4), in0=vr[:, :].rearrange("p (h c) -> p h c", h=4),
                                                                      in1=Wc.unsqueeze(2).to_broadcast([128, 4, 64]), op=ALU.mult), reads=["vr", "constf"], writes=["vtl"])
                  pDr, pDrk = PSF()
                  for d in range(2):
                      for p_ in range(2):
                          op("pe", lambda e, d=d, p_=p_: e.matmul(pDr[:, d * 256 + p_ * 128:d * 256 + (p_ + 1) * 128], lhsT=kr[:, p_ * 128:(p_ + 1) * 128], rhs=vtl[:, d * 256 + p_ * 128:d * 256 + (p_ + 1) * 128], start=True, stop=True),
                             reads=["kr", "vtl"], writes=[pDrk])
                  op("dve", lambda e: e.tensor_tensor(out=tmpS[:, 256:512].rearrange("p (h c) -> p h c", h=4), in0=S_r[:, :].rearrange("p (h c) -> p h c", h=4),
                                                      in1=C("retdec").unsqueeze(2).to_broadcast([128, 4, 64]), op=ALU.mult), reads=["S_r", "constf"], writes=["tmpS"])
                  op("dve", lambda e: e.tensor_tensor(out=S_r[:, :], in0=tmpS[:, 256:512], in1=pDr[:, 0:256], op=ALU.add), reads=["tmpS", pDrk], writes=["S_r"])
                  op("dve", lambda e: e.tensor_tensor(out=Sb_r[:, :].rearrange("p (a c) -> p a c", a=2), in0=S_r[:, :].rearrange("p (a c) -> p a c", a=2),
                                                      in1=C("bmr").unsqueeze(1).to_broadcast([128, 2, 128]), op=ALU.mult), reads=["S_r", "constf"], writes=["Sb_r"])
                  op("act", lambda e: e.activation(out=stA[:, 512:768], in_=pDr[:, 256:512], func=AF.Copy), reads=[pDrk], writes=["stA"])
                  REL(pDrk)
                  op("sp", lambda e, t=t: e.dma_start(out=sA_d[t, :, :], in_=stA[:, :]), reads=["stA"], writes=[("sA", t)], dma=True)
                  op("sp", lambda e, t=t: e.dma_start(out=sB_d[t, :, :], in_=stB[:, :]), reads=["stB"], writes=[("sB", t)], dma=True)
                  _chk("A%dd" % t)
                  if not first_in_seg:
                      ssd_tile(t - 1)
                  if last_in_seg:
                      ssd_tile(t)
                  _chk("A%d" % t)

              _chk("A")
              barrier()
              op("sp", lambda e, l=l: e.dma_start(out=w_out_s, in_=wb_out[l, :, :].rearrange("(k p) c -> p k c", p=128)), reads=W8("wout", l), writes=["wbig"], dma=True)
              op("sp", lambda e, l=l: e.dma_start(out=w2_s, in_=wb_2[l, :, :].rearrange("(j p) c -> p j c", p=128)), reads=W22("w2", l), writes=["wbig"], dma=True)
              for nm, tns in (("S_g", S_g), ("S_s", S_s), ("S_r", S_r), ("Sb_g", Sb_g), ("Sb_s", Sb_s), ("Sb_r", Sb_r)):
                  op("pool", lambda e, tns=tns: e.memset(tns[:, :], 0.0), writes=[nm])
              order = list(range(NCT - 1, -1, -1)) + list(range(NT - 1, NCT - 1, -1))
              def load_staging(t):
                  op("sp", lambda e, t=t: e.dma_start(out=stA[:, :], in_=sA_d[t, :, :]), reads=[("sA", t)], writes=["stA"], dma=True)
                  op("sp", lambda e, t=t: e.dma_start(out=stB[:, :], in_=sB_d[t, :, :]), reads=[("sB", t)], writes=["stB"], dma=True)
                  op("sp", lambda e, t=t: e.dma_start(out=stS[:, :], in_=sS_d[t, :, :]), reads=[("sS", t)], writes=["stS"], dma=True)
                  op("sp", lambda e, t=t: e.dma_start(out=stT[:, :], in_=sT_d[t, :, :]), reads=[("sT", t)], writes=["stT"], dma=True)
              load_staging(order[0])
              def resid_update(seg, wsel, nK, lhs_of, vi, tag):
                  pys = []
                  op("pool", lambda e: e.memset(st[:, 28:30], 0.0), writes=["st28"])
                  for half in range(2):
                      py, pyk = PSF()
                      pys.append((py, pyk))
                      for k in range(nK):
                          op("pe", lambda e, py=py, k=k, half=half: e.matmul(py[:, :], lhsT=lhs_of(k), rhs=wsel[:, k, half * 512:(half + 1) * 512], start=(k == 0), stop=(k == nK - 1)),
                             reads=["wbig", tag], writes=[pyk])
                      op("act", lambda e, py=py, half=half: e.activation(out=junk[:, half * 512:(half + 1) * 512], in_=py[:, :], func=AF.Square, accum_out=st[:, 28 + half:29 + half]),
                         reads=[pyk, "st28"], writes=["junk", "st28"])
                  op("dve", lambda e: e.tensor_tensor(out=st[:, 30:31], in0=st[:, 28:29], in1=st[:, 29:30], op=ALU.add), reads=["st28"], writes=["st30"])
                  rstd_from(st[:, 30:31], st[:, 31:32], D, ["st30"], ["st31"])
                  for half in range(2):
                      py, pyk = pys[half]
                      op("dve", lambda e, py=py, half=half: e.scalar_tensor_tensor(out=junk[:, half * 512:(half + 1) * 512], in0=py[:, :], scalar=st[:, 31:32],
                                                                                   in1=Gv[:, seg, vi, half * 512:(half + 1) * 512], op0=ALU.mult, op1=ALU.mult), reads=[pyk, "st31", "Grep", "junk"], writes=["junk"])
                      REL(pyk)
                  op("dve", lambda e: e.tensor_tensor(out=xt[:, :], in0=xt[:, :], in1=junk[:, :], op=ALU.add), reads=["xt", "junk"], writes=["xt"])

              def stage1(t):
                  seg = 0 if t < NCT else 1
                  nxt = order[order.index(t) + 1] if order.index(t) + 1 < len(order) else None
                  op("act", lambda e: e.activation(out=silb[:, :], in_=stB[:, 0:1024], func=AF.Silu), reads=["stB"], writes=["silb"])
                  pI, pIk = PSF(); pIs, pIsk = PSF()
                  for p_ in range(2):
                      op("pe", lambda e, p_=p_: e.matmul(pI[:, p_ * 128:(p_ + 1) * 128], lhsT=stB[0:64, 1280 + p_ * 128:1280 + (p_ + 1) * 128], rhs=Sb_g[:, p_ * 128:(p_ + 1) * 128], start=True, stop=True, tile_position=(0, 0)), reads=["stB", "Sb_g"], writes=[pIk])
                      op("pe", lambda e, p_=p_: e.matmul(pI[:, 256 + p_ * 128:256 + (p_ + 1) * 128], lhsT=stB[:, 1024 + p_ * 128:1024 + (p_ + 1) * 128], rhs=Sb_r[:, p_ * 128:(p_ + 1) * 128], start=True, stop=True), reads=["stB", "Sb_r"], writes=[pIk])
                  for g in range(2):
                      op("pe", lambda e, g=g: e.matmul(pIs[:, g * 256:(g + 1) * 256], lhsT=stT[:, 512 + g * 128:512 + (g + 1) * 128], rhs=Sb_s[:, g * 256:(g + 1) * 256], start=True, stop=True), reads=["stT", "Sb_s"], writes=[pIsk])
                  op("dve", lambda e: e.tensor_tensor(out=Oall[:, 0:256], in0=stA[:, 0:256], in1=pI[:, 0:256], op=ALU.add), reads=["stA", pIk], writes=["Og"])
                  op("dve", lambda e: e.tensor_tensor(out=tmpS[:, :].rearrange("p (h c) -> p h c", h=8), in0=pIs[:, :].rearrange("p (h c) -> p h c", h=8),
                                                      in1=stS[:, 512:520].unsqueeze(2).to_broadcast([128, 8, 64]), op=ALU.mult), reads=[pIsk, "stS"], writes=["tmpS"])
                  op("dve", lambda e: e.tensor_tensor(out=Oall[:, 256:768], in0=tmpS[:, :], in1=stS[:, 0:512], op=ALU.add), reads=["tmpS", "stS"], writes=["Os"])
                  op("dve", lambda e: e.tensor_tensor(out=fsq[:, 0:256].rearrange("p (h c) -> p h c", h=4), in0=pI[:, 256:512].rearrange("p (h c) -> p h c", h=4),
                                                      in1=C("retEQb").unsqueeze(2).to_broadcast([128, 4, 64]), op=ALU.mult), reads=[pIk, "constf"], writes=["fsq"])
                  op("dve", lambda e: e.tensor_tensor(out=Oall[:, 768:1024], in0=fsq[:, 0:256], in1=stA[:, 256:512], op=ALU.add), reads=["fsq", "stA"], writes=["Or"])
                  REL(pIk, pIsk)
                  for p_ in range(2):
                      op("dve", lambda e, p_=p_: e.scalar_tensor_tensor(out=S_g[:, p_ * 128:(p_ + 1) * 128], in0=S_g[:, p_ * 128:(p_ + 1) * 128], scalar=stA[0:64, 1024 + p_:1025 + p_],
                                                                        in1=stA[0:64, 768 + p_ * 128:768 + (p_ + 1) * 128], op0=ALU.mult, op1=ALU.add), reads=["S_g", "stA", pIk], writes=["S_g"])
                  op("dve", lambda e: e.tensor_tensor(out=Sb_g[:, :].rearrange("q (p c) -> q p c", p=2), in0=S_g[:, :].rearrange("q (p c) -> q p c", p=2),
                                                      in1=C("bmg", slice(0, 64)).unsqueeze(1).to_broadcast([64, 2, 128]), op=ALU.mult), reads=["S_g", "constf"], writes=["Sb_g"])
                  op("dve", lambda e: e.tensor_tensor(out=tmpS[:, :].rearrange("p (h c) -> p h c", h=8), in0=S_s[:, :].rearrange("p (h c) -> p h c", h=8),
                                                      in1=stS[:, 520:528].unsqueeze(2).to_broadcast([128, 8, 64]), op=ALU.mult), reads=["S_s", "stS", "Os", pIsk], writes=["tmpS"])
                  op("dve", lambda e: e.tensor_tensor(out=S_s[:, :], in0=tmpS[:, :], in1=stS[:, 528:1040], op=ALU.add), reads=["tmpS", "stS"], writes=["S_s"])
                  op("act", lambda e: e.activation(out=Sb_s[:, :], in_=S_s[:, :], func=AF.Copy), reads=["S_s"], writes=["Sb_s"])
                  op("dve", lambda e: e.tensor_tensor(out=fsq[:, 256:512].rearrange("p (h c) -> p h c", h=4), in0=S_r[:, :].rearrange("p (h c) -> p h c", h=4),
                                                      in1=C("retdec").unsqueeze(2).to_broadcast([128, 4, 64]), op=ALU.mult), reads=["S_r", "constf", pIk], writes=["fsq2"])
                  op("dve", lambda e: e.tensor_tensor(out=S_r[:, :], in0=fsq[:, 256:512], in1=stA[:, 512:768], op=ALU.add), reads=["fsq2", "stA"], writes=["S_r"])
                  op("dve", lambda e: e.tensor_tensor(out=Sb_r[:, :].rearrange("p (a c) -> p a c", a=2), in0=S_r[:, :].rearrange("p (a c) -> p a c", a=2),
                                                      in1=C("bmr").unsqueeze(1).to_broadcast([128, 2, 128]), op=ALU.mult), reads=["S_r", "constf"], writes=["Sb_r"])
                  _chk("Bs%d" % t)
                  if last and seg == 0:
                      if nxt is not None:
                          load_staging(nxt)
                      return
                  op("dve", lambda e: e.tensor_tensor(out=fsq[:, 0:256], in0=Oall[:, 0:256], in1=Oall[:, 0:256], op=ALU.mult), reads=["Og", "Or"], writes=["fsq"])
                  op("dve", lambda e: e.tensor_reduce(out=st[:, 4:8], in_=fsq[:, 0:256].rearrange("p (h c) -> p h c", h=4), axis=AX.X, op=ALU.add), reads=["fsq"], writes=["st4"])
                  rstd_from(st[:, 4:8], st[:, 8:12], 64, ["st4"], ["st8"])
                  op("dve", lambda e: e.tensor_tensor(out=fsq[:, 0:256].rearrange("p (h c) -> p h c", h=4), in0=Oall[:, 0:256].rearrange("p (h c) -> p h c", h=4),
                                                      in1=st[:, 8:12].unsqueeze(2).to_broadcast([128, 4, 64]), op=ALU.mult), reads=["Og", "st8"], writes=["fsq"])
                  op("dve", lambda e: e.tensor_tensor(out=fsq[:, 0:256], in0=fsq[:, 0:256], in1=PL("glan"), op=ALU.mult), reads=["fsq", "pl"], writes=["fsq"])
                  op("dve", lambda e: e.tensor_tensor(out=mixed[:, 0:256], in0=fsq[:, 0:256], in1=silb[:, 0:256], op=ALU.mult), reads=["fsq", "silb"], writes=["mixed"])
                  op("dve", lambda e: e.tensor_tensor(out=fsq[:, :], in0=stT[:, 0:512], in1=PL("ssdd"), op=ALU.mult), reads=["stT", "pl", "mixed"], writes=["fsq"])
                  op("dve", lambda e: e.tensor_tensor(out=fsq[:, :], in0=fsq[:, :], in1=Oall[:, 256:768], op=ALU.add), reads=["fsq", "Os"], writes=["fsq"])
                  op("dve", lambda e: e.tensor_tensor(out=fsq[:, :], in0=fsq[:, :], in1=silb[:, 512:1024], op=ALU.mult), reads=["fsq", "silb"], writes=["fsq"])
                  op("pool", lambda e: e.memset(st[:, 12:13], 0.0), writes=["st12"])
                  op("act", lambda e: e.activation(out=sil[:, :], in_=fsq[:, :], func=AF.Square, accum_out=st[:, 12:13]), reads=["fsq", "st12"], writes=["sil", "st12"])
                  rstd_from(st[:, 12:13], st[:, 13:14], 512, ["st12"], ["st13"])
                  op("dve", lambda e: e.scalar_tensor_tensor(out=mixed[:, 256:768], in0=fsq[:, :], scalar=st[:, 13:14], in1=PL("ssdn"), op0=ALU.mult, op1=ALU.mult), reads=["fsq", "st13", "pl"], writes=["mixed"])
                  op("dve", lambda e: e.tensor_reduce(out=st[:, 16:20], in_=Oall[:, 768:1024].rearrange("p (h c) -> p h c", h=4), axis=AX.X, op=ALU.add), reads=["Or"], writes=["st16"])
                  op("dve", lambda e: e.tensor_scalar(out=st[:, 16:20], in0=st[:, 16:20], scalar1=1.0 / 64, scalar2=None, op0=ALU.mult), reads=["st16"], writes=["st16"])
                  op("dve", lambda e: e.tensor_tensor(out=fsq[:, 0:256].rearrange("p (h c) -> p h c", h=4), in0=Oall[:, 768:1024].rearrange("p (h c) -> p h c", h=4),
                                                      in1=st[:, 16:20].unsqueeze(2).to_broadcast([128, 4, 64]), op=ALU.subtract), reads=["Or", "st16", "mixed"], writes=["fsq"])
                  op("dve", lambda e: e.tensor_tensor(out=fsq[:, 256:512], in0=fsq[:, 0:256], in1=fsq[:, 0:256], op=ALU.mult), reads=["fsq"], writes=["fsqb"])
                  op("dve", lambda e: e.tensor_reduce(out=st[:, 20:24], in_=fsq[:, 256:512].rearrange("p (h c) -> p h c", h=4), axis=AX.X, op=ALU.add), reads=["fsqb"], writes=["st20"])
                  rstd_from(st[:, 20:24], st[:, 24:28], 64, ["st20"], ["st24"])
                  op("dve", lambda e: e.tensor_tensor(out=fsq[:, 0:256].rearrange("p (h c) -> p h c", h=4), in0=fsq[:, 0:256].rearrange("p (h c) -> p h c", h=4),
                                                      in1=st[:, 24:28].unsqueeze(2).to_broadcast([128, 4, 64]), op=ALU.mult), reads=["fsq", "st24", "fsqb"], writes=["fsq"])
                  op("dve", lambda e: e.tensor_tensor(out=fsq[:, 0:256], in0=fsq[:, 0:256], in1=PL("retn"), op=ALU.mult), reads=["fsq", "pl"], writes=["fsq"])
                  op("dve", lambda e: e.tensor_tensor(out=mixed[:, 768:1024], in0=fsq[:, 0:256], in1=silb[:, 256:512], op=ALU.mult), reads=["fsq", "silb"], writes=["mixed"])
                  if nxt is not None:
                      load_staging(nxt)
              def stage2(t, idx):
                  seg = 0 if t < NCT else 1
                  pT, pTk = PSB()
                  for k in range(8):
                      op("pe", lambda e, k=k: e.transpose(pT[:, k * 128:(k + 1) * 128], mixed[:, k * 128:(k + 1) * 128], identb[:, :]), reads=["mixed", "identb"], writes=[pTk])
                  op("act", lambda e: e.activation(out=mixT[:, :], in_=pT[:, :], func=AF.Copy), reads=[pTk], writes=["mixT"])
                  REL(pTk)
                  op("sp", lambda e, t=t: e.dma_start(out=xt[:, :], in_=res_d[t * 128:(t + 1) * 128, :]), reads=[("res", t)], writes=["xt"], dma=True)
                  resid_update(seg, w_out_s, 8, lambda k: mixTv[:, k, :], 0, "mixT")
                  op("pool", lambda e: e.memset(st[:, 0:1], 0.0), writes=["st0"])
                  op("act", lambda e: e.activation(out=junk[:, :], in_=xt[:, :], func=AF.Square, accum_out=st[:, 0:1]), reads=["xt", "st0"], writes=["junk", "st0"])
                  rstd_from(st[:, 0:1], st[:, 1:2], D, ["st0"], ["st1"])
                  op("dve", lambda e: e.tensor_scalar(out=xh[:, :], in0=xt[:, :], scalar1=st[:, 1:2], scalar2=None, op0=ALU.mult), reads=["xt", "st1"], writes=["xh"])
                  pT, pTk = PSB()
                  for k in range(8):
                      op("pe", lambda e, k=k: e.transpose(pT[:, k * 128:(k + 1) * 128], xh[:, k * 128:(k + 1) * 128], identb[:, :]), reads=["xh", "identb"], writes=[pTk])
                  for k in range(8):
                      op("act", lambda e, k=k, seg=seg, idx=idx: e.activation(out=hT2v[:, k, idx * 128:(idx + 1) * 128], in_=pT[:, k * 128:(k + 1) * 128], func=AF.Identity,
                                                                    scale=ABv[:, seg, 3, k:k + 1], bias=ABv[:, seg, 4, k:k + 1]), reads=[pTk, "AB"], writes=["hT"])
                  REL(pTk)
                  if idx == 0:
                      op("sp", lambda e, t=t: e.dma_start(out=res_d[t * 128:(t + 1) * 128, :], in_=xt[:, :]), reads=["xt"], writes=[("res", t)], dma=True)
                  _chk("B%d" % t)
              def ffn(t_prev, t):
                  seg = 0 if t < NCT else 1
                  for j in range(NJ):
                      wg_ = wgu[j % 4]; wgk = "wgu%d" % (j % 4)
                      wgv = wg_[:, :].rearrange("p (k c) -> p k c", k=8)
                      op("sp", lambda e, j=j, wg_=wg_, l=l: e.dma_start(out=wg_[:, :], in_=wb_13[l, j, :, :]), reads=W8("w13", l), writes=[wgk], dma=True)
                      pgu, pguk = PSF()
                      for k in range(8):
                          op("pe", lambda e, pgu=pgu, wgv=wgv, k=k: e.matmul(pgu[:, 0:256], lhsT=wgv[:, k, 0:128], rhs=hT2v[:, k, :], start=(k == 0), stop=(k == 7)), reads=[wgk, "hT"], writes=[pguk])
                      for k in range(8):
                          op("pe", lambda e, pgu=pgu, wgv=wgv, k=k: e.matmul(pgu[:, 256:512], lhsT=wgv[:, k, 128:256], rhs=hT2v[:, k, :], start=(k == 0), stop=(k == 7)), reads=[wgk, "hT"], writes=[pguk])
                      op("act", lambda e, pgu=pgu: e.activation(out=sgt[:, :], in_=pgu[:, 0:256], func=AF.Silu), reads=[pguk], writes=["sgt"])
                      op("dve", lambda e, pgu=pgu, j=j: e.tensor_tensor(out=actTv[:, j, :], in0=sgt[:, 0:128], in1=pgu[:, 256:384], op=ALU.mult), reads=["sgt", pguk], writes=["actT"])
                      op("dve", lambda e, pgu=pgu, j=j: e.tensor_tensor(out=actTbv[:, j, :], in0=sgt[:, 128:256], in1=pgu[:, 384:512], op=ALU.mult), reads=["sgt", pguk], writes=["actTb"])
                      REL(pguk)
                  for which_, tt in ((1, t), (0, t_prev)):
                      if which_ == 0:
                          op("sp", lambda e, tt=tt: e.dma_start(out=xt[:, :], in_=res_d[tt * 128:(tt + 1) * 128, :]), reads=[("res", tt)], writes=["xt"], dma=True)
                          resid_update(seg, w2_s, NJ, lambda k: actTv[:, k, :], 1, "actT")
                      else:
                          resid_update(seg, w2_s, NJ, lambda k: actTbv[:, k, :], 1, "actTb")
                      op("sp", lambda e, tt=tt: e.dma_start(out=res_d[tt * 128:(tt + 1) * 128, :], in_=xt[:, :]), reads=["xt"], writes=[("res", tt)], dma=True)
                      if last and seg == 1:
                          fo = op("sp", lambda e, tt=tt: e.dma_start(out=out_d[(tt - NCT) * 128:(tt - NCT + 1) * 128, :], in_=xt[:, :]), reads=["xt"], writes=[("out", tt)], dma=True)
                          finals.append(fo)
                  _chk("F%d" % t)
              active = [t for t in order if not (last and t < NCT)]
              for t in order:
                  if t not in active:
                      stage1(t)
              pairs = [(active[i], active[i + 1]) for i in range(0, len(active), 2)]
              stage1(pairs[0][0])
              for pi_, (a_, b_) in enumerate(pairs):
                  stage2(a_, 0)
                  stage1(b_)
                  stage2(b_, 1)
                  if pi_ + 1 < len(pairs):
                      stage1(pairs[pi_ + 1][0])
                  ffn(a_, b_)
        except _Stop:
            fo = op("sp", lambda e: e.dma_start(out=out_d[0:128, :], in_=xt[:, :]), reads=["xt"], writes=[("out", -1)], dma=True)
            finals.append(fo)
        lastop = {}
        for o_ in P.ops:
            lastop[(o_["eng"], o_["dma"])] = o_["idx"]
        for v_ in lastop.values():
            if v_ not in finals:
                finals.append(v_)
        P.emit(final_wait_ops=finals)
        global LASTP
        LASTP = P
    return nc


finals = []


def kernel(**inp):
    global finals
    finals = []
    inp = {k: np.asarray(v) for k, v in inp.items()}
    depth = DEPTH
    cm = _colmap()
    w_in = inp["w_in"]
    w_in_r = np.where(cm[None, None, :] >= 0, w_in[:, :, np.maximum(cm, 0)], np.float32(0)).astype(np.float32)
    constf = _host_consts()
    rope = _host_rope()
    pl = np.stack([_host_pl(inp, l) for l in range(depth)], 0)
    nc = build_nc(depth)
    in_maps = []
    for b in range(4):
        xin = np.concatenate([inp["ctx"][b], inp["x"][b]], 0).astype(np.float32)
        cv = np.zeros((128, 16), np.float32)
        cv[:, 0::2] = _fm(inp["c_ctx"], 8)
        cv[:, 1::2] = _fm(inp["c"][b], 8)
        in_maps.append({"xin": xin, "cvec": cv, "constf": constf, "rope": rope, "pl": pl, "cbrow": np.ascontiguousarray(inp["ssd_conv_b"][:depth, None, 0:768]).astype(np.float32), "w_in": w_in_r[:depth],
                        "w_out": inp["w_out"][:depth], "w13": inp["ffn_w13"][:depth], "w2": inp["ffn_w2"][:depth], "ada_w": inp["ada_w"][:depth]})
    res = run_bass_kernel_spmd(nc, in_maps, core_ids=[0, 1, 2, 3])
    return np.stack([np.asarray(r["out"], np.float32) for r in res.results], 0)
```
